# Optimizing a Trainium2 kernel written in Bass

```python
import jax, jax.numpy as jnp
from jax import lax
import numpy as np

D_MODEL = 1024
BATCH = 8
SEQ = 8192
DEPTH = 2
DEC_BATCH = 16
DEC_SEQ = 2048
PAST_LEN = 128

HEAD_DIM = 64
A_HEADS = 6
A_WIDTH = A_HEADS * HEAD_DIM
B_HEADS = 6
B_WIDTH = B_HEADS * HEAD_DIM
C_HEADS = 4
C_KV_HEADS = 2
C_GROUP = C_HEADS // C_KV_HEADS
C_WIDTH = C_HEADS * HEAD_DIM
C_KV_WIDTH = C_KV_HEADS * HEAD_DIM
MIX_WIDTH = A_WIDTH + B_WIDTH + C_WIDTH
CONV_K = 5
CHUNK = 64
WINDOW = 128
BLOCK = 128
PLE_DIM = 256
EPS = 1e-6
IN_COLS = (3 * A_WIDTH, A_WIDTH, 2 * A_HEADS, 2 * A_HEADS,
           B_WIDTH, B_WIDTH, B_WIDTH, B_WIDTH,
           C_WIDTH, C_KV_WIDTH, C_KV_WIDTH, C_WIDTH)
IN_WIDTH = sum(IN_COLS)

kernel_name = 'hybrid_bidir_deltanet_retention_swa_encoder'


def rms_norm(x, g):
    xf = x.astype(jnp.float32)
    y = xf * lax.rsqrt(jnp.mean(xf * xf, axis=-1, keepdims=True) + EPS)
    return (y * g.astype(jnp.float32)).astype(x.dtype)


def l2_normalize(x):
    return x * lax.rsqrt(jnp.sum(x * x, axis=-1, keepdims=True) + EPS)


def head_layer_norm(x, g):
    mu = jnp.mean(x, axis=-1, keepdims=True)
    xc = x - mu
    y = xc * lax.rsqrt(jnp.mean(xc * xc, axis=-1, keepdims=True) + EPS)
    return y.reshape(x.shape[:2] + (-1,)) * g


def centred_dwconv(x, w):
    return lax.conv_general_dilated(
        x, w[:, None, :], window_strides=(1,),
        padding=[(CONV_K // 2, CONV_K // 2)],
        dimension_numbers=('NWC', 'WIO', 'NWC'),
        feature_group_count=x.shape[-1])


def to_chunks(a, size):
    b, t, h = a.shape[:3]
    a = a.reshape((b, t // size, size, h) + a.shape[3:])
    return jnp.moveaxis(a, (1, 3), (0, 2))


def from_chunks(a):
    n, b, h, c = a.shape[:4]
    a = jnp.moveaxis(a, (0, 2), (1, 3))
    return a.reshape((b, n * c, h) + a.shape[4:])


def flip_t(a):
    return jnp.flip(a, axis=1)


def gated_delta_rule(q, k, v, beta, g):
    b, t, h, dk = q.shape
    dv = v.shape[-1]
    q = to_chunks(q, CHUNK) * (dk ** -0.5)
    k = to_chunks(k, CHUNK)
    v = to_chunks(v, CHUNK)
    beta = to_chunks(beta, CHUNK)
    decay = jnp.cumsum(to_chunks(g, CHUNK), axis=-1)
    idx = jnp.arange(CHUNK)
    incl = idx[:, None] >= idx[None, :]
    strict = idx[:, None] > idx[None, :]
    gamma = jnp.exp(jnp.where(incl, decay[..., :, None] - decay[..., None, :], -jnp.inf))
    k_beta = k * beta[..., None]
    a_mat = jnp.einsum('nbhcd,nbhsd->nbhcs', k_beta, k) * gamma
    t_mat = jnp.where(strict, a_mat, 0.0) + jnp.eye(CHUNK, dtype=q.dtype)
    rhs = jnp.concatenate([v * beta[..., None], k_beta * jnp.exp(decay)[..., None]], axis=-1)
    sol = lax.linalg.triangular_solve(t_mat, rhs, left_side=True, lower=True, unit_diagonal=True)
    u, w = sol[..., :dv], sol[..., dv:]
    qk = jnp.einsum('nbhcd,nbhsd->nbhcs', q, k) * gamma
    q_dec = q * jnp.exp(decay)[..., None]
    k_dec = k * jnp.exp(decay[..., -1:] - decay)[..., None]
    chunk_decay = jnp.exp(decay[..., -1])[..., None, None]

    def step(state, inp):
        u_c, w_c, qk_c, qd_c, kd_c, cd_c = inp
        v_new = u_c - jnp.einsum('bhcd,bhde->bhce', w_c, state)
        o = jnp.einsum('bhcd,bhde->bhce', qd_c, state) + jnp.einsum('bhcs,bhse->bhce', qk_c, v_new)
        state = state * cd_c + jnp.einsum('bhcd,bhce->bhde', kd_c, v_new)
        return state, o

    s0 = jnp.zeros((b, h, dk, dv), q.dtype)
    _, o = lax.scan(step, s0, (u, w, qk, q_dec, k_dec, chunk_decay))
    return from_chunks(o)


def retention(q, k, v, log_gamma):
    b, t, h, dk = q.shape
    dv = v.shape[-1]
    q = to_chunks(q, CHUNK)
    k = to_chunks(k, CHUNK) * (dk ** -0.5)
    v = to_chunks(v, CHUNK)
    pos = jnp.arange(CHUNK, dtype=jnp.float32)
    idx = jnp.arange(CHUNK)
    incl = idx[:, None] >= idx[None, :]
    lg = log_gamma[:, None]
    intra_decay = jnp.exp(jnp.where(incl[None], (pos[:, None] - pos[None, :])[None] * lg[:, :, None], -jnp.inf))
    o_intra = jnp.einsum('nbhcs,nbhse->nbhce', jnp.einsum('nbhcd,nbhsd->nbhcs', q, k) * intra_decay, v)
    q_dec = q * jnp.exp((pos[None, :] + 1.0) * lg)[..., None]
    k_dec = k * jnp.exp((CHUNK - 1.0 - pos[None, :]) * lg)[..., None]
    chunk_decay = jnp.exp(CHUNK * log_gamma)[:, None, None]

    def step(state, inp):
        qd_c, kd_c, v_c = inp
        o = jnp.einsum('bhcd,bhde->bhce', qd_c, state)
        state = state * chunk_decay + jnp.einsum('bhcd,bhce->bhde', kd_c, v_c)
        return state, o

    s0 = jnp.zeros((b, h, dk, dv), q.dtype)
    _, o_inter = lax.scan(step, s0, (q_dec, k_dec, v))
    return from_chunks(o_intra + o_inter)


def window_attention(q, k, v, sink):
    b, t, _, d = q.shape
    nb = t // BLOCK
    qb = q.reshape(b, nb, BLOCK, C_KV_HEADS, C_GROUP, d)

    def band(a):
        ap = jnp.pad(a, ((0, 0), (BLOCK, BLOCK), (0, 0), (0, 0))).reshape(b, nb + 2, BLOCK, C_KV_HEADS, d)
        return jnp.concatenate([ap[:, :-2], ap[:, 1:-1], ap[:, 2:]], axis=2)

    kb, vb = band(k), band(v)
    scores = jnp.einsum('bnqhgd,bnkhd->bnhgqk', qb, kb) * (d ** -0.5)
    qi = jnp.arange(BLOCK)
    kj = jnp.arange(3 * BLOCK)
    rel = kj[None, :] - BLOCK - qi[:, None]
    key_pos = jnp.arange(nb)[:, None] * BLOCK - BLOCK + kj[None, :]
    mask = (jnp.abs(rel) <= WINDOW)[None] & ((key_pos >= 0) & (key_pos < t))[:, None, :]
    slopes = (2.0 ** (-8.0 * jnp.arange(1, C_HEADS + 1, dtype=jnp.float32) / C_HEADS)).reshape(C_KV_HEADS, C_GROUP)
    alibi = -slopes[:, :, None, None] * jnp.abs(rel).astype(jnp.float32)[None, None]
    scores = jnp.where(mask[None, :, None, None], scores + alibi, -jnp.inf)
    sink = sink.reshape(C_KV_HEADS, C_GROUP)[:, :, None, None]
    m = jnp.maximum(jnp.max(scores, axis=-1, keepdims=True), sink)
    pr = jnp.exp(scores - m)
    pr = pr / (jnp.sum(pr, axis=-1, keepdims=True) + jnp.exp(sink - m))
    o = jnp.einsum('bnhgqk,bnkhd->bnqhgd', pr, vb)
    return o.reshape(b, t, C_WIDTH)


def encoder_layer(h, p_i, w_in, w_out, norm_g, conv_w, dn_a_log, dn_dt_bias, dn_norm_g,
                  ret_decay_z, ret_norm_g, attn_sink, w_ple, w_pg):
    f32 = jnp.float32
    b, t, _ = h.shape
    xn = rms_norm(h, norm_g)
    proj = jnp.matmul(xn, w_in).astype(f32)
    splits = [int(s) for s in np.cumsum(IN_COLS)[:-1]]
    (a_qkv, a_gate, a_beta, a_alpha, b_q, b_k, b_v, b_gate,
     c_q, c_k, c_v, c_gate) = jnp.split(proj, splits, axis=-1)

    def heads(z, n):
        return z.reshape(b, t, n, HEAD_DIM)

    qkv = jax.nn.silu(centred_dwconv(a_qkv, conv_w.astype(f32)))
    aq, ak, av = jnp.split(qkv, 3, axis=-1)
    aq = l2_normalize(heads(aq, A_HEADS))
    ak = l2_normalize(heads(ak, A_HEADS))
    av = heads(av, A_HEADS)
    beta = jax.nn.sigmoid(a_beta.reshape(b, t, 2, A_HEADS))
    g = -jnp.exp(dn_a_log.astype(f32)) * jax.nn.softplus(a_alpha.reshape(b, t, 2, A_HEADS) + dn_dt_bias.astype(f32))
    o_fwd = gated_delta_rule(aq, ak, av, beta[:, :, 0], g[:, :, 0])
    o_bwd = flip_t(gated_delta_rule(flip_t(aq), flip_t(ak), flip_t(av), flip_t(beta[:, :, 1]), flip_t(g[:, :, 1])))
    o_a = (rms_norm(o_fwd + o_bwd, dn_norm_g) * jax.nn.silu(heads(a_gate, A_HEADS))).reshape(b, t, A_WIDTH)

    log_gamma = jax.nn.log_sigmoid(ret_decay_z.astype(f32))
    bq, bk, bv = heads(b_q, B_HEADS), heads(b_k, B_HEADS), heads(b_v, B_HEADS)
    r = retention(bq, bk, bv, log_gamma[0]) + flip_t(retention(flip_t(bq), flip_t(bk), flip_t(bv), log_gamma[1]))
    o_b = head_layer_norm(r, ret_norm_g.astype(f32)) * jax.nn.silu(b_gate)

    o_c = window_attention(heads(c_q, C_HEADS), heads(c_k, C_KV_HEADS), heads(c_v, C_KV_HEADS),
                           attn_sink.astype(f32)) * jax.nn.silu(c_gate)

    mix = jnp.concatenate([o_a, o_b, o_c], axis=-1).astype(h.dtype)
    h = h + jnp.matmul(mix, w_out)
    h = h + jnp.matmul(p_i, w_ple) * jax.nn.sigmoid(jnp.matmul(h, w_pg))
    return h


def encoder_trunk(x, p, w_in, w_out, norm_g, conv_w, dn_a_log, dn_dt_bias, dn_norm_g,
                  ret_decay_z, ret_norm_g, attn_sink, w_ple, w_pg, final_g):
    h = x
    for i in range(DEPTH):
        h = encoder_layer(h, p[i], w_in[i], w_out[i], norm_g[i], conv_w[i], dn_a_log[i], dn_dt_bias[i],
                          dn_norm_g[i], ret_decay_z[i], ret_norm_g[i], attn_sink[i], w_ple[i], w_pg[i])
    return rms_norm(h, final_g)


def setup_inputs(seed: int = 0) -> dict:
    key = jax.random.key(seed)
    ks = jax.random.split(key, 18)
    f32 = jnp.float32

    def nrm(k, shape, scale):
        return jax.random.normal(k, shape, f32) * scale

    h_idx = jnp.arange(B_HEADS, dtype=f32)
    ret_gamma = 1.0 - 2.0 ** (-5.0 - h_idx)
    ret_z0 = jnp.log(ret_gamma) - jnp.log1p(-ret_gamma)
    dt = jnp.exp(jax.random.uniform(ks[8], (DEPTH, 2, A_HEADS), f32) * (jnp.log(0.1) - jnp.log(0.001)) + jnp.log(0.001))
    return {
        'x_prompt': nrm(ks[0], (BATCH, SEQ, D_MODEL), 1.0),
        'x_sample': nrm(ks[1], (DEC_BATCH, DEC_SEQ, D_MODEL), 1.0),
        'p_prompt': nrm(ks[2], (DEPTH, BATCH, SEQ, PLE_DIM), 1.0),
        'p_sample': nrm(ks[3], (DEPTH, DEC_BATCH, DEC_SEQ, PLE_DIM), 1.0),
        'w_in': nrm(ks[4], (DEPTH, D_MODEL, IN_WIDTH), D_MODEL ** -0.5),
        'w_out': nrm(ks[5], (DEPTH, MIX_WIDTH, D_MODEL), MIX_WIDTH ** -0.5),
        'norm_g': 1.0 + nrm(ks[6], (DEPTH, D_MODEL), 0.05),
        'conv_w': nrm(ks[7], (DEPTH, CONV_K, 3 * A_WIDTH), CONV_K ** -0.5),
        'dn_a_log': jnp.log(jax.random.uniform(ks[9], (DEPTH, 2, A_HEADS), f32, 1.0, 16.0)),
        'dn_dt_bias': dt + jnp.log(-jnp.expm1(-dt)),
        'dn_norm_g': 1.0 + nrm(ks[10], (DEPTH, HEAD_DIM), 0.05),
        'ret_decay_z': ret_z0 + nrm(ks[11], (DEPTH, 2, B_HEADS), 0.1),
        'ret_norm_g': 1.0 + nrm(ks[12], (DEPTH, B_WIDTH), 0.05),
        'attn_sink': nrm(ks[13], (DEPTH, C_HEADS), 0.5),
        'w_ple': nrm(ks[14], (DEPTH, PLE_DIM, D_MODEL), PLE_DIM ** -0.5),
        'w_pg': nrm(ks[15], (DEPTH, D_MODEL, D_MODEL), D_MODEL ** -0.5),
        'final_g': 1.0 + nrm(ks[16], (D_MODEL,), 0.05),
    }


def reference(x_prompt, x_sample, p_prompt, p_sample, w_in, w_out, norm_g, conv_w, dn_a_log, dn_dt_bias,
              dn_norm_g, ret_decay_z, ret_norm_g, attn_sink, w_ple, w_pg, final_g):
    y_prompt = encoder_trunk(x_prompt, p_prompt, w_in, w_out, norm_g, conv_w, dn_a_log, dn_dt_bias,
                             dn_norm_g, ret_decay_z, ret_norm_g, attn_sink, w_ple, w_pg, final_g)
    y_sample = encoder_trunk(x_sample, p_sample, w_in, w_out, norm_g, conv_w, dn_a_log, dn_dt_bias,
                             dn_norm_g, ret_decay_z, ret_norm_g, attn_sink, w_ple, w_pg, final_g)
    return (y_prompt, y_sample)
```

```python
from contextlib import ExitStack
import numpy as np
import concourse.bass as bass
import concourse.mybir as mybir
from concourse.bass_utils import run_bass_kernel_spmd

F32 = mybir.dt.float32
BF16 = mybir.dt.bfloat16
ALU = mybir.AluOpType
AF = mybir.ActivationFunctionType
AX = mybir.AxisListType

D = 1024
EPS = 1e-6
NEG = -30000.0
PLE = 256
PADS = 640
EPOCH = 24000
NCONST = 11 * 128 + 4 + 4 * 3 * 128 + 64 + 128


class Res:
    __slots__ = ("name", "last_w", "readers")

    def __init__(self, name=""):
        self.name = name
        self.last_w = None
        self.readers = []


class SemCtr:
    __slots__ = ("nc", "name", "sems", "count")

    def __init__(self, nc, name):
        self.nc = nc
        self.name = name
        self.sems = []
        self.count = 0

    def next_event(self, inc):
        ep = self.count // EPOCH
        while len(self.sems) <= ep:
            self.sems.append(self.nc.alloc_semaphore(name=f"{self.name}_{len(self.sems)}"))
        self.count += inc
        return self.sems[ep], self.count - ep * EPOCH


class Op:
    __slots__ = ("eng", "fn", "deps", "needs_inc", "ev_sem", "ev_val", "dsem")

    def __init__(self, eng, fn, dsem):
        self.eng = eng
        self.fn = fn
        self.deps = None
        self.needs_inc = False
        self.ev_sem = None
        self.ev_val = None
        self.dsem = dsem


ENGS = ("pe", "act", "dve", "pool", "sp")


class Prog:
    def __init__(self, nc):
        self.nc = nc
        self.ops = {e: [] for e in ENGS}
        self.final_ops = []
        self.extra = []
        self.last_dma = {}
        self.nd = 0

    def new_dsem(self, name=None):
        self.nd += 1
        return SemCtr(self.nc, name or f"dq{self.nd}")

    def barrier(self):
        ex = []
        for e in ENGS:
            for o in reversed(self.ops[e]):
                if o.dsem is None:
                    ex.append(o)
                    break
        ex.extend(self.last_dma.values())
        for o in ex:
            o.needs_inc = True
        self.extra = ex

    def op(self, eng, fn, reads=(), writes=(), dsem=None, after=()):
        o = Op(eng, fn, dsem)
        deps = list(after)
        raw = set()
        for r in reads:
            if r.last_w is not None:
                deps.append(r.last_w)
                raw.add(id(r.last_w))
        for w in writes:
            if w.last_w is not None:
                deps.append(w.last_w)
            deps.extend(w.readers)
        deps.extend(self.extra)
        fl = []
        seen = set()
        for d in deps:
            if id(d) in seen or d is o:
                continue
            seen.add(id(d))
            if d.dsem is None and d.eng == eng and eng == "pe":
                continue
            fl.append(d)
            d.needs_inc = True
        o.deps = fl
        for r in reads:
            r.readers.append(o)
        for w in writes:
            w.last_w = o
            w.readers = []
        self.ops[eng].append(o)
        if dsem is not None:
            self.last_dma[id(dsem)] = o
        return o

    def emit(self):
        nc = self.nc
        for e in ENGS:
            ctr = SemCtr(nc, f"eng_{e}")
            for o in self.ops[e]:
                if o.dsem is not None:
                    o.ev_sem, o.ev_val = o.dsem.next_event(16)
                elif o.needs_inc:
                    o.ev_sem, o.ev_val = ctr.next_event(1)
        final_waits = [(o.ev_sem, o.ev_val) for o in self.final_ops]
        prog = self

        def run_engine(ename, eng):
            waited = {}
            for o in prog.ops[ename]:
                need = {}
                for d in o.deps:
                    k = id(d.ev_sem)
                    if k not in need or need[k][1] < d.ev_val:
                        need[k] = (d.ev_sem, d.ev_val)
                for k, (s, v) in need.items():
                    if waited.get(k, 0) >= v:
                        continue
                    eng.wait_ge(s, v)
                    waited[k] = v
                inst = o.fn(eng)
                if o.dsem is not None:
                    inst.then_inc(o.ev_sem, 16)
                elif o.needs_inc:
                    inst.then_inc(o.ev_sem, 1)
            if ename == "sp":
                for (s, v) in final_waits:
                    eng.wait_ge(s, v)

        with nc.Block() as block:
            @block.tensor
            def _(eng):
                run_engine("pe", eng)

            @block.scalar
            def _(eng):
                run_engine("act", eng)

            @block.vector
            def _(eng):
                run_engine("dve", eng)

            @block.gpsimd
            def _(eng):
                run_engine("pool", eng)

            @block.sync
            def _(eng):
                run_engine("sp", eng)


class Tl:
    __slots__ = ("ap", "res")

    def __init__(self, ap, name=""):
        self.ap = ap
        self.res = Res(name)

    def __getitem__(self, k):
        return self.ap[k]


def _rl(lst):
    out = []
    for t in lst:
        if isinstance(t, Res):
            out.append(t)
        elif isinstance(t, Tl):
            out.append(t.res)
        else:
            out.extend(_rl(t))
    return out


def make_consts():
    p = np.arange(128)[:, None].astype(np.float64)
    f = np.arange(128)[None, :].astype(np.float64)
    c = np.zeros((128, NCONST), np.float32)
    o = 0

    def put(a):
        nonlocal o
        a = np.asarray(a, np.float32)
        c[:, o:o + a.shape[1]] = a
        o += a.shape[1]

    put(p == f)
    put(np.where(f < p, 0.0, NEG))
    put(np.where(f > p, 0.0, NEG))
    put(p <= f)
    put(p >= f)
    put(np.ones((128, 128)))
    put((p // 64) == (f // 64))
    put(np.maximum(f - p, 0))
    put(np.maximum(p - f, 0))
    put(p <= f)
    put(p >= f)
    put(np.concatenate([p + 1, 128 - p, 127 - p, p], axis=1))
    b = np.zeros((128, 4, 3, 128), np.float64)
    for h in range(4):
        slope = 2.0 ** (-8.0 * (h + 1) / 4)
        for jj in range(3):
            rel = (jj - 1) * 128 + p - f
            b[:, h, jj, :] = np.where(np.abs(rel) <= 128, -slope * np.abs(rel), NEG)
    put(b.reshape(128, -1))
    put(np.ones((128, 64)))
    put((p // 64) != (f // 64))
    assert o == NCONST
    return c


def build_program(seqs, L=2, dbg=False):
    nc = bass.Bass("TRN2", target_bir_lowering=False)
    P = Prog(nc)
    TOK = sum(seqs)
    xoff = [sum(seqs[:i]) for i in range(len(seqs))]
    bases = []
    g = 0
    for T in seqs:
        bases.append(g)
        g += T + PADS
    G = g

    def din(name, shape, dt=F32):
        return nc.dram_tensor(name, list(shape), dt, kind="ExternalInput").ap()

    def dscr(name, shape, dt):
        kind = "ExternalOutput" if (dbg and name in ("of", "ob", "oc", "aqT", "akT", "aktm", "avtm", "tsc", "dT")) else "Internal"
        return nc.dram_tensor(name, list(shape), dt, kind=kind).ap()

    x_d = din("x", [TOK, D])
    p_d = din("p", [L, TOK, PLE])
    wfm_d = din("wfm", [L, 128, 8, 2304])
    wtm_d = din("wtm", [L, 128, 8, 1944])
    wout_d = din("wout", [L, 128, 8, 1024])
    wpg_d = din("wpg", [L, 128, 8, 1024])
    wple_d = din("wple", [L, 128, 2, 1024])
    normg_d = din("normg", [L, 128, 8])
    convw_d = din("convw", [L, 128, 9, 5])
    alog_d = din("alog", [L, 12])
    dtb_d = din("dtb", [L, 12])
    dng_d = din("dng", [L, 64])
    retz_d = din("retz", [L, 12])
    retg_d = din("retg", [L, 384])
    sink_d = din("sink", [L, 4])
    fing_d = din("fing", [1, D])
    consts_d = din("consts", [128, NCONST])
    y_d = nc.dram_tensor("y", [TOK, D], F32, kind="ExternalOutput").ap()

    aqT_d = dscr("aqT", [384, G], BF16)
    akT_d = dscr("akT", [384, G], BF16)
    aktm_d = dscr("aktm", [G, 384], BF16)
    avtm_d = dscr("avtm", [G, 384], BF16)
    bqT_d = dscr("bqT", [384, G], BF16)
    bkT_d = dscr("bkT", [384, G], BF16)
    cqT_d = dscr("cqT", [256, G], BF16)
    ckT_d = dscr("ckT", [128, G], BF16)
    kvtm_d = dscr("kvtm", [G, 896], BF16)
    gate_d = dscr("gatetm", [G, 1024], BF16)
    dT_d = dscr("dT", [2, 6, G], F32)
    tsc_d = dscr("tsc", [G, 84], F32)
    of_d = dscr("of", [G, 768], F32)
    ob_d = dscr("ob", [G, 768], F32)
    oc_d = dscr("oc", [G, 256], F32)
    h_d = dscr("hbuf", [G, D], F32)

    dres = {}

    def DR(name, lo, hi):
        out = []
        for cidx in range(lo // 128, (hi - 1) // 128 + 1):
            k = (name, cidx)
            if k not in dres:
                dres[k] = Res(f"{name}{cidx}")
            out.append(dres[k])
        return out

    def mm(out, lhsT, rhs, R, W, start=True, stop=True):
        P.op("pe", lambda e: e.matmul(out, lhsT=lhsT, rhs=rhs, start=start, stop=stop), _rl(R), _rl(W))

    def tr(out, in_, ident, R, W):
        P.op("pe", lambda e: e.transpose(out=out, in_=in_, identity=ident), _rl(R), _rl(W))

    def act(out, in_, func, R, W, bias=None, scale=None, accum=None):
        kw = {}
        if bias is not None:
            kw["bias"] = bias
        if scale is not None:
            kw["scale"] = scale
        if accum is not None:
            kw["accum_out"] = accum
        P.op("act", lambda e: e.activation(out=out, in_=in_, func=func, **kw), _rl(R), _rl(W))

    def tt(eng, out, in0, in1, op, R, W):
        P.op(eng, lambda e: e.tensor_tensor(out=out, in0=in0, in1=in1, op=op), _rl(R), _rl(W))

    def ts(eng, out, in0, s1, op0, R, W, s2=None, op1=None):
        if op1 is None:
            P.op(eng, lambda e: e.tensor_scalar(out=out, in0=in0, scalar1=s1, scalar2=None, op0=op0), _rl(R), _rl(W))
        else:
            P.op(eng, lambda e: e.tensor_scalar(out=out, in0=in0, scalar1=s1, scalar2=s2, op0=op0, op1=op1),
                 _rl(R), _rl(W))

    def stt(eng, out, in0, scalar, in1, op0, op1, R, W):
        P.op(eng, lambda e: e.scalar_tensor_tensor(out=out, in0=in0, scalar=scalar, in1=in1, op0=op0, op1=op1),
             _rl(R), _rl(W))

    def cp(eng, out, in_, R, W):
        if eng == "act":
            P.op("act", lambda e: e.activation(out=out, in_=in_, func=AF.Copy), _rl(R), _rl(W))
        else:
            P.op(eng, lambda e: e.tensor_copy(out=out, in_=in_), _rl(R), _rl(W))

    def recip(out, in_, R, W):
        P.op("dve", lambda e: e.reciprocal(out=out, in_=in_), _rl(R), _rl(W))

    def mset(eng, out, val, W):
        P.op(eng, lambda e: e.memset(out, val), (), _rl(W))

    dq = {}
    DQK = {"st_qk": 4, "st_fm": 4, "g0": 6, "g1": 6, "gc": 4}

    def dma(key, out, in_, R, W, final=False, eng="sp"):
        if key not in dq:
            k = DQK.get(key, 2)
            dq[key] = [[P.new_dsem(f"d_{key}{i}"), None] for i in range(k)] + [0]
        ent = dq[key]
        slot = ent[ent[-1] % (len(ent) - 1)]
        ent[-1] += 1
        o = P.op(eng, lambda e: e.dma_start(out=out, in_=in_), _rl(R), _rl(W), dsem=slot[0],
                 after=([slot[1]] if slot[1] is not None else ()))
        slot[1] = o
        if final:
            P.final_ops.append(o)
        return o

    banks = []
    for i in range(8):
        t = nc.alloc_psum_tensor(f"pb{i}", [128, 512], F32)
        banks.append(Tl(t.ap(), f"pb{i}"))
    bstate = [0]

    def bank():
        b = banks[bstate[0] % 8]
        bstate[0] += 1
        return b

    def bf(b):
        return b.ap.bitcast(BF16)

    uid = [0]

    def un(name):
        uid[0] += 1
        return f"s{uid[0]}_{name}"

    def sb(name, shape, dt):
        return Tl(nc.alloc_sbuf_tensor(un(name), list(shape), dt).ap(), name)

    cst = sb("consts", [128, NCONST], F32)
    dma("c0", cst.ap, consts_d, [], [cst])

    def cblk(i):
        return cst[:, i * 128:(i + 1) * 128]

    identf, NMf, NMb, CUMf, CUMb, onesf = [cblk(i) for i in range(6)]
    RELP, RELN, MASKF, MASKB = [cblk(i) for i in range(7, 11)]
    posc = cst[:, 1408:1412]
    cbias = cst[:, 1412:1412 + 1536].rearrange("p (h j q) -> p h j q", h=4, j=3)
    ones64 = cst[:, 1412 + 1536:1412 + 1536 + 64]
    OFF64 = cst[:, 1412 + 1536 + 64:1412 + 1536 + 64 + 128]
    BD64 = cblk(6)
    identb = sb("identb", [128, 128], BF16)
    cp("dve", identb.ap, identf, [cst], [identb])
    blockones = sb("blockones", [128, 128], BF16)
    cp("dve", blockones.ap, cblk(6), [cst], [blockones])
    NM = [NMf, NMb]
    CUM = [CUMf, CUMb]
    epsc = sb("epsc", [128, 1], F32)
    mset("pool", epsc.ap, EPS, [epsc])

    rings = {}

    def ring(es, name, shape, dt, n=2):
        tl = [Tl(es.enter_context(nc.sbuf_tensor(un(f"{name}{i}"), list(shape), dt)).ap(), f"{name}{i}")
              for i in range(n)]
        st = [0]

        def nxt():
            t = tl[st[0] % n]
            st[0] += 1
            return t
        return nxt

    def one(es, name, shape, dt):
        return Tl(es.enter_context(nc.sbuf_tensor(un(name), list(shape), dt)).ap(), name)

    ENG3 = ("dve", "pool")
    rr = [0]

    def alt(choices=("act", "dve")):
        rr[0] += 1
        return choices[rr[0] % len(choices)]

    def phase1(l):
        src_d = x_d if l == 0 else h_d
        with ExitStack() as es:
            Wfm = es.enter_context(nc.sbuf_tensor(un("Wfm"), [128, 8, 2304], BF16)).ap()
            Wtm = es.enter_context(nc.sbuf_tensor(un("Wtm"), [128, 8, 1944], BF16)).ap()
            Wfm_r = [Res(f"Wfm{i}") for i in range(9)]
            Wtm_r = [Res(f"Wtm{i}") for i in range(8)]
            with ExitStack() as es2:
                stg = ring(es2, "wstg", [128, 8, 256], F32, 2)
                for (dst, src, ncols, rs) in ((Wfm, wfm_d[l], 2304, Wfm_r), (Wtm, wtm_d[l], 1944, Wtm_r)):
                    for bi, c0 in enumerate(range(0, ncols, 256)):
                        cw = min(256, ncols - c0)
                        st = stg()
                        dma("wld", st[:, :, 0:cw], src[:, :, c0:c0 + cw], [], [st])
                        cp(alt(("act", "dve")), dst[:, :, c0:c0 + cw], st[:, :, 0:cw], [st], [rs[bi]])
                P.barrier()
            normg = one(es, "normg", [128, 8], F32)
            dma("sm", normg.ap, normg_d[l], [], [normg])
            cw_t = one(es, "convw", [128, 9, 5], F32)
            dma("sm", cw_t.ap, convw_d[l], [], [cw_t])
            negA = one(es, "negA", [128, 12], F32)
            dma("sm", negA.ap, alog_d[l:l + 1, :].partition_broadcast(128).rearrange("p a b -> p (a b)"), [], [negA])
            act(negA.ap, negA.ap, AF.Exp, [negA], [negA])
            ts("dve", negA.ap, negA.ap, -1.0, ALU.mult, [negA], [negA])
            dtb = one(es, "dtb", [128, 12], F32)
            dma("sm", dtb.ap, dtb_d[l:l + 1, :].partition_broadcast(128).rearrange("p a b -> p (a b)"), [], [dtb])

            xt_r = ring(es, "xt", [128, 4, 1024], F32, 1)
            junk = one(es, "junk", [128, 1024], BF16)
            ss_r = ring(es, "ss", [128, 4], F32, 2)
            xnb_r = ring(es, "xnb", [128, 4, 1024], BF16, 1)
            xnT_r = ring(es, "xnT", [128, 8, 512], BF16, 2)
            Rb = es.enter_context(nc.sbuf_tensor(un("Rraw"), [128, 9, 516], F32)).ap()
            R_r = [Res(f"R{m}") for m in range(9)]
            acc_r = ring(es, "acc", [128, 512], F32, 4)
            tmpc_r = ring(es, "tmpc", [128, 512], F32, 2)
            sact_r = ring(es, "sact", [128, 512], F32, 4)
            sq_r = ring(es, "sq", [128, 512], BF16, 4)
            rn_r = ring(es, "rn", [128, 512], F32, 4)
            qn_r = ring(es, "qn", [128, 512], BF16, 5)
            stgb_r = ring(es, "stgb", [128, 512], BF16, 5)
            kst_r = ring(es, "kst", [128, 4, 384], BF16, 1)
            vst_r = ring(es, "vst", [128, 4, 384], BF16, 1)
            gst_r = ring(es, "gst", [128, 1024], BF16, 2)
            kvst_r = ring(es, "kvst", [128, 896], BF16, 2)
            tsst_r = ring(es, "tsst", [128, 4, 84], F32, 1)
            dts_r = ring(es, "dts", [6, 2, 512], F32, 1)
            ba_r = ring(es, "ba", [128, 24], F32, 5)
            sm_r = ring(es, "smt", [128, 12 * 8], F32, 5)

            def win_m(m, lo, hi, kst, vst):
                acc = acc_r()
                if m != 4:
                    ts("dve", acc.ap, Rb[:, m, 0:512], cw_t[:, m, 0:1], ALU.mult, [R_r[m], cw_t], [acc])
                    yield
                    for j in range(1, 5):
                        stt("dve", acc.ap, Rb[:, m, j:j + 512], cw_t[:, m, j:j + 1], acc.ap, ALU.mult, ALU.add,
                            [R_r[m], cw_t, acc], [acc])
                        yield
                else:
                    act(acc.ap, Rb[:, m, 0:512], AF.Copy, [R_r[m], cw_t], [acc], scale=cw_t[:, m, 0:1])
                    yield
                    for j in range(1, 5):
                        tmpc = tmpc_r()
                        act(tmpc.ap, Rb[:, m, j:j + 512], AF.Copy, [R_r[m], cw_t], [tmpc], scale=cw_t[:, m, j:j + 1])
                        tt("pool", acc.ap, acc.ap, tmpc.ap, ALU.add, [acc, tmpc], [acc])
                        yield
                cp("pool", Rb[:, m, 0:4], Rb[:, m, 512:516], [R_r[m]], [R_r[m]])
                sact = sact_r()
                act(sact.ap, acc.ap, AF.Silu, [acc], [sact])
                yield
                if m < 6:
                    sq = sq_r()
                    act(sq.ap, sact.ap, AF.Square, [sact], [sq])
                    yield
                    pb = bank()
                    mm(pb.ap, blockones.ap, sq.ap, [blockones, sq], [pb])
                    rn = rn_r()
                    act(rn.ap, pb.ap, AF.Ln, [pb, epsc], [rn, pb], bias=epsc[:, 0:1])
                    yield
                    act(rn.ap, rn.ap, AF.Exp, [rn], [rn], scale=-0.5)
                    yield
                    qn = qn_r()
                    stt("dve", qn.ap, sact.ap, 0.125 if m < 3 else 1.0, rn.ap, ALU.mult, ALU.mult,
                        [sact, rn], [qn])
                    dst = aqT_d if m < 3 else akT_d
                    nm = "aqT" if m < 3 else "akT"
                    r0 = (m % 3) * 128
                    dma("st_qk", dst[r0:r0 + 128, lo:hi], qn.ap, [qn], DR(nm, lo, hi))
                    yield
                    if m >= 3:
                        pb2 = bank()
                        for s in range(4):
                            tr(bf(pb2)[:, s * 128:(s + 1) * 128], qn[:, s * 128:(s + 1) * 128], identb.ap,
                               [qn, identb], [pb2])
                        cp(alt(), kst[:, :, (m - 3) * 128:(m - 2) * 128],
                           bf(pb2)[:, 0:512].rearrange("p (s c) -> p s c", s=4), [pb2], [kst, pb2])
                        yield
                else:
                    vb = qn_r()
                    cp("dve", vb.ap, sact.ap, [sact], [vb])
                    yield
                    pb2 = bank()
                    for s in range(4):
                        tr(bf(pb2)[:, s * 128:(s + 1) * 128], vb[:, s * 128:(s + 1) * 128], identb.ap,
                           [vb, identb], [pb2])
                    cp(alt(), vst[:, :, (m - 6) * 128:(m - 5) * 128],
                       bf(pb2)[:, 0:512].rearrange("p (s c) -> p s c", s=4), [pb2], [vst, pb2])
                    yield

            def window(si, T, t0, base):
                kst = kst_r()
                vst = vst_r()
                lo, hi = base + t0, base + t0 + 512
                pending = [win_m(m, lo, hi, kst, vst) for m in range(9)]
                live = []
                while pending or live:
                    while len(live) < 3 and pending:
                        live.append(pending.pop(0))
                    nxt = []
                    for ch in live:
                        try:
                            next(ch)
                            nxt.append(ch)
                        except StopIteration:
                            pass
                    live = nxt
                    yield
                dma("st_ktm", aktm_d[lo:hi, :].rearrange("(s p) f -> p s f", p=128), kst.ap, [kst],
                    DR("aktm", lo, hi))
                dma("st_vtm", avtm_d[lo:hi, :].rearrange("(s p) f -> p s f", p=128), vst.ap, [vst],
                    DR("avtm", lo, hi))

            for si, T in enumerate(seqs):
                base = bases[si]
                hb = xoff[si] if l == 0 else base
                for m in range(9):
                    mset("pool", Rb[:, m, 0:4], 0.0, [R_r[m]])
                def partA(t0):
                    lo, hi = base + t0, base + t0 + 512
                    xt = xt_r()
                    rd = [] if l == 0 else DR("h", hb + t0, hb + t0 + 512)
                    dma("ld_x", xt.ap, src_d[hb + t0:hb + t0 + 512, :].rearrange("(s p) f -> p s f", p=128), rd, [xt])
                    ss = ss_r()
                    mset("pool", ss.ap, 0.0, [ss])
                    for s in range(4):
                        act(junk.ap, xt[:, s, :], AF.Square, [xt, ss], [junk, ss], accum=ss[:, s:s + 1])
                    act(ss.ap, ss.ap, AF.Ln, [ss, epsc], [ss], bias=epsc[:, 0:1], scale=1.0 / D)
                    act(ss.ap, ss.ap, AF.Exp, [ss], [ss], scale=-0.5)
                    xnb = xnb_r()
                    for s in range(4):
                        if s % 2 == 0:
                            ts("dve", xnb[:, s, :], xt[:, s, :], ss[:, s:s + 1], ALU.mult, [xt, ss], [xnb])
                        else:
                            act(xnb[:, s, :], xt[:, s, :], AF.Copy, [xt, ss], [xnb], scale=ss[:, s:s + 1])
                    return xnb

                def partA2(xnb):
                    xnT = xnT_r()
                    for k in range(8):
                        pb = bank()
                        for s in range(4):
                            tr(bf(pb)[:, s * 128:(s + 1) * 128], xnb[:, s, k * 128:(k + 1) * 128], identb.ap,
                               [xnb, identb], [pb])
                        if k % 2 == 0:
                            act(xnT[:, k, :], bf(pb)[:, 0:512], AF.Copy, [pb, normg], [xnT, pb], scale=normg[:, k:k + 1])
                        else:
                            ts("dve", xnT[:, k, :], bf(pb)[:, 0:512], normg[:, k:k + 1], ALU.mult, [pb, normg], [xnT, pb])
                    return xnT

                def partC(t0, xnT):
                    lo, hi = base + t0, base + t0 + 512
                    for m in range(18):
                        pb = bank()
                        for k in range(8):
                            mm(pb.ap, Wfm[:, k, m * 128:(m + 1) * 128], xnT[:, k, :], [Wfm_r[(m * 128) // 256], xnT],
                               [pb], start=(k == 0), stop=(k == 7))
                        if m < 9:
                            cp(alt(), Rb[:, m, 4:516], pb.ap, [pb], [R_r[m], pb])
                        else:
                            st = stgb_r()
                            if m < 12:
                                ts("dve", st.ap, pb.ap, 0.125, ALU.mult, [pb], [st, pb])
                                dst, nm, r0 = bqT_d, "bqT", (m - 9) * 128
                            elif m < 15:
                                cp(alt(), st.ap, pb.ap, [pb], [st, pb])
                                dst, nm, r0 = bkT_d, "bkT", (m - 12) * 128
                            elif m < 17:
                                act(st.ap, pb.ap, AF.Copy, [pb], [st, pb], scale=0.125)
                                dst, nm, r0 = cqT_d, "cqT", (m - 15) * 128
                            else:
                                cp(alt(), st.ap, pb.ap, [pb], [st, pb])
                                dst, nm, r0 = ckT_d, "ckT", 0
                            dma("st_fm", dst[r0:r0 + 128, lo:hi], st.ap, [st], DR(nm, lo, hi))

                xnT_next = partA2(partA(0))
                for t0 in range(0, T, 512):
                    lo, hi = base + t0, base + t0 + 512
                    xnT = xnT_next
                    def tm_part(xnT=xnT, lo=lo, hi=hi, t0=t0):
                        for s in range(4):
                            gst = gst_r()
                            kvst = kvst_r()
                            ba = ba_r()
                            for g4 in range(4):
                                c0 = g4 * 512
                                cw = min(512, 1944 - c0)
                                pb = bank()
                                for k in range(8):
                                    mm(pb[:, 0:cw], xnT[:, k, s * 128:(s + 1) * 128], Wtm[:, k, c0:c0 + cw],
                                       [xnT, Wtm_r[c0 // 256], Wtm_r[(c0 + cw - 1) // 256]], [pb],
                                       start=(k == 0), stop=(k == 7))
                                if g4 < 2:
                                    act(gst[:, c0:c0 + 512], pb.ap, AF.Silu, [pb], [gst, pb])
                                elif g4 == 2:
                                    cp(alt(), kvst[:, 0:512], pb.ap, [pb], [kvst, pb])
                                else:
                                    cp(alt(), kvst[:, 512:896], pb[:, 0:384], [pb], [kvst, pb])
                                    cp("dve", ba.ap, pb[:, 384:408], [pb], [ba, pb])
                            sl, sh = lo + s * 128, lo + (s + 1) * 128
                            dma("st_gate", gate_d[sl:sh, :], gst.ap, [gst], DR("gate", sl, sh))
                            dma("st_kv", kvtm_d[sl:sh, :], kvst.ap, [kvst], DR("kvtm", sl, sh))
                            lanes.append(sm_chain(s, ba, tsst_s[s], dts_s[s]))
                            yield
                        yield

                    def sm_chain(s, ba, tsl_t, dts_t):
                        sm = sm_r()
                        z, nz, mn, ee, sp_, gg, bet, tmp = [sm[:, i * 12:(i + 1) * 12] for i in range(8)]
                        tsl = tsl_t.ap
                        act(bet, ba[:, 0:12], AF.Sigmoid, [ba], [sm])
                        tt("dve", z, ba[:, 12:24], dtb.ap, ALU.add, [ba, dtb], [sm])
                        yield
                        cp("dve", tsl[:, 24:36], bet, [sm], [tsl_t])
                        ts("dve", tsl[:, 0:12], bet, -1.0, ALU.mult, [sm], [tsl_t])
                        ts("dve", nz, z, -1.0, ALU.mult, [sm], [sm])
                        yield
                        tt("dve", mn, z, nz, ALU.min, [sm], [sm])
                        yield
                        act(ee, mn, AF.Exp, [sm], [sm])
                        ts("dve", sp_, z, 0.0, ALU.max, [sm], [sm])
                        yield
                        act(ee, ee, AF.Ln, [sm], [sm], bias=1.0)
                        yield
                        tt("dve", sp_, sp_, ee, ALU.add, [sm], [sm])
                        yield
                        tt("dve", gg, sp_, negA.ap, ALU.mult, [sm, negA], [sm])
                        yield
                        pd = bank()
                        mm(pd[:, 0:6], CUMf, gg[:, 0:6], [cst, sm], [pd])
                        mm(pd[:, 6:12], CUMb, gg[:, 6:12], [cst, sm], [pd])
                        mm(pd[:, 12:24], onesf, gg, [cst, sm], [pd])
                        mm(pd[0:6, 128:256], gg[:, 0:6], CUMf, [cst, sm], [pd])
                        mm(pd[0:6, 256:384], gg[:, 6:12], CUMb, [cst, sm], [pd])
                        cp("dve", tsl[:, 12:24], pd[:, 0:12], [pd], [tsl_t, pd])
                        act(tsl[:, 48:60], pd[:, 0:12], AF.Exp, [pd], [tsl_t, pd])
                        act(tsl[:, 72:84], pd[:, 12:24], AF.Exp, [pd], [tsl_t, pd])
                        tt("dve", tmp, pd[:, 12:24], tsl[:, 12:24], ALU.subtract, [pd, tsl_t], [sm, pd])
                        cp("dve", dts_t.ap, pd[0:6, 128:384].rearrange("p (d c) -> p d c", d=2), [pd], [dts_t, pd])
                        yield
                        act(tsl[:, 60:72], tmp, AF.Exp, [sm], [tsl_t])
                        tt("dve", tsl[:, 36:48], tsl[:, 24:36], tsl[:, 48:60], ALU.mult, [tsl_t], [tsl_t])
                        yield

                    def run_l(lanes):
                        live = []
                        idx = 0
                        while idx < len(lanes) or live:
                            while idx < len(lanes):
                                live.append(lanes[idx])
                                idx += 1
                            nxt = []
                            for ch in live:
                                try:
                                    next(ch)
                                    nxt.append(ch)
                                except StopIteration:
                                    pass
                            live = nxt

                    tsst = tsst_r()
                    dts = dts_r()
                    tsst_s = [Tl(tsst[:, s_, :], f"tsst{s_}") for s_ in range(4)]
                    dts_s = [Tl(dts[:, :, s_ * 128:(s_ + 1) * 128], f"dts{s_}") for s_ in range(4)]
                    for s_ in range(4):
                        tsst_s[s_].res.readers = list(tsst.res.readers)
                        tsst_s[s_].res.last_w = tsst.res.last_w
                        dts_s[s_].res.readers = list(dts.res.readers)
                        dts_s[s_].res.last_w = dts.res.last_w
                    lanes = [tm_part()]
                    if t0 > 0:
                        lanes.append(window(si, T, t0 - 512, base))
                    run_l(lanes)
                    dma("st_tsc", tsc_d[lo:hi, :].rearrange("(s p) f -> p s f", p=128), tsst.ap, tsst_s, [tsst] + DR("tsc", lo, hi))
                    dma("st_dT", dT_d[:, :, lo:hi].rearrange("d h t -> h d t"), dts.ap, dts_s, [dts] + DR("dT", lo, hi))
                    xnb_next = partA(t0 + 512) if t0 + 512 < T else None
                    partC(t0, xnT)
                    if xnb_next is not None:
                        xnT_next = partA2(xnb_next)
                def run2(g1, g2):
                    for _ in g1:
                        pass
                run2(window(si, T, T - 512, base), None)
                for m in range(9):
                    mset("pool", Rb[:, m, 4:516], 0.0, [R_r[m]])
                run2(window(si, T, T, base), None)

    def phase2(l):
        with ExitStack() as es:
            if dbg:
                print("P2 start sbuf remaining", nc.sbuf_bytes_remaining)
            lg = one(es, "lg", [128, 12], F32)
            dma("sm", lg.ap, retz_d[l:l + 1, :].partition_broadcast(128).rearrange("p a b -> p (a b)"), [], [lg])
            act(lg.ap, lg.ap, AF.Exp, [lg], [lg], scale=-1.0)
            act(lg.ap, lg.ap, AF.Ln, [lg], [lg], bias=1.0)
            ts("dve", lg.ap, lg.ap, -1.0, ALU.mult, [lg], [lg])
            RSKS = one(es, "rsks", [128, 36], F32)
            act(RSKS[:, 0:6], lg[:, 0:6], AF.Exp, [lg, cst], [RSKS], scale=posc[:, 0:1])
            act(RSKS[:, 6:12], lg[:, 6:12], AF.Exp, [lg, cst], [RSKS], scale=posc[:, 1:2])
            act(RSKS[:, 12:18], lg[:, 0:6], AF.Exp, [lg, cst], [RSKS], scale=posc[:, 2:3])
            act(RSKS[:, 18:24], lg[:, 6:12], AF.Exp, [lg, cst], [RSKS], scale=posc[:, 3:4])
            act(RSKS[:, 24:36], lg.ap, AF.Exp, [lg], [RSKS], scale=128.0)
            RSx = [one(es, f"RSx{d}", [128, 384], F32) for d in range(2)]
            KSx = [one(es, f"KSx{d}", [128, 384], F32) for d in range(2)]
            CDx = [one(es, f"CDx{d}", [128, 384], F32) for d in range(2)]
            for d in range(2):
                for h in range(6):
                    hs = slice(h * 64, (h + 1) * 64)
                    ts("dve", RSx[d][:, hs], ones64, RSKS[:, d * 6 + h:d * 6 + h + 1], ALU.mult, [cst, RSKS], [RSx[d]])
                    ts("dve", KSx[d][:, hs], ones64, RSKS[:, 12 + d * 6 + h:12 + d * 6 + h + 1], ALU.mult,
                       [cst, RSKS], [KSx[d]])
                    ts("dve", CDx[d][:, hs], ones64, RSKS[:, 24 + d * 6 + h:24 + d * 6 + h + 1], ALU.mult,
                       [cst, RSKS], [CDx[d]])
            DsumT = one(es, "DsumT", [128, 768], F32)
            tmpm = one(es, "tmpm", [128, 256], F32)
            for h in range(6):
                act(tmpm[:, 0:128], RELP, AF.Exp, [cst, lg], [tmpm], scale=lg[:, h:h + 1])
                act(tmpm[:, 128:256], RELN, AF.Exp, [cst, lg], [tmpm], scale=lg[:, 6 + h:7 + h])
                tt("dve", tmpm[:, 0:128], tmpm[:, 0:128], MASKF, ALU.mult, [tmpm, cst], [tmpm])
                tt("dve", tmpm[:, 128:256], tmpm[:, 128:256], MASKB, ALU.mult, [tmpm, cst], [tmpm])
                tt("dve", DsumT[:, h * 128:(h + 1) * 128], tmpm[:, 0:128], tmpm[:, 128:256], ALU.add, [tmpm], [DsumT])
            esink = one(es, "esink", [128, 4], F32)
            dma("sm", esink.ap, sink_d[l:l + 1, :].partition_broadcast(128).rearrange("p a b -> p (a b)"), [], [esink])
            act(esink.ap, esink.ap, AF.Exp, [esink], [esink])

            GB = {}
            for d in range(2):
                GB[d] = dict(
                    aq=ring(es, f"aq{d}", [64, 6, 256], BF16, 2), ak=ring(es, f"ak{d}", [64, 6, 256], BF16, 2),
                    aktm=ring(es, f"aktm{d}", [128, 2, 384], BF16, 2), avtm=ring(es, f"avtm{d}", [128, 2, 384], BF16, 2),
                    tsc=ring(es, f"tsc{d}", [128, 2, 84], F32, 2), dB=ring(es, f"dB{d}", [128, 6, 256], F32, 1),
                    bq=ring(es, f"bq{d}", [64, 6, 256], BF16, 1), bk=ring(es, f"bk{d}", [64, 6, 256], BF16, 1),
                    bkv=ring(es, f"bkv{d}", [128, 2, 768], BF16, 1))
            cq_r = ring(es, "cq", [64, 4, 256], BF16, 2)
            ck_r = ring(es, "ck", [64, 2, 512], BF16, 2)
            cv_t = [one(es, f"cv{i}", [128, 4, 2, 65], BF16) for i in range(2)]
            for i in range(2):
                mset("pool", cv_t[i].ap, 1.0, [cv_t[i]])
            cvst = [0]

            CH = {}
            for d in range(2):
                for hg in range(2):
                    n = f"{d}{hg}"
                    CH[(d, hg)] = dict(
                        X=one(es, "X" + n, [128, 384], F32), Gs=one(es, "Gs" + n, [128, 384], F32),
                        M=[one(es, f"M{i}" + n, [128, 384], BF16) for i in range(2)],
                        MT=[one(es, f"MT{i}" + n, [128, 384], BF16) for i in range(2)],
                        PT=[one(es, f"PT{i}" + n, [128, 384], BF16) for i in range(2)],
                        Off=one(es, "Off" + n, [128, 384], BF16),
                        QKm=one(es, "QKm" + n, [128, 384], BF16),
                        qkT=[one(es, f"qkT{i}" + n, [128, 384], BF16) for i in range(2)],
                        Tu=[one(es, f"Tu{i}" + n, [128, 384], BF16) for i in range(2)],
                        Tw=[one(es, f"Tw{i}" + n, [128, 384], BF16) for i in range(2)], nw=one(es, "nw" + n, [64, 384], BF16),
                        vn=one(es, "vn" + n, [128, 192], BF16), vd=one(es, "vd" + n, [128, 192], BF16),
                        to=one(es, "to" + n, [128, 192], F32),
                        S32=one(es, "S32" + n, [64, 192], F32), Sbf=one(es, "Sbf" + n, [64, 192], BF16))
            for kk_ in CH:
                CH[kk_]["Gi"] = CH[kk_]["X"]
                CH[kk_]["Ti"] = CH[kk_]["Gs"]
            RT = {}
            for d in range(2):
                RT[d] = dict(S32=one(es, f"rS32{d}", [64, 384], F32), Sbf=one(es, f"rSbf{d}", [64, 384], BF16),
                             vdec=one(es, f"rvdec{d}", [128, 384], BF16), tmp=one(es, f"rtmp{d}", [128, 384], F32))
            rqk = one(es, "rqk", [128, 768], BF16)
            odir_r = [ring(es, f"odir{d}", [128, 768], F32, 2) for d in range(2)]
            oc_r = ring(es, "oct", [128, 256], F32, 2)
            sbt_r = ring(es, "sbt", [128, 384], F32, 2)
            pTt_r = ring(es, "pTt", [128, 384], BF16, 4)
            den_r = ring(es, "den", [128, 8], F32, 2)

            for si, T in enumerate(seqs):
                base = bases[si]
                N = T // 128
                for d in range(2):
                    for hg in range(2):
                        mset("pool", CH[(d, hg)]["S32"].ap, 0.0, [CH[(d, hg)]["S32"]])
                        mset("pool", CH[(d, hg)]["Sbf"].ap, 0.0, [CH[(d, hg)]["Sbf"]])
                    mset("pool", RT[d]["S32"].ap, 0.0, [RT[d]["S32"]])
                    mset("pool", RT[d]["Sbf"].ap, 0.0, [RT[d]["Sbf"]])
                cur = {}
                curc = {}

                def load_A(d, gi):
                    t0 = gi * 256
                    lo, hi = base + t0, base + t0 + 256
                    wl, wh = lo + 2, hi + 2
                    b = {k: GB[d][k]() for k in ("aq", "ak", "aktm", "avtm", "tsc", "dB")}
                    key = f"g{d}"
                    dma(key, b["aq"].ap, aqT_d[:, wl:wh].rearrange("(h d) t -> d h t", d=64), DR("aqT", wl, wh), [b["aq"]])
                    dma(key, b["ak"].ap, akT_d[:, wl:wh].rearrange("(h d) t -> d h t", d=64), DR("akT", wl, wh), [b["ak"]])
                    dma(key, b["aktm"].ap, aktm_d[wl:wh, :].rearrange("(s p) f -> p s f", p=128), DR("aktm", wl, wh), [b["aktm"]])
                    dma(key, b["avtm"].ap, avtm_d[wl:wh, :].rearrange("(s p) f -> p s f", p=128), DR("avtm", wl, wh), [b["avtm"]])
                    dma(key, b["tsc"].ap, tsc_d[lo:hi, :].rearrange("(s p) f -> p s f", p=128), DR("tsc", lo, hi), [b["tsc"]])
                    dma(key, b["dB"].ap, dT_d[d, :, lo:hi].partition_broadcast(128), DR("dT", lo, hi), [b["dB"]])
                    dbv = b["dB"].ap.rearrange("p h (c t) -> p (h c) t", t=128)
                    tt("pool", dbv, dbv, NM[d].unsqueeze(1).broadcast_to([128, 12, 128]), ALU.subtract, [b["dB"], cst], [b["dB"]])
                    return b

                def load_B(d, gi):
                    t0 = gi * 256
                    lo, hi = base + t0, base + t0 + 256
                    b = {k: GB[d][k]() for k in ("bq", "bk", "bkv")}
                    key = f"g{d}"
                    dma(key, b["bq"].ap, bqT_d[:, lo:hi].rearrange("(h d) t -> d h t", d=64), DR("bqT", lo, hi), [b["bq"]])
                    dma(key, b["bk"].ap, bkT_d[:, lo:hi].rearrange("(h d) t -> d h t", d=64), DR("bkT", lo, hi), [b["bk"]])
                    dma(key, b["bkv"].ap, kvtm_d[lo:hi, 0:768].rearrange("(s p) f -> p s f", p=128), DR("kvtm", lo, hi), [b["bkv"]])
                    cur[d] = b
                    if d == 0:
                        cq = cq_r()
                        ck = ck_r()
                        cv = cv_t[cvst[0] % 2]
                        cvst[0] += 1
                        dma("gc", cq.ap, cqT_d[:, lo:hi].rearrange("(h d) t -> d h t", d=64), DR("cqT", lo, hi), [cq])
                        klo = max(t0 - 128, 0)
                        khi = min(t0 + 384, T)
                        off = klo - (t0 - 128)
                        nb = (khi - klo) // 128
                        dma("gc", ck[:, :, off:off + (khi - klo)],
                            ckT_d[:, base + klo:base + khi].rearrange("(h d) t -> d h t", d=64),
                            DR("ckT", base + klo, base + khi), [ck])
                        for kvh in range(2):
                            dma("gc", cv[:, off // 128:off // 128 + nb, kvh, 0:64],
                                kvtm_d[base + klo:base + khi, 768 + kvh * 64:832 + kvh * 64].rearrange(
                                    "(s p) e -> p s e", p=128),
                                DR("kvtm", base + klo, base + khi), [cv])
                        curc["cq"], curc["ck"], curc["cv"] = cq, ck, cv

                def dn_chain(part, d, hg, c, b, slot, odir):
                    W = dict(CH[(d, hg)])
                    for nm_ in ("Tu", "Tw", "qkT"):
                        W[nm_] = CH[(d, hg)][nm_][slot]
                    sc = c % 2
                    cs = slice(sc * 128, (sc + 1) * 128)
                    tsc = b["tsc"]
                    hs = [3 * hg + j for j in range(3)]

                    def col(k, h):
                        return tsc[:, sc, k * 12 + d * 6 + h:k * 12 + d * 6 + h + 1]

                    def colb(k, n, rows=128):
                        c0 = k * 12 + d * 6 + 3 * hg
                        return tsc[0:rows, sc, c0:c0 + 3].unsqueeze(2).broadcast_to([rows, 3, n])

                    def v3(ap, n):
                        return ap.rearrange("p (j c) -> p j c", c=n)

                    def J(j):
                        return slice(j * 128, (j + 1) * 128)

                    def J6(j):
                        return slice(j * 64, (j + 1) * 64)
                    if part == 1:
                        M, MT, PT = W["M"], W["MT"], W["PT"]
                        for j, h in enumerate(hs):
                            act(W["Gs"][:, J(j)], b["dB"][:, h, cs], AF.Exp, [b["dB"], tsc], [W["Gs"]], scale=-1.0, bias=col(1, h))
                        yield
                        tt("pool", v3(W["Gi"].ap, 128), v3(W["Gs"].ap, 128), identf.unsqueeze(1).broadcast_to([128, 3, 128]), ALU.add, [W["Gs"], cst], [W["Gi"]])
                        bG = bank()
                        for j, h in enumerate(hs):
                            mm(bG[:, J(j)], b["ak"][:, h, cs], b["ak"][:, h, cs], [b["ak"]], [bG])
                        for j, h in enumerate(hs):
                            stt("dve", M[0][:, J(j)], bG[:, J(j)], col(0, h), W["Gs"][:, J(j)], ALU.mult, ALU.mult,
                                [bG, tsc, W["Gs"]], [M[0], bG])
                        yield
                        bQ = bank()
                        for j, h in enumerate(hs):
                            mm(bQ[:, J(j)], b["aq"][:, h, cs], b["ak"][:, h, cs], [b["aq"], b["ak"]], [bQ])
                        tt("dve", W["QKm"].ap, bQ[:, 0:384], W["Gi"].ap, ALU.mult, [bQ, W["Gi"]], [W["QKm"], bQ])
                        tt("pool", v3(W["Off"].ap, 128), v3(M[0].ap, 128), OFF64.unsqueeze(1).broadcast_to([128, 3, 128]), ALU.mult, [M[0], cst], [W["Off"]])
                        yield
                        tt("pool", v3(M[0].ap, 128), v3(M[0].ap, 128), BD64.unsqueeze(1).broadcast_to([128, 3, 128]), ALU.mult, [M[0], cst], [M[0]])
                        bT2 = bank()
                        for j in range(3):
                            tr(bf(bT2)[:, J(j)], W["QKm"][:, J(j)], identb.ap, [W["QKm"], identb], [bT2])
                        cp("dve", W["qkT"].ap, bf(bT2)[:, 0:384], [bT2], [W["qkT"], bT2])
                        yield
                        bT1 = bank()
                        for j in range(3):
                            tr(bf(bT1)[:, J(j)], M[0][:, J(j)], identb.ap, [M[0], identb], [bT1])
                        cp("act", MT[0].ap, bf(bT1)[:, 0:384], [bT1], [MT[0], bT1])
                        yield
                        tt("pool", v3(PT[0].ap, 128), v3(MT[0].ap, 128), identf.unsqueeze(1).broadcast_to([128, 3, 128]), ALU.add, [MT[0], cst], [PT[0]])
                        yield
                        pc = 0
                        NLEV = 5
                        for k in range(1, NLEV + 2):
                            a, n = (k - 1) % 2, k % 2
                            if k <= NLEV:
                                bA = bank()
                                for j in range(3):
                                    mm(bA[:, J(j)], MT[a][:, J(j)], M[a][:, J(j)], [MT[a], M[a]], [bA])
                                if k < NLEV:
                                    bB = bank()
                                    for j in range(3):
                                        mm(bB[:, J(j)], M[a][:, J(j)], MT[a][:, J(j)], [MT[a], M[a]], [bB])
                            if k >= 2:
                                bC = bank()
                                for j in range(3):
                                    mm(bC[:, J(j)], M[a][:, J(j)], PT[pc][:, J(j)], [M[a], PT[pc]], [bC], start=True, stop=False)
                                    mm(bC[:, J(j)], identb.ap, PT[pc][:, J(j)], [identb, PT[pc]], [bC], start=False, stop=True)
                            if k <= NLEV:
                                cp("act", M[n].ap, bA[:, 0:384], [bA], [M[n], bA])
                                if k < NLEV:
                                    cp("dve", MT[n].ap, bB[:, 0:384], [bB], [MT[n], bB])
                            if k >= 2:
                                if k <= NLEV:
                                    cp("act" if k % 2 == 0 else "dve", PT[1 - pc].ap, bC[:, 0:384], [bC], [PT[1 - pc], bC])
                                    pc = 1 - pc
                                else:
                                    cp("dve", W["Ti"].ap, bC[:, 0:384], [bC], [W["Ti"], bC])
                                    cp("act", PT[1 - pc].ap, bC[:, 0:384], [bC], [PT[1 - pc], bC])
                                    pc = 1 - pc
                            yield
                        XTb, Xb, Yb = PT[pc], M[0], MT[0]
                        bX = bank()
                        bY = bank()
                        for j in range(3):
                            tr(bf(bX)[:, J(j)], XTb[:, J(j)], identb.ap, [XTb, identb], [bX])
                        for j in range(3):
                            mm(bY[:, J(j)], W["Off"][:, J(j)], XTb[:, J(j)], [W["Off"], XTb], [bY])
                        cp("act", Xb.ap, bf(bX)[:, 0:384], [bX], [Xb, bX])
                        cp("dve", Yb.ap, bY[:, 0:384], [bY], [Yb, bY])
                        yield
                        bZ = bank()
                        for j in range(3):
                            mm(bZ[:, J(j)], Xb[:, J(j)], Yb[:, J(j)], [Xb, Yb], [bZ])
                        tt("dve", W["Ti"].ap, bZ[:, 0:384], W["Ti"].ap, ALU.add, [bZ, W["Ti"]], [W["Ti"], bZ])
                        tt("pool", v3(W["Tu"].ap, 128), v3(W["Ti"].ap, 128), colb(2, 128), ALU.mult, [W["Ti"], tsc], [W["Tu"]])
                        tt("dve", v3(W["Tw"].ap, 128), v3(W["Ti"].ap, 128), colb(3, 128), ALU.mult, [W["Ti"], tsc], [W["Tw"]])
                        yield
                        return
                    bW = bank()
                    for j, h in enumerate(hs):
                        mm(bW[0:64, J(j)], b["aktm"][:, sc, h * 64:(h + 1) * 64], W["Tw"][:, J(j)], [b["aktm"], W["Tw"]], [bW])
                    act(W["nw"].ap, bW[0:64, 0:384], AF.Copy, [bW], [W["nw"], bW], scale=-1.0)
                    yield
                    bV = bank()
                    for j, h in enumerate(hs):
                        mm(bV[:, J6(j)], W["Tu"][:, J(j)], b["avtm"][:, sc, h * 64:(h + 1) * 64], [W["Tu"], b["avtm"]], [bV],
                           start=True, stop=False)
                        mm(bV[:, J6(j)], W["nw"][:, J(j)], W["Sbf"][:, J6(j)], [W["nw"], W["Sbf"]], [bV],
                           start=False, stop=True)
                    cp("act", W["vn"].ap, bV[:, 0:192], [bV], [W["vn"], bV])
                    tt("dve", v3(W["vd"].ap, 64), v3(bV[:, 0:192], 64), colb(5, 64), ALU.mult, [bV, tsc], [W["vd"], bV])
                    yield
                    bO1 = bank()
                    bO2 = bank()
                    for j, h in enumerate(hs):
                        mm(bO1[:, J6(j)], W["qkT"][:, J(j)], W["vn"][:, J6(j)], [W["qkT"], W["vn"]], [bO1])
                    for j, h in enumerate(hs):
                        mm(bO2[:, J6(j)], b["aq"][:, h, cs], W["Sbf"][:, J6(j)], [b["aq"], W["Sbf"]], [bO2])
                    tt("dve", v3(W["to"].ap, 64), v3(bO2[:, 0:192], 64), colb(4, 64), ALU.mult, [bO2, tsc], [W["to"], bO2])
                    tt("dve", odir[:, hg * 192:(hg + 1) * 192], bO1[:, 0:192], W["to"].ap, ALU.add, [bO1, W["to"]],
                       [odir, bO1])
                    yield
                    bS = bank()
                    for j, h in enumerate(hs):
                        mm(bS[0:64, J6(j)], b["aktm"][:, sc, h * 64:(h + 1) * 64], W["vd"][:, J6(j)], [b["aktm"], W["vd"]], [bS])
                    tt("pool", v3(W["S32"].ap, 64), v3(W["S32"].ap, 64), colb(6, 64, rows=64), ALU.mult, [W["S32"], tsc], [W["S32"]])
                    tt("dve", W["S32"].ap, W["S32"].ap, bS[0:64, 0:192], ALU.add, [W["S32"], bS], [W["S32"], bS])
                    cp("pool", W["Sbf"].ap, W["S32"].ap, [W["S32"]], [W["Sbf"]])
                    yield

                def ret_chain(d, c, odir):
                    b = cur[d]
                    sc = c % 2
                    cs = slice(sc * 128, (sc + 1) * 128)
                    R_ = RT[d]
                    if d == 0:
                        b1 = bank()
                        b2 = bank()
                        for h in range(6):
                            bb = b1 if h < 4 else b2
                            hh = h % 4
                            mm(bb[:, hh * 128:(hh + 1) * 128], b["bk"][:, h, cs], b["bq"][:, h, cs], [b["bk"], b["bq"]], [bb])
                        tt("dve", rqk[:, 0:512], b1.ap, DsumT[:, 0:512], ALU.mult, [b1, DsumT], [rqk, b1])
                        tt("dve", rqk[:, 512:768], b2[:, 0:256], DsumT[:, 512:768], ALU.mult, [b2, DsumT], [rqk, b2])
                    yield
                    bO2 = bank()
                    for h in range(6):
                        mm(bO2[:, h * 64:(h + 1) * 64], b["bq"][:, h, cs], R_["Sbf"][:, h * 64:(h + 1) * 64],
                           [b["bq"], R_["Sbf"]], [bO2])
                    if d == 0:
                        bO1 = bank()
                        for h in range(6):
                            mm(bO1[:, h * 64:(h + 1) * 64], rqk[:, h * 128:(h + 1) * 128],
                               b["bkv"][:, sc, 384 + h * 64:384 + (h + 1) * 64], [rqk, b["bkv"]], [bO1])
                        tt("dve", R_["tmp"].ap, bO2[:, 0:384], RSx[d].ap, ALU.mult, [bO2, RSx[d]], [R_["tmp"], bO2])
                        tt("dve", odir[:, 384:768], bO1[:, 0:384], R_["tmp"].ap, ALU.add, [bO1, R_["tmp"]], [odir, bO1])
                    else:
                        tt("dve", odir[:, 384:768], bO2[:, 0:384], RSx[d].ap, ALU.mult, [bO2, RSx[d]], [odir, bO2])
                    tt("pool", R_["vdec"].ap, b["bkv"][:, sc, 384:768], KSx[d].ap, ALU.mult, [b["bkv"], KSx[d]], [R_["vdec"]])
                    yield
                    bS = bank()
                    for h in range(6):
                        mm(bS[0:64, h * 64:(h + 1) * 64], b["bkv"][:, sc, h * 64:(h + 1) * 64],
                           R_["vdec"][:, h * 64:(h + 1) * 64], [b["bkv"], R_["vdec"]], [bS])
                    tt("pool", R_["S32"].ap, R_["S32"].ap, CDx[d][0:64, :], ALU.mult, [R_["S32"], CDx[d]], [R_["S32"]])
                    tt("dve", R_["S32"].ap, R_["S32"].ap, bS[0:64, 0:384], ALU.add, [R_["S32"], bS], [R_["S32"], bS])
                    cp("pool", R_["Sbf"].ap, R_["S32"].ap, [R_["S32"]], [R_["Sbf"]])
                    yield

                def att_chain(c, oct_):
                    sc = c % 2
                    cs = slice(sc * 128, (sc + 1) * 128)
                    cq, ck, cv = curc["cq"], curc["ck"], curc["cv"]
                    jvalid = [jj for jj in range(3) if 0 <= c - 1 + jj < N]
                    j0, j1 = jvalid[0], jvalid[-1] + 1
                    pts = []
                    for h in range(4):
                        kvh = h // 2
                        bs_ = bank()
                        for jj in jvalid:
                            kb = sc + jj
                            mm(bs_[:, jj * 128:(jj + 1) * 128], ck[:, kvh, kb * 128:(kb + 1) * 128], cq[:, h, cs],
                               [ck, cq], [bs_])
                        sbt = sbt_r()
                        tt("dve", sbt[:, j0 * 128:j1 * 128], bs_[:, j0 * 128:j1 * 128],
                           cbias[:, h, j0:j1, :].rearrange("p j q -> p (j q)"), ALU.add, [bs_, cst], [sbt, bs_])
                        pT = pTt_r()
                        act(pT[:, j0 * 128:j1 * 128], sbt[:, j0 * 128:j1 * 128], AF.Exp, [sbt], [pT])
                        pts.append(pT)
                        if h % 2 == 1:
                            yield
                    bo = bank()
                    for h2 in range(4):
                        for jj in jvalid:
                            kb = sc + jj
                            mm(bo[:, h2 * 65:h2 * 65 + 65], pts[h2][:, jj * 128:(jj + 1) * 128],
                               cv[:, kb, h2 // 2, :], [pts[h2], cv], [bo], start=(jj == j0), stop=(jj == j1 - 1))
                    den = den_r()
                    tt("dve", den[:, 0:4], bo[:, 0:260].rearrange("p (h e) -> p h e", e=65)[:, :, 64], esink.ap, ALU.add,
                       [bo, esink], [den, bo])
                    recip(den[:, 4:8], den[:, 0:4], [den], [den])
                    tt("dve", oct_.ap.rearrange("p (h e) -> p h e", e=64),
                       bo[:, 0:260].rearrange("p (h e) -> p h e", e=65)[:, :, 0:64],
                       den[:, 4:8].unsqueeze(2).broadcast_to([128, 4, 64]), ALU.mult, [bo, den], [oct_, bo])
                    yield

                def run_lanes(chains):
                    live = list(chains)
                    while live:
                        nxt = []
                        for ch in live:
                            try:
                                next(ch)
                                nxt.append(ch)
                            except StopIteration:
                                pass
                        live = nxt

                gA = {}

                def groupA(d, c):
                    gi = c // 2
                    if gA.get(d, (None, None))[0] != gi:
                        gA[d] = (gi, load_A(d, gi))
                    return gA[d][1]

                def chunk_of(d, t):
                    return t if d == 0 else N - 1 - t

                info = {}
                for d in range(2):
                    info[(d, 0)] = groupA(d, chunk_of(d, 0))
                run_lanes([dn_chain(1, d, hg, chunk_of(d, 0), info[(d, 0)], 0, None) for hg in range(2) for d in range(2)])
                gB = {}
                for t in range(N):
                    cf, cb = t, N - 1 - t
                    for d in range(2):
                        gi = chunk_of(d, t) // 2
                        if gB.get(d) != gi:
                            load_B(d, gi)
                            gB[d] = gi
                    od = [odir_r[0](), odir_r[1]()]
                    oct_ = oc_r()
                    chains = []
                    for hg in range(2):
                        for d in range(2):
                            chains.append(dn_chain(2, d, hg, chunk_of(d, t), info[(d, t)], t % 2, od[d]))
                    if t + 1 < N:
                        for d in range(2):
                            info[(d, t + 1)] = groupA(d, chunk_of(d, t + 1))
                        for hg in range(2):
                            for d in range(2):
                                chains.append(dn_chain(1, d, hg, chunk_of(d, t + 1), info[(d, t + 1)], (t + 1) % 2, None))
                    chains += [ret_chain(0, cf, od[0]), ret_chain(1, cb, od[1]), att_chain(cf, oct_)]
                    run_lanes(chains)
                    info.pop((0, t), None)
                    info.pop((1, t), None)
                    lo = base + cf * 128
                    dma("st_of", of_d[lo:lo + 128, :], od[0].ap, [od[0]], DR("of", lo, lo + 128))
                    dma("st_oc", oc_d[lo:lo + 128, :], oct_.ap, [oct_], DR("oc", lo, lo + 128))
                    lo = base + cb * 128
                    dma("st_ob", ob_d[lo:lo + 128, :], od[1].ap, [od[1]], DR("ob", lo, lo + 128))

    def phase3(l):
        last = (l == L - 1)
        src_d = x_d if l == 0 else h_d
        with ExitStack() as es:
            Wout = es.enter_context(nc.sbuf_tensor(un("Wout"), [128, 8, 1024], BF16)).ap()
            Wpg = es.enter_context(nc.sbuf_tensor(un("Wpg"), [128, 8, 1024], BF16)).ap()
            Wple = es.enter_context(nc.sbuf_tensor(un("Wple"), [128, 2, 1024], BF16)).ap()
            Wout_r = [Res(f"Wout{i}") for i in range(4)]
            Wpg_r = [Res(f"Wpg{i}") for i in range(4)]
            Wple_r = [Res(f"Wple{i}") for i in range(4)]
            with ExitStack() as es2:
                stg = ring(es2, "wstg3", [128, 8, 256], F32, 2)
                for (dst, src, kk, rs) in ((Wout, wout_d[l], 8, Wout_r), (Wpg, wpg_d[l], 8, Wpg_r),
                                           (Wple, wple_d[l], 2, Wple_r)):
                    for bi, c0 in enumerate(range(0, 1024, 256)):
                        st = stg()
                        dma("wld", st[:, 0:kk, :], src[:, :, c0:c0 + 256], [], [st])
                        cp(alt(("act", "dve")), dst[:, :, c0:c0 + 256], st[:, 0:kk, :], [st], [rs[bi]])
                P.barrier()
            dngx = one(es, "dngx", [128, 384], F32)
            for h in range(6):
                dma("sm", dngx[:, h * 64:(h + 1) * 64],
                    dng_d[l:l + 1, :].partition_broadcast(128).rearrange("p a b -> p (a b)"), [], [dngx])
            retg = one(es, "retgx", [128, 384], F32)
            dma("sm", retg.ap, retg_d[l:l + 1, :].partition_broadcast(128).rearrange("p a b -> p (a b)"), [], [retg])
            fing = one(es, "fing", [128, 1024], F32)
            dma("sm", fing.ap, fing_d.partition_broadcast(128).rearrange("p a b -> p (a b)"), [], [fing])

            of_r = ring(es, "of3", [128, 768], F32, 3)
            ob_r = ring(es, "ob3", [128, 768], F32, 3)
            oc_r3 = ring(es, "oc3", [128, 256], F32, 3)
            gt_r = ring(es, "gt3", [128, 1024], BF16, 3)
            x_r = ring(es, "x3", [128, 1024], F32, 3)
            p_r = ring(es, "p3", [128, 256], F32, 3)
            sq_r = ring(es, "sq3", [128, 768], F32, 3)
            st_r = ring(es, "st3", [128, 32], F32, 3)
            t12_r = ring(es, "t12", [128, 768], F32, 3)
            xc_r = ring(es, "xc3", [128, 384], F32, 3)
            mix_r = ring(es, "mix3", [128, 1024], BF16, 3)
            mixT_r = ring(es, "mixT3", [128, 1024], BF16, 3)
            h1_r = ring(es, "h13", [128, 1024], F32, 3)
            h1b_r = ring(es, "h1b3", [128, 1024], BF16, 3)
            h1T_r = ring(es, "h1T3", [128, 1024], BF16, 3)
            sig_r = ring(es, "sig3", [128, 1024], F32, 3)
            pb16_r = ring(es, "pb163", [128, 256], BF16, 3)
            pT_r = ring(es, "pT3", [128, 256], BF16, 3)
            h2_r = ring(es, "h23", [128, 1024], F32, 3)
            y_r = ring(es, "y3", [128, 1024], F32, 3)
            junk = one(es, "junk3", [128, 1024], BF16)

            def p3_sub(si, base, hb, c):
                lo, hi = base + c * 128, base + c * 128 + 128
                xlo = hb + c * 128
                of_, ob_, oc_, gt, xt, pt = of_r(), ob_r(), oc_r3(), gt_r(), x_r(), p_r()
                dma("l3a", of_.ap, of_d[lo:hi, :], DR("of", lo, hi), [of_])
                dma("l3b", ob_.ap, ob_d[lo:hi, :], DR("ob", lo, hi), [ob_])
                dma("l3c", oc_.ap, oc_d[lo:hi, :], DR("oc", lo, hi), [oc_])
                dma("l3d", gt.ap, gate_d[lo:hi, :], DR("gate", lo, hi), [gt])
                dma("l3e", xt.ap, src_d[xlo:xlo + 128, :], ([] if l == 0 else DR("h", xlo, xlo + 128)), [xt])
                po = xoff[si] + c * 128
                dma("l3f", pt.ap, p_d[l, po:po + 128, :], [], [pt])
                tt("pool", of_.ap, of_.ap, ob_.ap, ALU.add, [of_, ob_], [of_])
                t12 = t12_r()
                tt("pool", t12[:, 0:384], gt[:, 0:384], dngx.ap, ALU.mult, [gt, dngx], [t12])
                yield
                sq = sq_r()
                tt("pool", sq.ap, of_.ap, of_.ap, ALU.mult, [of_], [sq])
                tt("pool", t12[:, 384:768], gt[:, 384:768], retg.ap, ALU.mult, [gt, retg], [t12])
                yield
                st = st_r()
                P.op("dve", lambda e, st=st, sq=sq: e.tensor_reduce(
                    out=st[:, 0:12], in_=sq.ap.rearrange("p (h e) -> p h e", e=64), axis=AX.X, op=ALU.add),
                    _rl([sq]), _rl([st]))
                P.op("dve", lambda e, st=st, of_=of_: e.tensor_reduce(
                    out=st[:, 12:18], in_=of_[:, 384:768].rearrange("p (h e) -> p h e", e=64), axis=AX.X, op=ALU.add),
                    _rl([of_]), _rl([st]))
                mix = mix_r()
                tt("pool", mix[:, 768:1024], oc_.ap, gt[:, 768:1024], ALU.mult, [oc_, gt], [mix])
                yield
                act(st[:, 18:24], st[:, 0:6], AF.Sqrt, [st], [st], bias=EPS, scale=1.0 / 64)
                ts("dve", st[:, 12:18], st[:, 12:18], 1.0 / 64, ALU.mult, [st], [st])
                yield
                tt("dve", st[:, 24:30], st[:, 12:18], st[:, 12:18], ALU.mult, [st], [st])
                yield
                stt("dve", st[:, 24:30], st[:, 6:12], 1.0 / 64, st[:, 24:30], ALU.mult, ALU.subtract, [st], [st])
                yield
                act(st[:, 24:30], st[:, 24:30], AF.Sqrt, [st], [st], bias=EPS)
                recip(st[:, 18:24], st[:, 18:24], [st], [st])
                yield
                recip(st[:, 24:30], st[:, 24:30], [st], [st])
                xc = xc_r()
                h6 = lambda ap_: ap_.rearrange("p (h e) -> p h e", e=64)
                b6 = lambda ap_: ap_.unsqueeze(2).broadcast_to([128, 6, 64])
                tt("dve", h6(xc.ap), h6(of_[:, 0:384]), b6(st[:, 18:24]), ALU.mult, [of_, st], [xc])
                yield
                tt("dve", mix[:, 0:384], xc.ap, t12[:, 0:384], ALU.mult, [xc, t12], [mix])
                yield
                tt("pool", h6(xc.ap), h6(of_[:, 384:768]), b6(st[:, 12:18]), ALU.subtract, [of_, st, mix], [xc])
                yield
                tt("dve", h6(xc.ap), h6(xc.ap), b6(st[:, 24:30]), ALU.mult, [xc, st], [xc])
                yield
                tt("dve", mix[:, 384:768], xc.ap, t12[:, 384:768], ALU.mult, [xc, t12], [mix])
                yield
                pb = bank()
                for k in range(8):
                    tr(bf(pb)[:, k * 128:(k + 1) * 128], mix[:, k * 128:(k + 1) * 128], identb.ap, [mix, identb], [pb])
                mixT = mixT_r()
                cp("act", mixT.ap, bf(pb), [pb], [mixT, pb])
                yield
                h1 = h1_r()
                for cg in range(2):
                    pb = bank()
                    for k in range(8):
                        mm(pb.ap, mixT[:, k * 128:(k + 1) * 128], Wout[:, k, cg * 512:(cg + 1) * 512],
                           [mixT, Wout_r[2 * cg], Wout_r[2 * cg + 1]], [pb], start=(k == 0), stop=(k == 7))
                    tt("dve", h1[:, cg * 512:(cg + 1) * 512], pb.ap, xt[:, cg * 512:(cg + 1) * 512], ALU.add, [pb, xt],
                       [h1, pb])
                h1b = h1b_r()
                yield
                cp("act", h1b.ap, h1.ap, [h1], [h1b])
                pb = bank()
                for k in range(8):
                    tr(bf(pb)[:, k * 128:(k + 1) * 128], h1b[:, k * 128:(k + 1) * 128], identb.ap, [h1b, identb], [pb])
                h1T = h1T_r()
                cp("dve", h1T.ap, bf(pb), [pb], [h1T, pb])
                yield
                sig = sig_r()
                for cg in range(2):
                    pb = bank()
                    for k in range(8):
                        mm(pb.ap, h1T[:, k * 128:(k + 1) * 128], Wpg[:, k, cg * 512:(cg + 1) * 512],
                           [h1T, Wpg_r[2 * cg], Wpg_r[2 * cg + 1]], [pb], start=(k == 0), stop=(k == 7))
                    act(sig[:, cg * 512:(cg + 1) * 512], pb.ap, AF.Sigmoid, [pb], [sig, pb])
                pb16 = pb16_r()
                yield
                cp("act", pb16.ap, pt.ap, [pt], [pb16])
                pb = bank()
                for k in range(2):
                    tr(bf(pb)[:, k * 128:(k + 1) * 128], pb16[:, k * 128:(k + 1) * 128], identb.ap, [pb16, identb], [pb])
                pT = pT_r()
                cp("act", pT.ap, bf(pb)[:, 0:256], [pb], [pT, pb])
                yield
                h2 = h2_r()
                for cg in range(2):
                    pb = bank()
                    for k in range(2):
                        mm(pb.ap, pT[:, k * 128:(k + 1) * 128], Wple[:, k, cg * 512:(cg + 1) * 512],
                           [pT, Wple_r[2 * cg], Wple_r[2 * cg + 1]], [pb], start=(k == 0), stop=(k == 1))
                    cs_ = slice(cg * 512, (cg + 1) * 512)
                    tt("dve", h2[:, cs_], pb.ap, sig[:, cs_], ALU.mult, [pb, sig], [h2, pb])
                    tt("pool", h2[:, cs_], h2[:, cs_], h1[:, cs_], ALU.add, [h2, h1], [h2])
                yield
                if not last:
                    dma("st_h", h_d[lo:hi, :], h2.ap, [h2], DR("h", lo, hi))
                else:
                    mset("pool", st[:, 30:31], 0.0, [st])
                    act(junk.ap, h2.ap, AF.Square, [h2, st], [junk, st], accum=st[:, 30:31])
                    act(st[:, 31:32], st[:, 30:31], AF.Sqrt, [st], [st], bias=EPS, scale=1.0 / D)
                    recip(st[:, 31:32], st[:, 31:32], [st], [st])
                    yt = y_r()
                    stt("dve", yt.ap, h2.ap, st[:, 31:32], fing.ap, ALU.mult, ALU.mult, [h2, st, fing], [yt])
                    dma("st_y", y_d[po:po + 128, :], yt.ap, [yt], [], final=True)

                yield

            K3 = 3
            work = []
            for si, T in enumerate(seqs):
                base = bases[si]
                hb = xoff[si] if l == 0 else base
                for c in range(T // 128):
                    work.append((si, base, hb, c))
            live = []
            wi = 0
            rounds = 0
            while wi < len(work) or live:
                if wi < len(work) and len(live) < K3 and (not live or rounds % 4 == 0):
                    live.append(p3_sub(*work[wi]))
                    wi += 1
                nxt = []
                for ch in live:
                    try:
                        next(ch)
                        nxt.append(ch)
                    except StopIteration:
                        pass
                live = nxt
                rounds += 1

    for l in range(L):
        phase1(l)
        P.barrier()
        phase2(l)
        P.barrier()
        phase3(l)
        P.barrier()
    P.emit()
    return nc


def _wl(w, kk):
    L_, _, C = w.shape
    return np.ascontiguousarray(w.reshape(L_, kk, 128, C).transpose(0, 2, 1, 3))


def prep_shared(w_in, w_out, norm_g, conv_w, dn_a_log, dn_dt_bias, dn_norm_g, ret_decay_z, ret_norm_g,
                attn_sink, w_ple, w_pg, final_g):
    L_ = w_in.shape[0]
    sp = np.cumsum([0, 1152, 384, 12, 12, 384, 384, 384, 384, 256, 128, 128, 256])
    seg = lambda i: w_in[:, :, sp[i]:sp[i + 1]]
    wfm = np.concatenate([seg(0), seg(4), seg(5), seg(8), seg(9)], axis=2)
    wtm = np.concatenate([seg(1), seg(7), seg(11), seg(5), seg(6), seg(10), seg(2), seg(3)], axis=2)
    return dict(
        wfm=_wl(wfm, 8), wtm=_wl(wtm, 8), wout=_wl(w_out, 8), wpg=_wl(w_pg, 8), wple=_wl(w_ple, 2),
        normg=np.ascontiguousarray(norm_g.reshape(L_, 8, 128).transpose(0, 2, 1)),
        convw=np.ascontiguousarray(conv_w.reshape(L_, 5, 9, 128).transpose(0, 3, 2, 1)),
        alog=np.ascontiguousarray(dn_a_log.reshape(L_, 12)), dtb=np.ascontiguousarray(dn_dt_bias.reshape(L_, 12)),
        dng=np.ascontiguousarray(dn_norm_g), retz=np.ascontiguousarray(ret_decay_z.reshape(L_, 12)),
        retg=np.ascontiguousarray(ret_norm_g), sink=np.ascontiguousarray(attn_sink),
        fing=np.ascontiguousarray(final_g.reshape(1, D)), consts=make_consts())


def kernel(x_prompt, x_sample, p_prompt, p_sample, w_in, w_out, norm_g, conv_w, dn_a_log, dn_dt_bias,
           dn_norm_g, ret_decay_z, ret_norm_g, attn_sink, w_ple, w_pg, final_g):
    f = lambda a: np.asarray(a, np.float32)
    x_prompt, x_sample, p_prompt, p_sample = f(x_prompt), f(x_sample), f(p_prompt), f(p_sample)
    shared = prep_shared(*[f(a) for a in (w_in, w_out, norm_g, conv_w, dn_a_log, dn_dt_bias, dn_norm_g,
                                          ret_decay_z, ret_norm_g, attn_sink, w_ple, w_pg, final_g)])
    n = 8
    B, S, _ = x_prompt.shape
    DB, DS, _ = x_sample.shape
    per = DB // n
    seqs = [S] + [DS] * per
    nc = build_program(seqs, L=w_in.shape[0])
    in_maps = []
    for c in range(n):
        xs = [x_prompt[c]] + [x_sample[c * per + i] for i in range(per)]
        ps = [p_prompt[:, c]] + [p_sample[:, c * per + i] for i in range(per)]
        m = dict(shared)
        m["x"] = np.ascontiguousarray(np.concatenate(xs, axis=0))
        m["p"] = np.ascontiguousarray(np.concatenate(ps, axis=1))
        in_maps.append(m)
    res = run_bass_kernel_spmd(nc, in_maps, core_ids=list(range(n)))
    yp = np.empty((B, S, D), np.float32)
    ys = np.empty((DB, DS, D), np.float32)
    for c in range(n):
        y = res.results[c]["y"]
        yp[c] = y[0:S]
        for i in range(per):
            ys[c * per + i] = y[S + i * DS:S + (i + 1) * DS]
    return (yp, ys)
```

```python
from contextlib import ExitStack
import numpy as np
import concourse.bass as bass
import concourse.mybir as mybir
from concourse.bass_utils import run_bass_kernel_spmd

F32 = mybir.dt.float32
BF16 = mybir.dt.bfloat16
ALU = mybir.AluOpType
AF = mybir.ActivationFunctionType
AX = mybir.AxisListType

D = 1024
EPS = 1e-6
NEG = -30000.0
PLE = 256
PADS = 640
EPOCH = 24000
NCONST = 11 * 128 + 4 + 4 * 3 * 128 + 64 + 128


class Res:
    __slots__ = ("name", "last_w", "readers")

    def __init__(self, name=""):
        self.name = name
        self.last_w = None
        self.readers = []


class SemCtr:
    __slots__ = ("nc", "name", "sems", "count")

    def __init__(self, nc, name):
        self.nc = nc
        self.name = name
        self.sems = []
        self.count = 0

    def next_event(self, inc):
        ep = self.count // EPOCH
        while len(self.sems) <= ep:
            self.sems.append(self.nc.alloc_semaphore(name=f"{self.name}_{len(self.sems)}"))
        self.count += inc
        return self.sems[ep], self.count - ep * EPOCH


class Op:
    __slots__ = ("eng", "fn", "deps", "needs_inc", "ev_sem", "ev_val", "dsem")

    def __init__(self, eng, fn, dsem):
        self.eng = eng
        self.fn = fn
        self.deps = None
        self.needs_inc = False
        self.ev_sem = None
        self.ev_val = None
        self.dsem = dsem


ENGS = ("pe", "act", "dve", "pool", "sp")


class Prog:
    def __init__(self, nc):
        self.nc = nc
        self.ops = {e: [] for e in ENGS}
        self.final_ops = []
        self.extra = []
        self.last_dma = {}
        self.nd = 0

    def new_dsem(self, name=None):
        self.nd += 1
        return SemCtr(self.nc, name or f"dq{self.nd}")

    def barrier(self):
        ex = []
        for e in ENGS:
            for o in reversed(self.ops[e]):
                if o.dsem is None:
                    ex.append(o)
                    break
        ex.extend(self.last_dma.values())
        for o in ex:
            o.needs_inc = True
        self.extra = ex

    def op(self, eng, fn, reads=(), writes=(), dsem=None, after=()):
        o = Op(eng, fn, dsem)
        deps = list(after)
        raw = set()
        for r in reads:
            if r.last_w is not None:
                deps.append(r.last_w)
                raw.add(id(r.last_w))
        for w in writes:
            if w.last_w is not None:
                deps.append(w.last_w)
            deps.extend(w.readers)
        deps.extend(self.extra)
        fl = []
        seen = set()
        for d in deps:
            if id(d) in seen or d is o:
                continue
            seen.add(id(d))
            if d.dsem is None and d.eng == eng and eng == "pe":
                continue
            fl.append(d)
            d.needs_inc = True
        o.deps = fl
        for r in reads:
            r.readers.append(o)
        for w in writes:
            w.last_w = o
            w.readers = []
        self.ops[eng].append(o)
        if dsem is not None:
            self.last_dma[id(dsem)] = o
        return o

    def emit(self):
        nc = self.nc
        for e in ENGS:
            ctr = SemCtr(nc, f"eng_{e}")
            for o in self.ops[e]:
                if o.dsem is not None:
                    o.ev_sem, o.ev_val = o.dsem.next_event(16)
                elif o.needs_inc:
                    o.ev_sem, o.ev_val = ctr.next_event(1)
        final_waits = [(o.ev_sem, o.ev_val) for o in self.final_ops]
        prog = self

        def run_engine(ename, eng):
            waited = {}
            for o in prog.ops[ename]:
                need = {}
                for d in o.deps:
                    k = id(d.ev_sem)
                    if k not in need or need[k][1] < d.ev_val:
                        need[k] = (d.ev_sem, d.ev_val)
                for k, (s, v) in need.items():
                    if waited.get(k, 0) >= v:
                        continue
                    eng.wait_ge(s, v)
                    waited[k] = v
                inst = o.fn(eng)
                if o.dsem is not None:
                    inst.then_inc(o.ev_sem, 16)
                elif o.needs_inc:
                    inst.then_inc(o.ev_sem, 1)
            if ename == "sp":
                for (s, v) in final_waits:
                    eng.wait_ge(s, v)

        with nc.Block() as block:
            @block.tensor
            def _(eng):
                run_engine("pe", eng)

            @block.scalar
            def _(eng):
                run_engine("act", eng)

            @block.vector
            def _(eng):
                run_engine("dve", eng)

            @block.gpsimd
            def _(eng):
                run_engine("pool", eng)

            @block.sync
            def _(eng):
                run_engine("sp", eng)


class Tl:
    __slots__ = ("ap", "res")

    def __init__(self, ap, name=""):
        self.ap = ap
        self.res = Res(name)

    def __getitem__(self, k):
        return self.ap[k]


def _rl(lst):
    out = []
    for t in lst:
        if isinstance(t, Res):
            out.append(t)
        elif isinstance(t, Tl):
            out.append(t.res)
        else:
            out.extend(_rl(t))
    return out


def make_consts():
    p = np.arange(128)[:, None].astype(np.float64)
    f = np.arange(128)[None, :].astype(np.float64)
    c = np.zeros((128, NCONST), np.float32)
    o = 0

    def put(a):
        nonlocal o
        a = np.asarray(a, np.float32)
        c[:, o:o + a.shape[1]] = a
        o += a.shape[1]

    put(p == f)
    put(np.where(f < p, 0.0, NEG))
    put(np.where(f > p, 0.0, NEG))
    put(p <= f)
    put(p >= f)
    put(np.ones((128, 128)))
    put((p // 64) == (f // 64))
    put(np.maximum(f - p, 0))
    put(np.maximum(p - f, 0))
    put(p <= f)
    put(p >= f)
    put(np.concatenate([p + 1, 128 - p, 127 - p, p], axis=1))
    b = np.zeros((128, 4, 3, 128), np.float64)
    for h in range(4):
        slope = 2.0 ** (-8.0 * (h + 1) / 4)
        for jj in range(3):
            rel = (jj - 1) * 128 + p - f
            b[:, h, jj, :] = np.where(np.abs(rel) <= 128, -slope * np.abs(rel), NEG)
    put(b.reshape(128, -1))
    put(np.ones((128, 64)))
    put((p // 64) != (f // 64))
    assert o == NCONST
    return c


def build_program(seqs, L=2, dbg=False):
    nc = bass.Bass("TRN2", target_bir_lowering=False)
    P = Prog(nc)
    TOK = sum(seqs)
    xoff = [sum(seqs[:i]) for i in range(len(seqs))]
    bases = []
    g = 0
    for T in seqs:
        bases.append(g)
        g += T + PADS
    G = g

    def din(name, shape, dt=F32):
        return nc.dram_tensor(name, list(shape), dt, kind="ExternalInput").ap()

    def dscr(name, shape, dt):
        kind = "ExternalOutput" if (dbg and name in ("of", "ob", "oc", "aqT", "akT", "aktm", "avtm", "tsc", "dT")) else "Internal"
        return nc.dram_tensor(name, list(shape), dt, kind=kind).ap()

    x_d = din("x", [TOK, D])
    p_d = din("p", [L, TOK, PLE])
    wfm_d = din("wfm", [L, 128, 8, 2304])
    wtm_d = din("wtm", [L, 128, 8, 1944])
    wout_d = din("wout", [L, 128, 8, 1024])
    wpg_d = din("wpg", [L, 128, 8, 1024])
    wple_d = din("wple", [L, 128, 2, 1024])
    normg_d = din("normg", [L, 128, 8])
    convw_d = din("convw", [L, 128, 9, 5])
    alog_d = din("alog", [L, 12])
    dtb_d = din("dtb", [L, 12])
    dng_d = din("dng", [L, 64])
    retz_d = din("retz", [L, 12])
    retg_d = din("retg", [L, 384])
    sink_d = din("sink", [L, 4])
    fing_d = din("fing", [1, D])
    consts_d = din("consts", [128, NCONST])
    y_d = nc.dram_tensor("y", [TOK, D], F32, kind="ExternalOutput").ap()

    aqT_d = dscr("aqT", [384, G], BF16)
    akT_d = dscr("akT", [384, G], BF16)
    aktm_d = dscr("aktm", [G, 384], BF16)
    avtm_d = dscr("avtm", [G, 384], BF16)
    bqT_d = dscr("bqT", [384, G], BF16)
    bkT_d = dscr("bkT", [384, G], BF16)
    cqT_d = dscr("cqT", [256, G], BF16)
    ckT_d = dscr("ckT", [128, G], BF16)
    kvtm_d = dscr("kvtm", [G, 896], BF16)
    gate_d = dscr("gatetm", [G, 1024], BF16)
    dT_d = dscr("dT", [2, 6, G], F32)
    tsc_d = dscr("tsc", [G, 84], F32)
    of_d = dscr("of", [G, 768], F32)
    ob_d = dscr("ob", [G, 768], F32)
    oc_d = dscr("oc", [G, 256], F32)
    h_d = dscr("hbuf", [G, D], F32)

    dres = {}

    def DR(name, lo, hi):
        out = []
        for cidx in range(lo // 128, (hi - 1) // 128 + 1):
            k = (name, cidx)
            if k not in dres:
                dres[k] = Res(f"{name}{cidx}")
            out.append(dres[k])
        return out

    def mm(out, lhsT, rhs, R, W, start=True, stop=True):
        P.op("pe", lambda e: e.matmul(out, lhsT=lhsT, rhs=rhs, start=start, stop=stop), _rl(R), _rl(W))

    def tr(out, in_, ident, R, W):
        P.op("pe", lambda e: e.transpose(out=out, in_=in_, identity=ident), _rl(R), _rl(W))

    def act(out, in_, func, R, W, bias=None, scale=None, accum=None):
        kw = {}
        if bias is not None:
            kw["bias"] = bias
        if scale is not None:
            kw["scale"] = scale
        if accum is not None:
            kw["accum_out"] = accum
        P.op("act", lambda e: e.activation(out=out, in_=in_, func=func, **kw), _rl(R), _rl(W))

    def tt(eng, out, in0, in1, op, R, W):
        P.op(eng, lambda e: e.tensor_tensor(out=out, in0=in0, in1=in1, op=op), _rl(R), _rl(W))

    def ts(eng, out, in0, s1, op0, R, W, s2=None, op1=None):
        if op1 is None:
            P.op(eng, lambda e: e.tensor_scalar(out=out, in0=in0, scalar1=s1, scalar2=None, op0=op0), _rl(R), _rl(W))
        else:
            P.op(eng, lambda e: e.tensor_scalar(out=out, in0=in0, scalar1=s1, scalar2=s2, op0=op0, op1=op1),
                 _rl(R), _rl(W))

    def stt(eng, out, in0, scalar, in1, op0, op1, R, W):
        P.op(eng, lambda e: e.scalar_tensor_tensor(out=out, in0=in0, scalar=scalar, in1=in1, op0=op0, op1=op1),
             _rl(R), _rl(W))

    def cp(eng, out, in_, R, W):
        if eng == "act":
            P.op("act", lambda e: e.activation(out=out, in_=in_, func=AF.Copy), _rl(R), _rl(W))
        else:
            P.op(eng, lambda e: e.tensor_copy(out=out, in_=in_), _rl(R), _rl(W))

    def recip(out, in_, R, W):
        P.op("dve", lambda e: e.reciprocal(out=out, in_=in_), _rl(R), _rl(W))

    def mset(eng, out, val, W):
        P.op(eng, lambda e: e.memset(out, val), (), _rl(W))

    dq = {}
    DQK = {"st_qk": 4, "st_fm": 4, "g0": 6, "g1": 6, "gc": 4}

    def dma(key, out, in_, R, W, final=False, eng="sp"):
        if key not in dq:
            k = DQK.get(key, 2)
            dq[key] = [[P.new_dsem(f"d_{key}{i}"), None] for i in range(k)] + [0]
        ent = dq[key]
        slot = ent[ent[-1] % (len(ent) - 1)]
        ent[-1] += 1
        o = P.op(eng, lambda e: e.dma_start(out=out, in_=in_), _rl(R), _rl(W), dsem=slot[0],
                 after=([slot[1]] if slot[1] is not None else ()))
        slot[1] = o
        if final:
            P.final_ops.append(o)
        return o

    banks = []
    for i in range(8):
        t = nc.alloc_psum_tensor(f"pb{i}", [128, 512], F32)
        banks.append(Tl(t.ap(), f"pb{i}"))
    bstate = [0]

    def bank():
        b = banks[bstate[0] % 8]
        bstate[0] += 1
        return b

    def bf(b):
        return b.ap.bitcast(BF16)

    uid = [0]

    def un(name):
        uid[0] += 1
        return f"s{uid[0]}_{name}"

    def sb(name, shape, dt):
        return Tl(nc.alloc_sbuf_tensor(un(name), list(shape), dt).ap(), name)

    cst = sb("consts", [128, NCONST], F32)
    dma("c0", cst.ap, consts_d, [], [cst])

    def cblk(i):
        return cst[:, i * 128:(i + 1) * 128]

    identf, NMf, NMb, CUMf, CUMb, onesf = [cblk(i) for i in range(6)]
    RELP, RELN, MASKF, MASKB = [cblk(i) for i in range(7, 11)]
    posc = cst[:, 1408:1412]
    cbias = cst[:, 1412:1412 + 1536].rearrange("p (h j q) -> p h j q", h=4, j=3)
    ones64 = cst[:, 1412 + 1536:1412 + 1536 + 64]
    OFF64 = cst[:, 1412 + 1536 + 64:1412 + 1536 + 64 + 128]
    BD64 = cblk(6)
    identb = sb("identb", [128, 128], BF16)
    cp("dve", identb.ap, identf, [cst], [identb])
    blockones = sb("blockones", [128, 128], BF16)
    cp("dve", blockones.ap, cblk(6), [cst], [blockones])
    NM = [NMf, NMb]
    CUM = [CUMf, CUMb]
    epsc = sb("epsc", [128, 1], F32)
    mset("pool", epsc.ap, EPS, [epsc])

    rings = {}

    def ring(es, name, shape, dt, n=2):
        tl = [Tl(es.enter_context(nc.sbuf_tensor(un(f"{name}{i}"), list(shape), dt)).ap(), f"{name}{i}")
              for i in range(n)]
        st = [0]

        def nxt():
            t = tl[st[0] % n]
            st[0] += 1
            return t
        return nxt

    def one(es, name, shape, dt):
        return Tl(es.enter_context(nc.sbuf_tensor(un(name), list(shape), dt)).ap(), name)

    ENG3 = ("dve", "pool")
    rr = [0]

    def alt(choices=("act", "dve")):
        rr[0] += 1
        return choices[rr[0] % len(choices)]

    def phase1(l):
        src_d = x_d if l == 0 else h_d
        with ExitStack() as es:
            Wfm = es.enter_context(nc.sbuf_tensor(un("Wfm"), [128, 8, 2304], BF16)).ap()
            Wtm = es.enter_context(nc.sbuf_tensor(un("Wtm"), [128, 8, 1944], BF16)).ap()
            Wfm_r = [Res(f"Wfm{i}") for i in range(9)]
            Wtm_r = [Res(f"Wtm{i}") for i in range(8)]
            with ExitStack() as es2:
                stg = ring(es2, "wstg", [128, 8, 256], F32, 2)
                for (dst, src, ncols, rs) in ((Wfm, wfm_d[l], 2304, Wfm_r), (Wtm, wtm_d[l], 1944, Wtm_r)):
                    for bi, c0 in enumerate(range(0, ncols, 256)):
                        cw = min(256, ncols - c0)
                        st = stg()
                        dma("wld", st[:, :, 0:cw], src[:, :, c0:c0 + cw], [], [st])
                        cp(alt(("act", "dve")), dst[:, :, c0:c0 + cw], st[:, :, 0:cw], [st], [rs[bi]])
                P.barrier()
            normg = one(es, "normg", [128, 8], F32)
            dma("sm", normg.ap, normg_d[l], [], [normg])
            cw_t = one(es, "convw", [128, 9, 5], F32)
            dma("sm", cw_t.ap, convw_d[l], [], [cw_t])
            negA = one(es, "negA", [128, 12], F32)
            dma("sm", negA.ap, alog_d[l:l + 1, :].partition_broadcast(128).rearrange("p a b -> p (a b)"), [], [negA])
            act(negA.ap, negA.ap, AF.Exp, [negA], [negA])
            ts("dve", negA.ap, negA.ap, -1.0, ALU.mult, [negA], [negA])
            dtb = one(es, "dtb", [128, 12], F32)
            dma("sm", dtb.ap, dtb_d[l:l + 1, :].partition_broadcast(128).rearrange("p a b -> p (a b)"), [], [dtb])

            xt_r = ring(es, "xt", [128, 4, 1024], F32, 1)
            junk = one(es, "junk", [128, 1024], BF16)
            ss_r = ring(es, "ss", [128, 4], F32, 2)
            xnb_r = ring(es, "xnb", [128, 4, 1024], BF16, 1)
            xnT_r = ring(es, "xnT", [128, 8, 512], BF16, 2)
            Rb = es.enter_context(nc.sbuf_tensor(un("Rraw"), [128, 9, 516], F32)).ap()
            R_r = [Res(f"R{m}") for m in range(9)]
            acc_r = ring(es, "acc", [128, 512], F32, 4)
            tmpc_r = ring(es, "tmpc", [128, 512], F32, 2)
            sact_r = ring(es, "sact", [128, 512], F32, 4)
            sq_r = ring(es, "sq", [128, 512], BF16, 4)
            rn_r = ring(es, "rn", [128, 512], F32, 4)
            qn_r = ring(es, "qn", [128, 512], BF16, 5)
            stgb_r = ring(es, "stgb", [128, 512], BF16, 5)
            kst_r = ring(es, "kst", [128, 4, 384], BF16, 1)
            vst_r = ring(es, "vst", [128, 4, 384], BF16, 1)
            gst_r = ring(es, "gst", [128, 1024], BF16, 2)
            kvst_r = ring(es, "kvst", [128, 896], BF16, 2)
            tsst_r = ring(es, "tsst", [128, 4, 84], F32, 1)
            dts_r = ring(es, "dts", [6, 2, 512], F32, 1)
            ba_r = ring(es, "ba", [128, 24], F32, 5)
            sm_r = ring(es, "smt", [128, 12 * 8], F32, 5)

            def win_m(m, lo, hi, kst, vst):
                acc = acc_r()
                if m != 4:
                    ts("dve", acc.ap, Rb[:, m, 0:512], cw_t[:, m, 0:1], ALU.mult, [R_r[m], cw_t], [acc])
                    yield
                    for j in range(1, 5):
                        stt("dve", acc.ap, Rb[:, m, j:j + 512], cw_t[:, m, j:j + 1], acc.ap, ALU.mult, ALU.add,
                            [R_r[m], cw_t, acc], [acc])
                        yield
                else:
                    act(acc.ap, Rb[:, m, 0:512], AF.Copy, [R_r[m], cw_t], [acc], scale=cw_t[:, m, 0:1])
                    yield
                    for j in range(1, 5):
                        tmpc = tmpc_r()
                        act(tmpc.ap, Rb[:, m, j:j + 512], AF.Copy, [R_r[m], cw_t], [tmpc], scale=cw_t[:, m, j:j + 1])
                        tt("pool", acc.ap, acc.ap, tmpc.ap, ALU.add, [acc, tmpc], [acc])
                        yield
                cp("pool", Rb[:, m, 0:4], Rb[:, m, 512:516], [R_r[m]], [R_r[m]])
                sact = sact_r()
                act(sact.ap, acc.ap, AF.Silu, [acc], [sact])
                yield
                if m < 6:
                    sq = sq_r()
                    act(sq.ap, sact.ap, AF.Square, [sact], [sq])
                    yield
                    pb = bank()
                    mm(pb.ap, blockones.ap, sq.ap, [blockones, sq], [pb])
                    rn = rn_r()
                    act(rn.ap, pb.ap, AF.Ln, [pb, epsc], [rn, pb], bias=epsc[:, 0:1])
                    yield
                    act(rn.ap, rn.ap, AF.Exp, [rn], [rn], scale=-0.5)
                    yield
                    qn = qn_r()
                    stt("dve", qn.ap, sact.ap, 0.125 if m < 3 else 1.0, rn.ap, ALU.mult, ALU.mult,
                        [sact, rn], [qn])
                    dst = aqT_d if m < 3 else akT_d
                    nm = "aqT" if m < 3 else "akT"
                    r0 = (m % 3) * 128
                    dma("st_qk", dst[r0:r0 + 128, lo:hi], qn.ap, [qn], DR(nm, lo, hi))
                    yield
                    if m >= 3:
                        pb2 = bank()
                        for s in range(4):
                            tr(bf(pb2)[:, s * 128:(s + 1) * 128], qn[:, s * 128:(s + 1) * 128], identb.ap,
                               [qn, identb], [pb2])
                        cp(alt(), kst[:, :, (m - 3) * 128:(m - 2) * 128],
                           bf(pb2)[:, 0:512].rearrange("p (s c) -> p s c", s=4), [pb2], [kst, pb2])
                        yield
                else:
                    vb = qn_r()
                    cp("dve", vb.ap, sact.ap, [sact], [vb])
                    yield
                    pb2 = bank()
                    for s in range(4):
                        tr(bf(pb2)[:, s * 128:(s + 1) * 128], vb[:, s * 128:(s + 1) * 128], identb.ap,
                           [vb, identb], [pb2])
                    cp(alt(), vst[:, :, (m - 6) * 128:(m - 5) * 128],
                       bf(pb2)[:, 0:512].rearrange("p (s c) -> p s c", s=4), [pb2], [vst, pb2])
                    yield

            def window(si, T, t0, base):
                kst = kst_r()
                vst = vst_r()
                lo, hi = base + t0, base + t0 + 512
                pending = [win_m(m, lo, hi, kst, vst) for m in range(9)]
                live = []
                while pending or live:
                    while len(live) < 3 and pending:
                        live.append(pending.pop(0))
                    nxt = []
                    for ch in live:
                        try:
                            next(ch)
                            nxt.append(ch)
                        except StopIteration:
                            pass
                    live = nxt
                    yield
                dma("st_ktm", aktm_d[lo:hi, :].rearrange("(s p) f -> p s f", p=128), kst.ap, [kst],
                    DR("aktm", lo, hi))
                dma("st_vtm", avtm_d[lo:hi, :].rearrange("(s p) f -> p s f", p=128), vst.ap, [vst],
                    DR("avtm", lo, hi))

            for si, T in enumerate(seqs):
                base = bases[si]
                hb = xoff[si] if l == 0 else base
                for m in range(9):
                    mset("pool", Rb[:, m, 0:4], 0.0, [R_r[m]])
                def partA(t0):
                    lo, hi = base + t0, base + t0 + 512
                    xt = xt_r()
                    rd = [] if l == 0 else DR("h", hb + t0, hb + t0 + 512)
                    dma("ld_x", xt.ap, src_d[hb + t0:hb + t0 + 512, :].rearrange("(s p) f -> p s f", p=128), rd, [xt])
                    ss = ss_r()
                    mset("pool", ss.ap, 0.0, [ss])
                    for s in range(4):
                        act(junk.ap, xt[:, s, :], AF.Square, [xt, ss], [junk, ss], accum=ss[:, s:s + 1])
                    act(ss.ap, ss.ap, AF.Ln, [ss, epsc], [ss], bias=epsc[:, 0:1], scale=1.0 / D)
                    act(ss.ap, ss.ap, AF.Exp, [ss], [ss], scale=-0.5)
                    xnb = xnb_r()
                    for s in range(4):
                        if s % 2 == 0:
                            ts("dve", xnb[:, s, :], xt[:, s, :], ss[:, s:s + 1], ALU.mult, [xt, ss], [xnb])
                        else:
                            act(xnb[:, s, :], xt[:, s, :], AF.Copy, [xt, ss], [xnb], scale=ss[:, s:s + 1])
                    return xnb

                def partA2(xnb):
                    xnT = xnT_r()
                    for k in range(8):
                        pb = bank()
                        for s in range(4):
                            tr(bf(pb)[:, s * 128:(s + 1) * 128], xnb[:, s, k * 128:(k + 1) * 128], identb.ap,
                               [xnb, identb], [pb])
                        if k % 2 == 0:
                            act(xnT[:, k, :], bf(pb)[:, 0:512], AF.Copy, [pb, normg], [xnT, pb], scale=normg[:, k:k + 1])
                        else:
                            ts("dve", xnT[:, k, :], bf(pb)[:, 0:512], normg[:, k:k + 1], ALU.mult, [pb, normg], [xnT, pb])
                    return xnT

                def partC(t0, xnT):
                    lo, hi = base + t0, base + t0 + 512
                    for m in range(18):
                        pb = bank()
                        for k in range(8):
                            mm(pb.ap, Wfm[:, k, m * 128:(m + 1) * 128], xnT[:, k, :], [Wfm_r[(m * 128) // 256], xnT],
                               [pb], start=(k == 0), stop=(k == 7))
                        if m < 9:
                            cp(alt(), Rb[:, m, 4:516], pb.ap, [pb], [R_r[m], pb])
                        else:
                            st = stgb_r()
                            if m < 12:
                                ts("dve", st.ap, pb.ap, 0.125, ALU.mult, [pb], [st, pb])
                                dst, nm, r0 = bqT_d, "bqT", (m - 9) * 128
                            elif m < 15:
                                cp(alt(), st.ap, pb.ap, [pb], [st, pb])
                                dst, nm, r0 = bkT_d, "bkT", (m - 12) * 128
                            elif m < 17:
                                act(st.ap, pb.ap, AF.Copy, [pb], [st, pb], scale=0.125)
                                dst, nm, r0 = cqT_d, "cqT", (m - 15) * 128
                            else:
                                cp(alt(), st.ap, pb.ap, [pb], [st, pb])
                                dst, nm, r0 = ckT_d, "ckT", 0
                            dma("st_fm", dst[r0:r0 + 128, lo:hi], st.ap, [st], DR(nm, lo, hi))

                xnT_next = partA2(partA(0))
                for t0 in range(0, T, 512):
                    lo, hi = base + t0, base + t0 + 512
                    xnT = xnT_next
                    def tm_part(xnT=xnT, lo=lo, hi=hi, t0=t0):
                        for s in range(4):
                            gst = gst_r()
                            kvst = kvst_r()
                            ba = ba_r()
                            for g4 in range(4):
                                c0 = g4 * 512
                                cw = min(512, 1944 - c0)
                                pb = bank()
                                for k in range(8):
                                    mm(pb[:, 0:cw], xnT[:, k, s * 128:(s + 1) * 128], Wtm[:, k, c0:c0 + cw],
                                       [xnT, Wtm_r[c0 // 256], Wtm_r[(c0 + cw - 1) // 256]], [pb],
                                       start=(k == 0), stop=(k == 7))
                                if g4 < 2:
                                    act(gst[:, c0:c0 + 512], pb.ap, AF.Silu, [pb], [gst, pb])
                                elif g4 == 2:
                                    cp(alt(), kvst[:, 0:512], pb.ap, [pb], [kvst, pb])
                                else:
                                    cp(alt(), kvst[:, 512:896], pb[:, 0:384], [pb], [kvst, pb])
                                    cp("dve", ba.ap, pb[:, 384:408], [pb], [ba, pb])
                            sl, sh = lo + s * 128, lo + (s + 1) * 128
                            dma("st_gate", gate_d[sl:sh, :], gst.ap, [gst], DR("gate", sl, sh))
                            dma("st_kv", kvtm_d[sl:sh, :], kvst.ap, [kvst], DR("kvtm", sl, sh))
                            lanes.append(sm_chain(s, ba, tsst_s[s], dts_s[s]))
                            yield
                        yield

                    def sm_chain(s, ba, tsl_t, dts_t):
                        sm = sm_r()
                        z, nz, mn, ee, sp_, gg, bet, tmp = [sm[:, i * 12:(i + 1) * 12] for i in range(8)]
                        tsl = tsl_t.ap
                        act(bet, ba[:, 0:12], AF.Sigmoid, [ba], [sm])
                        tt("dve", z, ba[:, 12:24], dtb.ap, ALU.add, [ba, dtb], [sm])
                        yield
                        cp("dve", tsl[:, 24:36], bet, [sm], [tsl_t])
                        ts("dve", tsl[:, 0:12], bet, -1.0, ALU.mult, [sm], [tsl_t])
                        ts("dve", nz, z, -1.0, ALU.mult, [sm], [sm])
                        yield
                        tt("dve", mn, z, nz, ALU.min, [sm], [sm])
                        yield
                        act(ee, mn, AF.Exp, [sm], [sm])
                        ts("dve", sp_, z, 0.0, ALU.max, [sm], [sm])
                        yield
                        act(ee, ee, AF.Ln, [sm], [sm], bias=1.0)
                        yield
                        tt("dve", sp_, sp_, ee, ALU.add, [sm], [sm])
                        yield
                        tt("dve", gg, sp_, negA.ap, ALU.mult, [sm, negA], [sm])
                        yield
                        pd = bank()
                        mm(pd[:, 0:6], CUMf, gg[:, 0:6], [cst, sm], [pd])
                        mm(pd[:, 6:12], CUMb, gg[:, 6:12], [cst, sm], [pd])
                        mm(pd[:, 12:24], onesf, gg, [cst, sm], [pd])
                        mm(pd[0:6, 128:256], gg[:, 0:6], CUMf, [cst, sm], [pd])
                        mm(pd[0:6, 256:384], gg[:, 6:12], CUMb, [cst, sm], [pd])
                        cp("dve", tsl[:, 12:24], pd[:, 0:12], [pd], [tsl_t, pd])
                        act(tsl[:, 48:60], pd[:, 0:12], AF.Exp, [pd], [tsl_t, pd])
                        act(tsl[:, 72:84], pd[:, 12:24], AF.Exp, [pd], [tsl_t, pd])
                        tt("dve", tmp, pd[:, 12:24], tsl[:, 12:24], ALU.subtract, [pd, tsl_t], [sm, pd])
                        cp("dve", dts_t.ap, pd[0:6, 128:384].rearrange("p (d c) -> p d c", d=2), [pd], [dts_t, pd])
                        yield
                        act(tsl[:, 60:72], tmp, AF.Exp, [sm], [tsl_t])
                        tt("dve", tsl[:, 36:48], tsl[:, 24:36], tsl[:, 48:60], ALU.mult, [tsl_t], [tsl_t])
                        yield

                    def run_l(lanes):
                        live = []
                        idx = 0
                        while idx < len(lanes) or live:
                            while idx < len(lanes):
                                live.append(lanes[idx])
                                idx += 1
                            nxt = []
                            for ch in live:
                                try:
                                    next(ch)
                                    nxt.append(ch)
                                except StopIteration:
                                    pass
                            live = nxt

                    tsst = tsst_r()
                    dts = dts_r()
                    tsst_s = [Tl(tsst[:, s_, :], f"tsst{s_}") for s_ in range(4)]
                    dts_s = [Tl(dts[:, :, s_ * 128:(s_ + 1) * 128], f"dts{s_}") for s_ in range(4)]
                    for s_ in range(4):
                        tsst_s[s_].res.readers = list(tsst.res.readers)
                        tsst_s[s_].res.last_w = tsst.res.last_w
                        dts_s[s_].res.readers = list(dts.res.readers)
                        dts_s[s_].res.last_w = dts.res.last_w
                    lanes = [tm_part()]
                    if t0 > 0:
                        lanes.append(window(si, T, t0 - 512, base))
                    run_l(lanes)
                    dma("st_tsc", tsc_d[lo:hi, :].rearrange("(s p) f -> p s f", p=128), tsst.ap, tsst_s, [tsst] + DR("tsc", lo, hi))
                    dma("st_dT", dT_d[:, :, lo:hi].rearrange("d h t -> h d t"), dts.ap, dts_s, [dts] + DR("dT", lo, hi))
                    xnb_next = partA(t0 + 512) if t0 + 512 < T else None
                    partC(t0, xnT)
                    if xnb_next is not None:
                        xnT_next = partA2(xnb_next)
                def run2(g1, g2):
                    for _ in g1:
                        pass
                run2(window(si, T, T - 512, base), None)
                for m in range(9):
                    mset("pool", Rb[:, m, 4:516], 0.0, [R_r[m]])
                run2(window(si, T, T, base), None)

    def phase2(l):
        with ExitStack() as es:
            if dbg:
                print("P2 start sbuf remaining", nc.sbuf_bytes_remaining)
            lg = one(es, "lg", [128, 12], F32)
            dma("sm", lg.ap, retz_d[l:l + 1, :].partition_broadcast(128).rearrange("p a b -> p (a b)"), [], [lg])
            act(lg.ap, lg.ap, AF.Exp, [lg], [lg], scale=-1.0)
            act(lg.ap, lg.ap, AF.Ln, [lg], [lg], bias=1.0)
            ts("dve", lg.ap, lg.ap, -1.0, ALU.mult, [lg], [lg])
            RSKS = one(es, "rsks", [128, 36], F32)
            act(RSKS[:, 0:6], lg[:, 0:6], AF.Exp, [lg, cst], [RSKS], scale=posc[:, 0:1])
            act(RSKS[:, 6:12], lg[:, 6:12], AF.Exp, [lg, cst], [RSKS], scale=posc[:, 1:2])
            act(RSKS[:, 12:18], lg[:, 0:6], AF.Exp, [lg, cst], [RSKS], scale=posc[:, 2:3])
            act(RSKS[:, 18:24], lg[:, 6:12], AF.Exp, [lg, cst], [RSKS], scale=posc[:, 3:4])
            act(RSKS[:, 24:36], lg.ap, AF.Exp, [lg], [RSKS], scale=128.0)
            RSx = [one(es, f"RSx{d}", [128, 384], F32) for d in range(2)]
            KSx = [one(es, f"KSx{d}", [128, 384], F32) for d in range(2)]
            CDx = [one(es, f"CDx{d}", [128, 384], F32) for d in range(2)]
            for d in range(2):
                for h in range(6):
                    hs = slice(h * 64, (h + 1) * 64)
                    ts("dve", RSx[d][:, hs], ones64, RSKS[:, d * 6 + h:d * 6 + h + 1], ALU.mult, [cst, RSKS], [RSx[d]])
                    ts("dve", KSx[d][:, hs], ones64, RSKS[:, 12 + d * 6 + h:12 + d * 6 + h + 1], ALU.mult,
                       [cst, RSKS], [KSx[d]])
                    ts("dve", CDx[d][:, hs], ones64, RSKS[:, 24 + d * 6 + h:24 + d * 6 + h + 1], ALU.mult,
                       [cst, RSKS], [CDx[d]])
            DsumT = one(es, "DsumT", [128, 768], F32)
            tmpm = one(es, "tmpm", [128, 256], F32)
            for h in range(6):
                act(tmpm[:, 0:128], RELP, AF.Exp, [cst, lg], [tmpm], scale=lg[:, h:h + 1])
                act(tmpm[:, 128:256], RELN, AF.Exp, [cst, lg], [tmpm], scale=lg[:, 6 + h:7 + h])
                tt("dve", tmpm[:, 0:128], tmpm[:, 0:128], MASKF, ALU.mult, [tmpm, cst], [tmpm])
                tt("dve", tmpm[:, 128:256], tmpm[:, 128:256], MASKB, ALU.mult, [tmpm, cst], [tmpm])
                tt("dve", DsumT[:, h * 128:(h + 1) * 128], tmpm[:, 0:128], tmpm[:, 128:256], ALU.add, [tmpm], [DsumT])
            esink = one(es, "esink", [128, 4], F32)
            dma("sm", esink.ap, sink_d[l:l + 1, :].partition_broadcast(128).rearrange("p a b -> p (a b)"), [], [esink])
            act(esink.ap, esink.ap, AF.Exp, [esink], [esink])

            GB = {}
            for d in range(2):
                GB[d] = dict(
                    aq=ring(es, f"aq{d}", [64, 6, 256], BF16, 2), ak=ring(es, f"ak{d}", [64, 6, 256], BF16, 2),
                    aktm=ring(es, f"aktm{d}", [128, 2, 384], BF16, 2), avtm=ring(es, f"avtm{d}", [128, 2, 384], BF16, 2),
                    tsc=ring(es, f"tsc{d}", [128, 2, 84], F32, 2), dB=ring(es, f"dB{d}", [128, 6, 256], F32, 1),
                    bq=ring(es, f"bq{d}", [64, 6, 256], BF16, 1), bk=ring(es, f"bk{d}", [64, 6, 256], BF16, 1),
                    bkv=ring(es, f"bkv{d}", [128, 2, 768], BF16, 1))
            cq_r = ring(es, "cq", [64, 4, 256], BF16, 2)
            ck_r = ring(es, "ck", [64, 2, 512], BF16, 2)
            cv_t = [one(es, f"cv{i}", [128, 4, 2, 65], BF16) for i in range(2)]
            for i in range(2):
                mset("pool", cv_t[i].ap, 1.0, [cv_t[i]])
            cvst = [0]

            CH = {}
            for d in range(2):
                for hg in range(2):
                    n = f"{d}{hg}"
                    CH[(d, hg)] = dict(
                        X=one(es, "X" + n, [128, 384], F32), Gs=one(es, "Gs" + n, [128, 384], F32),
                        M=[one(es, f"M{i}" + n, [128, 384], BF16) for i in range(2)],
                        MT=[one(es, f"MT{i}" + n, [128, 384], BF16) for i in range(2)],
                        PT=[one(es, f"PT{i}" + n, [128, 384], BF16) for i in range(2)],
                        Off=one(es, "Off" + n, [128, 384], BF16),
                        QKm=one(es, "QKm" + n, [128, 384], BF16),
                        qkT=[one(es, f"qkT{i}" + n, [128, 384], BF16) for i in range(2)],
                        Tu=[one(es, f"Tu{i}" + n, [128, 384], BF16) for i in range(2)],
                        Tw=[one(es, f"Tw{i}" + n, [128, 384], BF16) for i in range(2)], nw=one(es, "nw" + n, [64, 384], BF16),
                        vn=one(es, "vn" + n, [128, 192], BF16), vd=one(es, "vd" + n, [128, 192], BF16),
                        to=one(es, "to" + n, [128, 192], F32),
                        S32=one(es, "S32" + n, [64, 192], F32), Sbf=one(es, "Sbf" + n, [64, 192], BF16))
            for kk_ in CH:
                CH[kk_]["Gi"] = CH[kk_]["X"]
                CH[kk_]["Ti"] = CH[kk_]["Gs"]
            RT = {}
            for d in range(2):
                RT[d] = dict(S32=one(es, f"rS32{d}", [64, 384], F32), Sbf=one(es, f"rSbf{d}", [64, 384], BF16),
                             vdec=one(es, f"rvdec{d}", [128, 384], BF16), tmp=one(es, f"rtmp{d}", [128, 384], F32))
            rqk = one(es, "rqk", [128, 768], BF16)
            odir_r = [ring(es, f"odir{d}", [128, 768], F32, 2) for d in range(2)]
            oc_r = ring(es, "oct", [128, 256], F32, 2)
            sbt_r = ring(es, "sbt", [128, 384], F32, 2)
            pTt_r = ring(es, "pTt", [128, 384], BF16, 4)
            den_r = ring(es, "den", [128, 8], F32, 2)

            for si, T in enumerate(seqs):
                base = bases[si]
                N = T // 128
                for d in range(2):
                    for hg in range(2):
                        mset("pool", CH[(d, hg)]["S32"].ap, 0.0, [CH[(d, hg)]["S32"]])
                        mset("pool", CH[(d, hg)]["Sbf"].ap, 0.0, [CH[(d, hg)]["Sbf"]])
                    mset("pool", RT[d]["S32"].ap, 0.0, [RT[d]["S32"]])
                    mset("pool", RT[d]["Sbf"].ap, 0.0, [RT[d]["Sbf"]])
                cur = {}
                curc = {}

                def load_A(d, gi):
                    t0 = gi * 256
                    lo, hi = base + t0, base + t0 + 256
                    wl, wh = lo + 2, hi + 2
                    b = {k: GB[d][k]() for k in ("aq", "ak", "aktm", "avtm", "tsc", "dB")}
                    key = f"g{d}"
                    dma(key, b["aq"].ap, aqT_d[:, wl:wh].rearrange("(h d) t -> d h t", d=64), DR("aqT", wl, wh), [b["aq"]])
                    dma(key, b["ak"].ap, akT_d[:, wl:wh].rearrange("(h d) t -> d h t", d=64), DR("akT", wl, wh), [b["ak"]])
                    dma(key, b["aktm"].ap, aktm_d[wl:wh, :].rearrange("(s p) f -> p s f", p=128), DR("aktm", wl, wh), [b["aktm"]])
                    dma(key, b["avtm"].ap, avtm_d[wl:wh, :].rearrange("(s p) f -> p s f", p=128), DR("avtm", wl, wh), [b["avtm"]])
                    dma(key, b["tsc"].ap, tsc_d[lo:hi, :].rearrange("(s p) f -> p s f", p=128), DR("tsc", lo, hi), [b["tsc"]])
                    dma(key, b["dB"].ap, dT_d[d, :, lo:hi].partition_broadcast(128), DR("dT", lo, hi), [b["dB"]])
                    dbv = b["dB"].ap.rearrange("p h (c t) -> p (h c) t", t=128)
                    tt("pool", dbv, dbv, NM[d].unsqueeze(1).broadcast_to([128, 12, 128]), ALU.subtract, [b["dB"], cst], [b["dB"]])
                    return b

                def load_B(d, gi):
                    t0 = gi * 256
                    lo, hi = base + t0, base + t0 + 256
                    b = {k: GB[d][k]() for k in ("bq", "bk", "bkv")}
                    key = f"g{d}"
                    dma(key, b["bq"].ap, bqT_d[:, lo:hi].rearrange("(h d) t -> d h t", d=64), DR("bqT", lo, hi), [b["bq"]])
                    dma(key, b["bk"].ap, bkT_d[:, lo:hi].rearrange("(h d) t -> d h t", d=64), DR("bkT", lo, hi), [b["bk"]])
                    dma(key, b["bkv"].ap, kvtm_d[lo:hi, 0:768].rearrange("(s p) f -> p s f", p=128), DR("kvtm", lo, hi), [b["bkv"]])
                    cur[d] = b
                    if d == 0:
                        cq = cq_r()
                        ck = ck_r()
                        cv = cv_t[cvst[0] % 2]
                        cvst[0] += 1
                        dma("gc", cq.ap, cqT_d[:, lo:hi].rearrange("(h d) t -> d h t", d=64), DR("cqT", lo, hi), [cq])
                        klo = max(t0 - 128, 0)
                        khi = min(t0 + 384, T)
                        off = klo - (t0 - 128)
                        nb = (khi - klo) // 128
                        dma("gc", ck[:, :, off:off + (khi - klo)],
                            ckT_d[:, base + klo:base + khi].rearrange("(h d) t -> d h t", d=64),
                            DR("ckT", base + klo, base + khi), [ck])
                        for kvh in range(2):
                            dma("gc", cv[:, off // 128:off // 128 + nb, kvh, 0:64],
                                kvtm_d[base + klo:base + khi, 768 + kvh * 64:832 + kvh * 64].rearrange(
                                    "(s p) e -> p s e", p=128),
                                DR("kvtm", base + klo, base + khi), [cv])
                        curc["cq"], curc["ck"], curc["cv"] = cq, ck, cv

                def dn_chain(part, d, hg, c, b, slot, odir):
                    W = dict(CH[(d, hg)])
                    for nm_ in ("Tu", "Tw", "qkT"):
                        W[nm_] = CH[(d, hg)][nm_][slot]
                    sc = c % 2
                    cs = slice(sc * 128, (sc + 1) * 128)
                    tsc = b["tsc"]
                    hs = [3 * hg + j for j in range(3)]

                    def col(k, h):
                        return tsc[:, sc, k * 12 + d * 6 + h:k * 12 + d * 6 + h + 1]

                    def colb(k, n, rows=128):
                        c0 = k * 12 + d * 6 + 3 * hg
                        return tsc[0:rows, sc, c0:c0 + 3].unsqueeze(2).broadcast_to([rows, 3, n])

                    def v3(ap, n):
                        return ap.rearrange("p (j c) -> p j c", c=n)

                    def J(j):
                        return slice(j * 128, (j + 1) * 128)

                    def J6(j):
                        return slice(j * 64, (j + 1) * 64)
                    if part == 1:
                        M, MT, PT = W["M"], W["MT"], W["PT"]
                        for j, h in enumerate(hs):
                            act(W["Gs"][:, J(j)], b["dB"][:, h, cs], AF.Exp, [b["dB"], tsc], [W["Gs"]], scale=-1.0, bias=col(1, h))
                        yield
                        tt("pool", v3(W["Gi"].ap, 128), v3(W["Gs"].ap, 128), identf.unsqueeze(1).broadcast_to([128, 3, 128]), ALU.add, [W["Gs"], cst], [W["Gi"]])
                        bG = bank()
                        for j, h in enumerate(hs):
                            mm(bG[:, J(j)], b["ak"][:, h, cs], b["ak"][:, h, cs], [b["ak"]], [bG])
                        for j, h in enumerate(hs):
                            stt("dve", M[0][:, J(j)], bG[:, J(j)], col(0, h), W["Gs"][:, J(j)], ALU.mult, ALU.mult,
                                [bG, tsc, W["Gs"]], [M[0], bG])
                        yield
                        bQ = bank()
                        for j, h in enumerate(hs):
                            mm(bQ[:, J(j)], b["aq"][:, h, cs], b["ak"][:, h, cs], [b["aq"], b["ak"]], [bQ])
                        tt("dve", W["QKm"].ap, bQ[:, 0:384], W["Gi"].ap, ALU.mult, [bQ, W["Gi"]], [W["QKm"], bQ])
                        tt("pool", v3(W["Off"].ap, 128), v3(M[0].ap, 128), OFF64.unsqueeze(1).broadcast_to([128, 3, 128]), ALU.mult, [M[0], cst], [W["Off"]])
                        yield
                        tt("pool", v3(M[0].ap, 128), v3(M[0].ap, 128), BD64.unsqueeze(1).broadcast_to([128, 3, 128]), ALU.mult, [M[0], cst], [M[0]])
                        bT2 = bank()
                        for j in range(3):
                            tr(bf(bT2)[:, J(j)], W["QKm"][:, J(j)], identb.ap, [W["QKm"], identb], [bT2])
                        cp("dve", W["qkT"].ap, bf(bT2)[:, 0:384], [bT2], [W["qkT"], bT2])
                        yield
                        bT1 = bank()
                        for j in range(3):
                            tr(bf(bT1)[:, J(j)], M[0][:, J(j)], identb.ap, [M[0], identb], [bT1])
                        cp("act", MT[0].ap, bf(bT1)[:, 0:384], [bT1], [MT[0], bT1])
                        yield
                        tt("pool", v3(PT[0].ap, 128), v3(MT[0].ap, 128), identf.unsqueeze(1).broadcast_to([128, 3, 128]), ALU.add, [MT[0], cst], [PT[0]])
                        yield
                        pc = 0
                        NLEV = 5
                        for k in range(1, NLEV + 2):
                            a, n = (k - 1) % 2, k % 2
                            if k <= NLEV:
                                bA = bank()
                                for j in range(3):
                                    mm(bA[:, J(j)], MT[a][:, J(j)], M[a][:, J(j)], [MT[a], M[a]], [bA])
                                if k < NLEV:
                                    bB = bank()
                                    for j in range(3):
                                        mm(bB[:, J(j)], M[a][:, J(j)], MT[a][:, J(j)], [MT[a], M[a]], [bB])
                            if k >= 2:
                                bC = bank()
                                for j in range(3):
                                    mm(bC[:, J(j)], M[a][:, J(j)], PT[pc][:, J(j)], [M[a], PT[pc]], [bC], start=True, stop=False)
                                    mm(bC[:, J(j)], identb.ap, PT[pc][:, J(j)], [identb, PT[pc]], [bC], start=False, stop=True)
                            if k <= NLEV:
                                cp("act", M[n].ap, bA[:, 0:384], [bA], [M[n], bA])
                                if k < NLEV:
                                    cp("dve", MT[n].ap, bB[:, 0:384], [bB], [MT[n], bB])
                            if k >= 2:
                                if k <= NLEV:
                                    cp("act" if k % 2 == 0 else "dve", PT[1 - pc].ap, bC[:, 0:384], [bC], [PT[1 - pc], bC])
                                    pc = 1 - pc
                                else:
                                    cp("dve", W["Ti"].ap, bC[:, 0:384], [bC], [W["Ti"], bC])
                                    cp("act", PT[1 - pc].ap, bC[:, 0:384], [bC], [PT[1 - pc], bC])
                                    pc = 1 - pc
                            yield
                        XTb, Xb, Yb = PT[pc], M[0], MT[0]
                        bX = bank()
                        bY = bank()
                        for j in range(3):
                            tr(bf(bX)[:, J(j)], XTb[:, J(j)], identb.ap, [XTb, identb], [bX])
                        for j in range(3):
                            mm(bY[:, J(j)], W["Off"][:, J(j)], XTb[:, J(j)], [W["Off"], XTb], [bY])
                        cp("act", Xb.ap, bf(bX)[:, 0:384], [bX], [Xb, bX])
                        cp("dve", Yb.ap, bY[:, 0:384], [bY], [Yb, bY])
                        yield
                        bZ = bank()
                        for j in range(3):
                            mm(bZ[:, J(j)], Xb[:, J(j)], Yb[:, J(j)], [Xb, Yb], [bZ])
                        tt("dve", W["Ti"].ap, bZ[:, 0:384], W["Ti"].ap, ALU.add, [bZ, W["Ti"]], [W["Ti"], bZ])
                        tt("pool", v3(W["Tu"].ap, 128), v3(W["Ti"].ap, 128), colb(2, 128), ALU.mult, [W["Ti"], tsc], [W["Tu"]])
                        tt("dve", v3(W["Tw"].ap, 128), v3(W["Ti"].ap, 128), colb(3, 128), ALU.mult, [W["Ti"], tsc], [W["Tw"]])
                        yield
                        return
                    bW = bank()
                    for j, h in enumerate(hs):
                        mm(bW[0:64, J(j)], b["aktm"][:, sc, h * 64:(h + 1) * 64], W["Tw"][:, J(j)], [b["aktm"], W["Tw"]], [bW])
                    act(W["nw"].ap, bW[0:64, 0:384], AF.Copy, [bW], [W["nw"], bW], scale=-1.0)
                    yield
                    bV = bank()
                    for j, h in enumerate(hs):
                        mm(bV[:, J6(j)], W["Tu"][:, J(j)], b["avtm"][:, sc, h * 64:(h + 1) * 64], [W["Tu"], b["avtm"]], [bV],
                           start=True, stop=False)
                        mm(bV[:, J6(j)], W["nw"][:, J(j)], W["Sbf"][:, J6(j)], [W["nw"], W["Sbf"]], [bV],
                           start=False, stop=True)
                    cp("act", W["vn"].ap, bV[:, 0:192], [bV], [W["vn"], bV])
                    tt("dve", v3(W["vd"].ap, 64), v3(bV[:, 0:192], 64), colb(5, 64), ALU.mult, [bV, tsc], [W["vd"], bV])
                    yield
                    bO1 = bank()
                    bO2 = bank()
                    for j, h in enumerate(hs):
                        mm(bO1[:, J6(j)], W["qkT"][:, J(j)], W["vn"][:, J6(j)], [W["qkT"], W["vn"]], [bO1])
                    for j, h in enumerate(hs):
                        mm(bO2[:, J6(j)], b["aq"][:, h, cs], W["Sbf"][:, J6(j)], [b["aq"], W["Sbf"]], [bO2])
                    tt("dve", v3(W["to"].ap, 64), v3(bO2[:, 0:192], 64), colb(4, 64), ALU.mult, [bO2, tsc], [W["to"], bO2])
                    tt("dve", odir[:, hg * 192:(hg + 1) * 192], bO1[:, 0:192], W["to"].ap, ALU.add, [bO1, W["to"]],
                       [odir, bO1])
                    yield
                    bS = bank()
                    for j, h in enumerate(hs):
                        mm(bS[0:64, J6(j)], b["aktm"][:, sc, h * 64:(h + 1) * 64], W["vd"][:, J6(j)], [b["aktm"], W["vd"]], [bS])
                    tt("pool", v3(W["S32"].ap, 64), v3(W["S32"].ap, 64), colb(6, 64, rows=64), ALU.mult, [W["S32"], tsc], [W["S32"]])
                    tt("dve", W["S32"].ap, W["S32"].ap, bS[0:64, 0:192], ALU.add, [W["S32"], bS], [W["S32"], bS])
                    cp("act", W["Sbf"].ap, W["S32"].ap, [W["S32"]], [W["Sbf"]])
                    yield

                def ret_chain(d, c, odir):
                    b = cur[d]
                    sc = c % 2
                    cs = slice(sc * 128, (sc + 1) * 128)
                    R_ = RT[d]
                    if d == 0:
                        b1 = bank()
                        b2 = bank()
                        for h in range(6):
                            bb = b1 if h < 4 else b2
                            hh = h % 4
                            mm(bb[:, hh * 128:(hh + 1) * 128], b["bk"][:, h, cs], b["bq"][:, h, cs], [b["bk"], b["bq"]], [bb])
                        tt("dve", rqk[:, 0:512], b1.ap, DsumT[:, 0:512], ALU.mult, [b1, DsumT], [rqk, b1])
                        tt("dve", rqk[:, 512:768], b2[:, 0:256], DsumT[:, 512:768], ALU.mult, [b2, DsumT], [rqk, b2])
                    yield
                    bO2 = bank()
                    for h in range(6):
                        mm(bO2[:, h * 64:(h + 1) * 64], b["bq"][:, h, cs], R_["Sbf"][:, h * 64:(h + 1) * 64],
                           [b["bq"], R_["Sbf"]], [bO2])
                    if d == 0:
                        bO1 = bank()
                        for h in range(6):
                            mm(bO1[:, h * 64:(h + 1) * 64], rqk[:, h * 128:(h + 1) * 128],
                               b["bkv"][:, sc, 384 + h * 64:384 + (h + 1) * 64], [rqk, b["bkv"]], [bO1])
                        tt("dve", R_["tmp"].ap, bO2[:, 0:384], RSx[d].ap, ALU.mult, [bO2, RSx[d]], [R_["tmp"], bO2])
                        tt("dve", odir[:, 384:768], bO1[:, 0:384], R_["tmp"].ap, ALU.add, [bO1, R_["tmp"]], [odir, bO1])
                    else:
                        tt("dve", odir[:, 384:768], bO2[:, 0:384], RSx[d].ap, ALU.mult, [bO2, RSx[d]], [odir, bO2])
                    tt("pool", R_["vdec"].ap, b["bkv"][:, sc, 384:768], KSx[d].ap, ALU.mult, [b["bkv"], KSx[d]], [R_["vdec"]])
                    yield
                    bS = bank()
                    for h in range(6):
                        mm(bS[0:64, h * 64:(h + 1) * 64], b["bkv"][:, sc, h * 64:(h + 1) * 64],
                           R_["vdec"][:, h * 64:(h + 1) * 64], [b["bkv"], R_["vdec"]], [bS])
                    tt("pool", R_["S32"].ap, R_["S32"].ap, CDx[d][0:64, :], ALU.mult, [R_["S32"], CDx[d]], [R_["S32"]])
                    tt("dve", R_["S32"].ap, R_["S32"].ap, bS[0:64, 0:384], ALU.add, [R_["S32"], bS], [R_["S32"], bS])
                    cp("act", R_["Sbf"].ap, R_["S32"].ap, [R_["S32"]], [R_["Sbf"]])
                    yield

                def att_chain(c, oct_):
                    sc = c % 2
                    cs = slice(sc * 128, (sc + 1) * 128)
                    cq, ck, cv = curc["cq"], curc["ck"], curc["cv"]
                    jvalid = [jj for jj in range(3) if 0 <= c - 1 + jj < N]
                    j0, j1 = jvalid[0], jvalid[-1] + 1
                    pts = []
                    for h in range(4):
                        kvh = h // 2
                        bs_ = bank()
                        for jj in jvalid:
                            kb = sc + jj
                            mm(bs_[:, jj * 128:(jj + 1) * 128], ck[:, kvh, kb * 128:(kb + 1) * 128], cq[:, h, cs],
                               [ck, cq], [bs_])
                        sbt = sbt_r()
                        tt("dve", sbt[:, j0 * 128:j1 * 128], bs_[:, j0 * 128:j1 * 128],
                           cbias[:, h, j0:j1, :].rearrange("p j q -> p (j q)"), ALU.add, [bs_, cst], [sbt, bs_])
                        pT = pTt_r()
                        act(pT[:, j0 * 128:j1 * 128], sbt[:, j0 * 128:j1 * 128], AF.Exp, [sbt], [pT])
                        pts.append(pT)
                        if h % 2 == 1:
                            yield
                    bo = bank()
                    for h2 in range(4):
                        for jj in jvalid:
                            kb = sc + jj
                            mm(bo[:, h2 * 65:h2 * 65 + 65], pts[h2][:, jj * 128:(jj + 1) * 128],
                               cv[:, kb, h2 // 2, :], [pts[h2], cv], [bo], start=(jj == j0), stop=(jj == j1 - 1))
                    den = den_r()
                    tt("dve", den[:, 0:4], bo[:, 0:260].rearrange("p (h e) -> p h e", e=65)[:, :, 64], esink.ap, ALU.add,
                       [bo, esink], [den, bo])
                    recip(den[:, 4:8], den[:, 0:4], [den], [den])
                    tt("dve", oct_.ap.rearrange("p (h e) -> p h e", e=64),
                       bo[:, 0:260].rearrange("p (h e) -> p h e", e=65)[:, :, 0:64],
                       den[:, 4:8].unsqueeze(2).broadcast_to([128, 4, 64]), ALU.mult, [bo, den], [oct_, bo])
                    yield

                def run_lanes(chains):
                    live = list(chains)
                    while live:
                        nxt = []
                        for ch in live:
                            try:
                                next(ch)
                                nxt.append(ch)
                            except StopIteration:
                                pass
                        live = nxt

                gA = {}

                def groupA(d, c):
                    gi = c // 2
                    if gA.get(d, (None, None))[0] != gi:
                        gA[d] = (gi, load_A(d, gi))
                    return gA[d][1]

                def chunk_of(d, t):
                    return t if d == 0 else N - 1 - t

                info = {}
                for d in range(2):
                    info[(d, 0)] = groupA(d, chunk_of(d, 0))
                run_lanes([dn_chain(1, d, hg, chunk_of(d, 0), info[(d, 0)], 0, None) for hg in range(2) for d in range(2)])
                gB = {}
                for t in range(N):
                    cf, cb = t, N - 1 - t
                    for d in range(2):
                        gi = chunk_of(d, t) // 2
                        if gB.get(d) != gi:
                            load_B(d, gi)
                            gB[d] = gi
                    od = [odir_r[0](), odir_r[1]()]
                    oct_ = oc_r()
                    chains = []
                    for hg in range(2):
                        for d in range(2):
                            chains.append(dn_chain(2, d, hg, chunk_of(d, t), info[(d, t)], t % 2, od[d]))
                    if t + 1 < N:
                        for d in range(2):
                            info[(d, t + 1)] = groupA(d, chunk_of(d, t + 1))
                        for hg in range(2):
                            for d in range(2):
                                chains.append(dn_chain(1, d, hg, chunk_of(d, t + 1), info[(d, t + 1)], (t + 1) % 2, None))
                    chains += [ret_chain(0, cf, od[0]), ret_chain(1, cb, od[1]), att_chain(cf, oct_)]
                    run_lanes(chains)
                    info.pop((0, t), None)
                    info.pop((1, t), None)
                    lo = base + cf * 128
                    dma("st_of", of_d[lo:lo + 128, :], od[0].ap, [od[0]], DR("of", lo, lo + 128))
                    dma("st_oc", oc_d[lo:lo + 128, :], oct_.ap, [oct_], DR("oc", lo, lo + 128))
                    lo = base + cb * 128
                    dma("st_ob", ob_d[lo:lo + 128, :], od[1].ap, [od[1]], DR("ob", lo, lo + 128))

    def phase3(l):
        last = (l == L - 1)
        src_d = x_d if l == 0 else h_d
        with ExitStack() as es:
            Wout = es.enter_context(nc.sbuf_tensor(un("Wout"), [128, 8, 1024], BF16)).ap()
            Wpg = es.enter_context(nc.sbuf_tensor(un("Wpg"), [128, 8, 1024], BF16)).ap()
            Wple = es.enter_context(nc.sbuf_tensor(un("Wple"), [128, 2, 1024], BF16)).ap()
            Wout_r = [Res(f"Wout{i}") for i in range(4)]
            Wpg_r = [Res(f"Wpg{i}") for i in range(4)]
            Wple_r = [Res(f"Wple{i}") for i in range(4)]
            with ExitStack() as es2:
                stg = ring(es2, "wstg3", [128, 8, 256], F32, 2)
                for (dst, src, kk, rs) in ((Wout, wout_d[l], 8, Wout_r), (Wpg, wpg_d[l], 8, Wpg_r),
                                           (Wple, wple_d[l], 2, Wple_r)):
                    for bi, c0 in enumerate(range(0, 1024, 256)):
                        st = stg()
                        dma("wld", st[:, 0:kk, :], src[:, :, c0:c0 + 256], [], [st])
                        cp(alt(("act", "dve")), dst[:, :, c0:c0 + 256], st[:, 0:kk, :], [st], [rs[bi]])
                P.barrier()
            dngx = one(es, "dngx", [128, 384], F32)
            for h in range(6):
                dma("sm", dngx[:, h * 64:(h + 1) * 64],
                    dng_d[l:l + 1, :].partition_broadcast(128).rearrange("p a b -> p (a b)"), [], [dngx])
            retg = one(es, "retgx", [128, 384], F32)
            dma("sm", retg.ap, retg_d[l:l + 1, :].partition_broadcast(128).rearrange("p a b -> p (a b)"), [], [retg])
            fing = one(es, "fing", [128, 1024], F32)
            dma("sm", fing.ap, fing_d.partition_broadcast(128).rearrange("p a b -> p (a b)"), [], [fing])

            of_r = ring(es, "of3", [128, 768], F32, 3)
            ob_r = ring(es, "ob3", [128, 768], F32, 3)
            oc_r3 = ring(es, "oc3", [128, 256], F32, 3)
            gt_r = ring(es, "gt3", [128, 1024], BF16, 3)
            x_r = ring(es, "x3", [128, 1024], F32, 3)
            p_r = ring(es, "p3", [128, 256], F32, 3)
            sq_r = ring(es, "sq3", [128, 768], F32, 3)
            st_r = ring(es, "st3", [128, 32], F32, 3)
            t12_r = ring(es, "t12", [128, 768], F32, 3)
            xc_r = ring(es, "xc3", [128, 384], F32, 3)
            mix_r = ring(es, "mix3", [128, 1024], BF16, 3)
            mixT_r = ring(es, "mixT3", [128, 1024], BF16, 3)
            h1_r = ring(es, "h13", [128, 1024], F32, 3)
            h1b_r = ring(es, "h1b3", [128, 1024], BF16, 3)
            h1T_r = ring(es, "h1T3", [128, 1024], BF16, 3)
            sig_r = ring(es, "sig3", [128, 1024], F32, 3)
            pb16_r = ring(es, "pb163", [128, 256], BF16, 3)
            pT_r = ring(es, "pT3", [128, 256], BF16, 3)
            h2_r = ring(es, "h23", [128, 1024], F32, 3)
            y_r = ring(es, "y3", [128, 1024], F32, 3)
            junk = one(es, "junk3", [128, 1024], BF16)

            def p3_sub(si, base, hb, c):
                lo, hi = base + c * 128, base + c * 128 + 128
                xlo = hb + c * 128
                of_, ob_, oc_, gt, xt, pt = of_r(), ob_r(), oc_r3(), gt_r(), x_r(), p_r()
                dma("l3a", of_.ap, of_d[lo:hi, :], DR("of", lo, hi), [of_])
                dma("l3b", ob_.ap, ob_d[lo:hi, :], DR("ob", lo, hi), [ob_])
                dma("l3c", oc_.ap, oc_d[lo:hi, :], DR("oc", lo, hi), [oc_])
                dma("l3d", gt.ap, gate_d[lo:hi, :], DR("gate", lo, hi), [gt])
                dma("l3e", xt.ap, src_d[xlo:xlo + 128, :], ([] if l == 0 else DR("h", xlo, xlo + 128)), [xt])
                po = xoff[si] + c * 128
                dma("l3f", pt.ap, p_d[l, po:po + 128, :], [], [pt])
                tt("pool", of_.ap, of_.ap, ob_.ap, ALU.add, [of_, ob_], [of_])
                t12 = t12_r()
                tt("pool", t12[:, 0:384], gt[:, 0:384], dngx.ap, ALU.mult, [gt, dngx], [t12])
                yield
                sq = sq_r()
                tt("pool", sq.ap, of_.ap, of_.ap, ALU.mult, [of_], [sq])
                tt("pool", t12[:, 384:768], gt[:, 384:768], retg.ap, ALU.mult, [gt, retg], [t12])
                yield
                st = st_r()
                P.op("dve", lambda e, st=st, sq=sq: e.tensor_reduce(
                    out=st[:, 0:12], in_=sq.ap.rearrange("p (h e) -> p h e", e=64), axis=AX.X, op=ALU.add),
                    _rl([sq]), _rl([st]))
                P.op("dve", lambda e, st=st, of_=of_: e.tensor_reduce(
                    out=st[:, 12:18], in_=of_[:, 384:768].rearrange("p (h e) -> p h e", e=64), axis=AX.X, op=ALU.add),
                    _rl([of_]), _rl([st]))
                mix = mix_r()
                tt("pool", mix[:, 768:1024], oc_.ap, gt[:, 768:1024], ALU.mult, [oc_, gt], [mix])
                yield
                act(st[:, 18:24], st[:, 0:6], AF.Sqrt, [st], [st], bias=EPS, scale=1.0 / 64)
                ts("dve", st[:, 12:18], st[:, 12:18], 1.0 / 64, ALU.mult, [st], [st])
                yield
                tt("dve", st[:, 24:30], st[:, 12:18], st[:, 12:18], ALU.mult, [st], [st])
                yield
                stt("dve", st[:, 24:30], st[:, 6:12], 1.0 / 64, st[:, 24:30], ALU.mult, ALU.subtract, [st], [st])
                yield
                act(st[:, 24:30], st[:, 24:30], AF.Sqrt, [st], [st], bias=EPS)
                recip(st[:, 18:24], st[:, 18:24], [st], [st])
                yield
                recip(st[:, 24:30], st[:, 24:30], [st], [st])
                xc = xc_r()
                h6 = lambda ap_: ap_.rearrange("p (h e) -> p h e", e=64)
                b6 = lambda ap_: ap_.unsqueeze(2).broadcast_to([128, 6, 64])
                tt("dve", h6(xc.ap), h6(of_[:, 0:384]), b6(st[:, 18:24]), ALU.mult, [of_, st], [xc])
                yield
                tt("dve", mix[:, 0:384], xc.ap, t12[:, 0:384], ALU.mult, [xc, t12], [mix])
                yield
                tt("pool", h6(xc.ap), h6(of_[:, 384:768]), b6(st[:, 12:18]), ALU.subtract, [of_, st, mix], [xc])
                yield
                tt("dve", h6(xc.ap), h6(xc.ap), b6(st[:, 24:30]), ALU.mult, [xc, st], [xc])
                yield
                tt("dve", mix[:, 384:768], xc.ap, t12[:, 384:768], ALU.mult, [xc, t12], [mix])
                yield
                pb = bank()
                for k in range(8):
                    tr(bf(pb)[:, k * 128:(k + 1) * 128], mix[:, k * 128:(k + 1) * 128], identb.ap, [mix, identb], [pb])
                mixT = mixT_r()
                cp("act", mixT.ap, bf(pb), [pb], [mixT, pb])
                yield
                h1 = h1_r()
                for cg in range(2):
                    pb = bank()
                    for k in range(8):
                        mm(pb.ap, mixT[:, k * 128:(k + 1) * 128], Wout[:, k, cg * 512:(cg + 1) * 512],
                           [mixT, Wout_r[2 * cg], Wout_r[2 * cg + 1]], [pb], start=(k == 0), stop=(k == 7))
                    tt("dve", h1[:, cg * 512:(cg + 1) * 512], pb.ap, xt[:, cg * 512:(cg + 1) * 512], ALU.add, [pb, xt],
                       [h1, pb])
                h1b = h1b_r()
                yield
                cp("act", h1b.ap, h1.ap, [h1], [h1b])
                pb = bank()
                for k in range(8):
                    tr(bf(pb)[:, k * 128:(k + 1) * 128], h1b[:, k * 128:(k + 1) * 128], identb.ap, [h1b, identb], [pb])
                h1T = h1T_r()
                cp("dve", h1T.ap, bf(pb), [pb], [h1T, pb])
                yield
                sig = sig_r()
                for cg in range(2):
                    pb = bank()
                    for k in range(8):
                        mm(pb.ap, h1T[:, k * 128:(k + 1) * 128], Wpg[:, k, cg * 512:(cg + 1) * 512],
                           [h1T, Wpg_r[2 * cg], Wpg_r[2 * cg + 1]], [pb], start=(k == 0), stop=(k == 7))
                    act(sig[:, cg * 512:(cg + 1) * 512], pb.ap, AF.Sigmoid, [pb], [sig, pb])
                pb16 = pb16_r()
                yield
                cp("act", pb16.ap, pt.ap, [pt], [pb16])
                pb = bank()
                for k in range(2):
                    tr(bf(pb)[:, k * 128:(k + 1) * 128], pb16[:, k * 128:(k + 1) * 128], identb.ap, [pb16, identb], [pb])
                pT = pT_r()
                cp("act", pT.ap, bf(pb)[:, 0:256], [pb], [pT, pb])
                yield
                h2 = h2_r()
                for cg in range(2):
                    pb = bank()
                    for k in range(2):
                        mm(pb.ap, pT[:, k * 128:(k + 1) * 128], Wple[:, k, cg * 512:(cg + 1) * 512],
                           [pT, Wple_r[2 * cg], Wple_r[2 * cg + 1]], [pb], start=(k == 0), stop=(k == 1))
                    cs_ = slice(cg * 512, (cg + 1) * 512)
                    tt("dve", h2[:, cs_], pb.ap, sig[:, cs_], ALU.mult, [pb, sig], [h2, pb])
                    tt("pool", h2[:, cs_], h2[:, cs_], h1[:, cs_], ALU.add, [h2, h1], [h2])
                yield
                if not last:
                    dma("st_h", h_d[lo:hi, :], h2.ap, [h2], DR("h", lo, hi))
                else:
                    mset("pool", st[:, 30:31], 0.0, [st])
                    act(junk.ap, h2.ap, AF.Square, [h2, st], [junk, st], accum=st[:, 30:31])
                    act(st[:, 31:32], st[:, 30:31], AF.Sqrt, [st], [st], bias=EPS, scale=1.0 / D)
                    recip(st[:, 31:32], st[:, 31:32], [st], [st])
                    yt = y_r()
                    stt("dve", yt.ap, h2.ap, st[:, 31:32], fing.ap, ALU.mult, ALU.mult, [h2, st, fing], [yt])
                    dma("st_y", y_d[po:po + 128, :], yt.ap, [yt], [], final=True)

                yield

            K3 = 3
            work = []
            for si, T in enumerate(seqs):
                base = bases[si]
                hb = xoff[si] if l == 0 else base
                for c in range(T // 128):
                    work.append((si, base, hb, c))
            live = []
            wi = 0
            rounds = 0
            while wi < len(work) or live:
                if wi < len(work) and len(live) < K3 and (not live or rounds % 4 == 0):
                    live.append(p3_sub(*work[wi]))
                    wi += 1
                nxt = []
                for ch in live:
                    try:
                        next(ch)
                        nxt.append(ch)
                    except StopIteration:
                        pass
                live = nxt
                rounds += 1

    for l in range(L):
        phase1(l)
        P.barrier()
        phase2(l)
        P.barrier()
        phase3(l)
        P.barrier()
    P.emit()
    return nc


def _wl(w, kk):
    L_, _, C = w.shape
    return np.ascontiguousarray(w.reshape(L_, kk, 128, C).transpose(0, 2, 1, 3))


def prep_shared(w_in, w_out, norm_g, conv_w, dn_a_log, dn_dt_bias, dn_norm_g, ret_decay_z, ret_norm_g,
                attn_sink, w_ple, w_pg, final_g):
    L_ = w_in.shape[0]
    sp = np.cumsum([0, 1152, 384, 12, 12, 384, 384, 384, 384, 256, 128, 128, 256])
    seg = lambda i: w_in[:, :, sp[i]:sp[i + 1]]
    wfm = np.concatenate([seg(0), seg(4), seg(5), seg(8), seg(9)], axis=2)
    wtm = np.concatenate([seg(1), seg(7), seg(11), seg(5), seg(6), seg(10), seg(2), seg(3)], axis=2)
    return dict(
        wfm=_wl(wfm, 8), wtm=_wl(wtm, 8), wout=_wl(w_out, 8), wpg=_wl(w_pg, 8), wple=_wl(w_ple, 2),
        normg=np.ascontiguousarray(norm_g.reshape(L_, 8, 128).transpose(0, 2, 1)),
        convw=np.ascontiguousarray(conv_w.reshape(L_, 5, 9, 128).transpose(0, 3, 2, 1)),
        alog=np.ascontiguousarray(dn_a_log.reshape(L_, 12)), dtb=np.ascontiguousarray(dn_dt_bias.reshape(L_, 12)),
        dng=np.ascontiguousarray(dn_norm_g), retz=np.ascontiguousarray(ret_decay_z.reshape(L_, 12)),
        retg=np.ascontiguousarray(ret_norm_g), sink=np.ascontiguousarray(attn_sink),
        fing=np.ascontiguousarray(final_g.reshape(1, D)), consts=make_consts())


def kernel(x_prompt, x_sample, p_prompt, p_sample, w_in, w_out, norm_g, conv_w, dn_a_log, dn_dt_bias,
           dn_norm_g, ret_decay_z, ret_norm_g, attn_sink, w_ple, w_pg, final_g):
    f = lambda a: np.asarray(a, np.float32)
    x_prompt, x_sample, p_prompt, p_sample = f(x_prompt), f(x_sample), f(p_prompt), f(p_sample)
    shared = prep_shared(*[f(a) for a in (w_in, w_out, norm_g, conv_w, dn_a_log, dn_dt_bias, dn_norm_g,
                                          ret_decay_z, ret_norm_g, attn_sink, w_ple, w_pg, final_g)])
    n = 8
    B, S, _ = x_prompt.shape
    DB, DS, _ = x_sample.shape
    per = DB // n
    seqs = [S] + [DS] * per
    nc = build_program(seqs, L=w_in.shape[0])
    in_maps = []
    for c in range(n):
        xs = [x_prompt[c]] + [x_sample[c * per + i] for i in range(per)]
        ps = [p_prompt[:, c]] + [p_sample[:, c * per + i] for i in range(per)]
        m = dict(shared)
        m["x"] = np.ascontiguousarray(np.concatenate(xs, axis=0))
        m["p"] = np.ascontiguousarray(np.concatenate(ps, axis=1))
        in_maps.append(m)
    res = run_bass_kernel_spmd(nc, in_maps, core_ids=list(range(n)))
    yp = np.empty((B, S, D), np.float32)
    ys = np.empty((DB, DS, D), np.float32)
    for c in range(n):
        y = res.results[c]["y"]
        yp[c] = y[0:S]
        for i in range(per):
            ys[c * per + i] = y[S + i * DS:S + (i + 1) * DS]
    return (yp, ys)
```

```python
from contextlib import ExitStack
import numpy as np
import concourse.bass as bass
import concourse.mybir as mybir
from concourse.bass_utils import run_bass_kernel_spmd

F32 = mybir.dt.float32
BF16 = mybir.dt.bfloat16
ALU = mybir.AluOpType
AF = mybir.ActivationFunctionType
AX = mybir.AxisListType

D = 1024
EPS = 1e-6
NEG = -30000.0
PLE = 256
PADS = 640
EPOCH = 24000
NCONST = 11 * 128 + 4 + 4 * 3 * 128 + 64 + 128


class Res:
    __slots__ = ("name", "last_w", "readers")

    def __init__(self, name=""):
        self.name = name
        self.last_w = None
        self.readers = []


class SemCtr:
    __slots__ = ("nc", "name", "sems", "count")

    def __init__(self, nc, name):
        self.nc = nc
        self.name = name
        self.sems = []
        self.count = 0

    def next_event(self, inc):
        ep = self.count // EPOCH
        while len(self.sems) <= ep:
            self.sems.append(self.nc.alloc_semaphore(name=f"{self.name}_{len(self.sems)}"))
        self.count += inc
        return self.sems[ep], self.count - ep * EPOCH


class Op:
    __slots__ = ("eng", "fn", "deps", "needs_inc", "ev_sem", "ev_val", "dsem")

    def __init__(self, eng, fn, dsem):
        self.eng = eng
        self.fn = fn
        self.deps = None
        self.needs_inc = False
        self.ev_sem = None
        self.ev_val = None
        self.dsem = dsem


ENGS = ("pe", "act", "dve", "pool", "sp")


class Prog:
    def __init__(self, nc):
        self.nc = nc
        self.ops = {e: [] for e in ENGS}
        self.final_ops = []
        self.extra = []
        self.last_dma = {}
        self.nd = 0

    def new_dsem(self, name=None):
        self.nd += 1
        return SemCtr(self.nc, name or f"dq{self.nd}")

    def barrier(self):
        ex = []
        for e in ENGS:
            for o in reversed(self.ops[e]):
                if o.dsem is None:
                    ex.append(o)
                    break
        ex.extend(self.last_dma.values())
        for o in ex:
            o.needs_inc = True
        self.extra = ex

    def op(self, eng, fn, reads=(), writes=(), dsem=None, after=()):
        o = Op(eng, fn, dsem)
        deps = list(after)
        raw = set()
        for r in reads:
            if r.last_w is not None:
                deps.append(r.last_w)
                raw.add(id(r.last_w))
        for w in writes:
            if w.last_w is not None:
                deps.append(w.last_w)
            deps.extend(w.readers)
        deps.extend(self.extra)
        fl = []
        seen = set()
        for d in deps:
            if id(d) in seen or d is o:
                continue
            seen.add(id(d))
            if d.dsem is None and d.eng == eng and eng == "pe":
                continue
            fl.append(d)
            d.needs_inc = True
        o.deps = fl
        for r in reads:
            r.readers.append(o)
        for w in writes:
            w.last_w = o
            w.readers = []
        self.ops[eng].append(o)
        if dsem is not None:
            self.last_dma[id(dsem)] = o
        return o

    def emit(self):
        nc = self.nc
        for e in ENGS:
            ctr = SemCtr(nc, f"eng_{e}")
            for o in self.ops[e]:
                if o.dsem is not None:
                    o.ev_sem, o.ev_val = o.dsem.next_event(16)
                elif o.needs_inc:
                    o.ev_sem, o.ev_val = ctr.next_event(1)
        final_waits = [(o.ev_sem, o.ev_val) for o in self.final_ops]
        prog = self

        def run_engine(ename, eng):
            waited = {}
            for o in prog.ops[ename]:
                need = {}
                for d in o.deps:
                    k = id(d.ev_sem)
                    if k not in need or need[k][1] < d.ev_val:
                        need[k] = (d.ev_sem, d.ev_val)
                for k, (s, v) in need.items():
                    if waited.get(k, 0) >= v:
                        continue
                    eng.wait_ge(s, v)
                    waited[k] = v
                inst = o.fn(eng)
                if o.dsem is not None:
                    inst.then_inc(o.ev_sem, 16)
                elif o.needs_inc:
                    inst.then_inc(o.ev_sem, 1)
            if ename == "sp":
                for (s, v) in final_waits:
                    eng.wait_ge(s, v)

        with nc.Block() as block:
            @block.tensor
            def _(eng):
                run_engine("pe", eng)

            @block.scalar
            def _(eng):
                run_engine("act", eng)

            @block.vector
            def _(eng):
                run_engine("dve", eng)

            @block.gpsimd
            def _(eng):
                run_engine("pool", eng)

            @block.sync
            def _(eng):
                run_engine("sp", eng)


class Tl:
    __slots__ = ("ap", "res")

    def __init__(self, ap, name=""):
        self.ap = ap
        self.res = Res(name)

    def __getitem__(self, k):
        return self.ap[k]


def _rl(lst):
    out = []
    for t in lst:
        if isinstance(t, Res):
            out.append(t)
        elif isinstance(t, Tl):
            out.append(t.res)
        else:
            out.extend(_rl(t))
    return out


def make_consts():
    p = np.arange(128)[:, None].astype(np.float64)
    f = np.arange(128)[None, :].astype(np.float64)
    c = np.zeros((128, NCONST), np.float32)
    o = 0

    def put(a):
        nonlocal o
        a = np.asarray(a, np.float32)
        c[:, o:o + a.shape[1]] = a
        o += a.shape[1]

    put(p == f)
    put(np.where(f < p, 0.0, NEG))
    put(np.where(f > p, 0.0, NEG))
    put(p <= f)
    put(p >= f)
    put(np.ones((128, 128)))
    put((p // 64) == (f // 64))
    put(np.maximum(f - p, 0))
    put(np.maximum(p - f, 0))
    put(p <= f)
    put(p >= f)
    put(np.concatenate([p + 1, 128 - p, 127 - p, p], axis=1))
    b = np.zeros((128, 4, 3, 128), np.float64)
    for h in range(4):
        slope = 2.0 ** (-8.0 * (h + 1) / 4)
        for jj in range(3):
            rel = (jj - 1) * 128 + p - f
            b[:, h, jj, :] = np.where(np.abs(rel) <= 128, -slope * np.abs(rel), NEG)
    put(b.reshape(128, -1))
    put(np.ones((128, 64)))
    put((p // 64) != (f // 64))
    assert o == NCONST
    return c


def build_program(seqs, L=2, dbg=False):
    nc = bass.Bass("TRN2", target_bir_lowering=False)
    P = Prog(nc)
    TOK = sum(seqs)
    xoff = [sum(seqs[:i]) for i in range(len(seqs))]
    bases = []
    g = 0
    for T in seqs:
        bases.append(g)
        g += T + PADS
    G = g

    def din(name, shape, dt=F32):
        return nc.dram_tensor(name, list(shape), dt, kind="ExternalInput").ap()

    def dscr(name, shape, dt):
        kind = "ExternalOutput" if (dbg and name in ("of", "ob", "oc", "aqT", "akT", "aktm", "avtm", "tsc", "dT")) else "Internal"
        return nc.dram_tensor(name, list(shape), dt, kind=kind).ap()

    x_d = din("x", [TOK, D])
    p_d = din("p", [L, TOK, PLE])
    wfm_d = din("wfm", [L, 128, 8, 2304])
    wtm_d = din("wtm", [L, 128, 8, 1944])
    wout_d = din("wout", [L, 128, 8, 1024])
    wpg_d = din("wpg", [L, 128, 8, 1024])
    wple_d = din("wple", [L, 128, 2, 1024])
    normg_d = din("normg", [L, 128, 8])
    convw_d = din("convw", [L, 128, 9, 5])
    alog_d = din("alog", [L, 12])
    dtb_d = din("dtb", [L, 12])
    dng_d = din("dng", [L, 64])
    retz_d = din("retz", [L, 12])
    retg_d = din("retg", [L, 384])
    sink_d = din("sink", [L, 4])
    fing_d = din("fing", [1, D])
    consts_d = din("consts", [128, NCONST])
    y_d = nc.dram_tensor("y", [TOK, D], F32, kind="ExternalOutput").ap()

    aqT_d = dscr("aqT", [384, G], BF16)
    akT_d = dscr("akT", [384, G], BF16)
    aktm_d = dscr("aktm", [G, 384], BF16)
    avtm_d = dscr("avtm", [G, 384], BF16)
    bqT_d = dscr("bqT", [384, G], BF16)
    bkT_d = dscr("bkT", [384, G], BF16)
    cqT_d = dscr("cqT", [256, G], BF16)
    ckT_d = dscr("ckT", [128, G], BF16)
    kvtm_d = dscr("kvtm", [G, 896], BF16)
    gate_d = dscr("gatetm", [G, 1024], BF16)
    dT_d = dscr("dT", [2, 6, G], F32)
    tsc_d = dscr("tsc", [G, 84], F32)
    of_d = dscr("of", [G, 768], F32)
    ob_d = dscr("ob", [G, 768], F32)
    oc_d = dscr("oc", [G, 256], F32)
    h_d = dscr("hbuf", [G, D], F32)

    dres = {}

    def DR(name, lo, hi):
        out = []
        for cidx in range(lo // 128, (hi - 1) // 128 + 1):
            k = (name, cidx)
            if k not in dres:
                dres[k] = Res(f"{name}{cidx}")
            out.append(dres[k])
        return out

    def mm(out, lhsT, rhs, R, W, start=True, stop=True):
        P.op("pe", lambda e: e.matmul(out, lhsT=lhsT, rhs=rhs, start=start, stop=stop), _rl(R), _rl(W))

    def tr(out, in_, ident, R, W):
        P.op("pe", lambda e: e.transpose(out=out, in_=in_, identity=ident), _rl(R), _rl(W))

    def act(out, in_, func, R, W, bias=None, scale=None, accum=None):
        kw = {}
        if bias is not None:
            kw["bias"] = bias
        if scale is not None:
            kw["scale"] = scale
        if accum is not None:
            kw["accum_out"] = accum
        P.op("act", lambda e: e.activation(out=out, in_=in_, func=func, **kw), _rl(R), _rl(W))

    def tt(eng, out, in0, in1, op, R, W):
        P.op(eng, lambda e: e.tensor_tensor(out=out, in0=in0, in1=in1, op=op), _rl(R), _rl(W))

    def ts(eng, out, in0, s1, op0, R, W, s2=None, op1=None):
        if op1 is None:
            P.op(eng, lambda e: e.tensor_scalar(out=out, in0=in0, scalar1=s1, scalar2=None, op0=op0), _rl(R), _rl(W))
        else:
            P.op(eng, lambda e: e.tensor_scalar(out=out, in0=in0, scalar1=s1, scalar2=s2, op0=op0, op1=op1),
                 _rl(R), _rl(W))

    def stt(eng, out, in0, scalar, in1, op0, op1, R, W):
        P.op(eng, lambda e: e.scalar_tensor_tensor(out=out, in0=in0, scalar=scalar, in1=in1, op0=op0, op1=op1),
             _rl(R), _rl(W))

    def cp(eng, out, in_, R, W):
        if eng == "act":
            P.op("act", lambda e: e.activation(out=out, in_=in_, func=AF.Copy), _rl(R), _rl(W))
        else:
            P.op(eng, lambda e: e.tensor_copy(out=out, in_=in_), _rl(R), _rl(W))

    def recip(out, in_, R, W):
        P.op("dve", lambda e: e.reciprocal(out=out, in_=in_), _rl(R), _rl(W))

    def mset(eng, out, val, W):
        P.op(eng, lambda e: e.memset(out, val), (), _rl(W))

    dq = {}
    DQK = {"st_qk": 4, "st_fm": 4, "g0": 6, "g1": 6, "gc": 4}

    def dma(key, out, in_, R, W, final=False, eng="sp"):
        if key not in dq:
            k = DQK.get(key, 2)
            dq[key] = [[P.new_dsem(f"d_{key}{i}"), None] for i in range(k)] + [0]
        ent = dq[key]
        slot = ent[ent[-1] % (len(ent) - 1)]
        ent[-1] += 1
        o = P.op(eng, lambda e: e.dma_start(out=out, in_=in_), _rl(R), _rl(W), dsem=slot[0],
                 after=([slot[1]] if slot[1] is not None else ()))
        slot[1] = o
        if final:
            P.final_ops.append(o)
        return o

    banks = []
    for i in range(8):
        t = nc.alloc_psum_tensor(f"pb{i}", [128, 512], F32)
        banks.append(Tl(t.ap(), f"pb{i}"))
    bstate = [0]

    def bank():
        b = banks[bstate[0] % 8]
        bstate[0] += 1
        return b

    def bf(b):
        return b.ap.bitcast(BF16)

    uid = [0]

    def un(name):
        uid[0] += 1
        return f"s{uid[0]}_{name}"

    def sb(name, shape, dt):
        return Tl(nc.alloc_sbuf_tensor(un(name), list(shape), dt).ap(), name)

    cst = sb("consts", [128, NCONST], F32)
    dma("c0", cst.ap, consts_d, [], [cst])

    def cblk(i):
        return cst[:, i * 128:(i + 1) * 128]

    identf, NMf, NMb, CUMf, CUMb, onesf = [cblk(i) for i in range(6)]
    RELP, RELN, MASKF, MASKB = [cblk(i) for i in range(7, 11)]
    posc = cst[:, 1408:1412]
    cbias = cst[:, 1412:1412 + 1536].rearrange("p (h j q) -> p h j q", h=4, j=3)
    ones64 = cst[:, 1412 + 1536:1412 + 1536 + 64]
    OFF64 = cst[:, 1412 + 1536 + 64:1412 + 1536 + 64 + 128]
    BD64 = cblk(6)
    identb = sb("identb", [128, 128], BF16)
    cp("dve", identb.ap, identf, [cst], [identb])
    blockones = sb("blockones", [128, 128], BF16)
    cp("dve", blockones.ap, cblk(6), [cst], [blockones])
    NM = [NMf, NMb]
    CUM = [CUMf, CUMb]
    epsc = sb("epsc", [128, 1], F32)
    mset("pool", epsc.ap, EPS, [epsc])

    rings = {}

    def ring(es, name, shape, dt, n=2):
        tl = [Tl(es.enter_context(nc.sbuf_tensor(un(f"{name}{i}"), list(shape), dt)).ap(), f"{name}{i}")
              for i in range(n)]
        st = [0]

        def nxt():
            t = tl[st[0] % n]
            st[0] += 1
            return t
        return nxt

    def one(es, name, shape, dt):
        return Tl(es.enter_context(nc.sbuf_tensor(un(name), list(shape), dt)).ap(), name)

    ENG3 = ("dve", "pool")
    rr = [0]

    def alt(choices=("act", "dve")):
        rr[0] += 1
        return choices[rr[0] % len(choices)]

    def phase1(l):
        src_d = x_d if l == 0 else h_d
        with ExitStack() as es:
            Wfm = es.enter_context(nc.sbuf_tensor(un("Wfm"), [128, 8, 2304], BF16)).ap()
            Wtm = es.enter_context(nc.sbuf_tensor(un("Wtm"), [128, 8, 1944], BF16)).ap()
            Wfm_r = [Res(f"Wfm{i}") for i in range(9)]
            Wtm_r = [Res(f"Wtm{i}") for i in range(8)]
            with ExitStack() as es2:
                stg = ring(es2, "wstg", [128, 8, 256], F32, 2)
                for (dst, src, ncols, rs) in ((Wfm, wfm_d[l], 2304, Wfm_r), (Wtm, wtm_d[l], 1944, Wtm_r)):
                    for bi, c0 in enumerate(range(0, ncols, 256)):
                        cw = min(256, ncols - c0)
                        st = stg()
                        dma("wld", st[:, :, 0:cw], src[:, :, c0:c0 + cw], [], [st])
                        cp(alt(("act", "dve")), dst[:, :, c0:c0 + cw], st[:, :, 0:cw], [st], [rs[bi]])
                P.barrier()
            normg = one(es, "normg", [128, 8], F32)
            dma("sm", normg.ap, normg_d[l], [], [normg])
            cw_t = one(es, "convw", [128, 9, 5], F32)
            dma("sm", cw_t.ap, convw_d[l], [], [cw_t])
            negA = one(es, "negA", [128, 12], F32)
            dma("sm", negA.ap, alog_d[l:l + 1, :].partition_broadcast(128).rearrange("p a b -> p (a b)"), [], [negA])
            act(negA.ap, negA.ap, AF.Exp, [negA], [negA])
            ts("dve", negA.ap, negA.ap, -1.0, ALU.mult, [negA], [negA])
            dtb = one(es, "dtb", [128, 12], F32)
            dma("sm", dtb.ap, dtb_d[l:l + 1, :].partition_broadcast(128).rearrange("p a b -> p (a b)"), [], [dtb])

            xt_r = ring(es, "xt", [128, 4, 1024], F32, 1)
            junk = one(es, "junk", [128, 1024], BF16)
            ss_r = ring(es, "ss", [128, 4], F32, 2)
            xnb_r = ring(es, "xnb", [128, 4, 1024], BF16, 1)
            xnT_r = ring(es, "xnT", [128, 8, 512], BF16, 2)
            Rb = es.enter_context(nc.sbuf_tensor(un("Rraw"), [128, 9, 516], F32)).ap()
            R_r = [Res(f"R{m}") for m in range(9)]
            acc_r = ring(es, "acc", [128, 512], F32, 4)
            tmpc_r = ring(es, "tmpc", [128, 512], F32, 2)
            sact_r = ring(es, "sact", [128, 512], F32, 4)
            sq_r = ring(es, "sq", [128, 512], BF16, 4)
            rn_r = ring(es, "rn", [128, 512], F32, 4)
            qn_r = ring(es, "qn", [128, 512], BF16, 5)
            stgb_r = ring(es, "stgb", [128, 512], BF16, 5)
            kst_r = ring(es, "kst", [128, 4, 384], BF16, 1)
            vst_r = ring(es, "vst", [128, 4, 384], BF16, 1)
            gst_r = ring(es, "gst", [128, 1024], BF16, 2)
            kvst_r = ring(es, "kvst", [128, 896], BF16, 2)
            tsst_r = ring(es, "tsst", [128, 4, 84], F32, 1)
            dts_r = ring(es, "dts", [6, 2, 512], F32, 1)
            ba_r = ring(es, "ba", [128, 24], F32, 5)
            sm_r = ring(es, "smt", [128, 12 * 8], F32, 5)

            def win_m(m, lo, hi, kst, vst):
                acc = acc_r()
                if m != 4:
                    ts("dve", acc.ap, Rb[:, m, 0:512], cw_t[:, m, 0:1], ALU.mult, [R_r[m], cw_t], [acc])
                    yield
                    for j in range(1, 5):
                        stt("dve", acc.ap, Rb[:, m, j:j + 512], cw_t[:, m, j:j + 1], acc.ap, ALU.mult, ALU.add,
                            [R_r[m], cw_t, acc], [acc])
                        yield
                else:
                    act(acc.ap, Rb[:, m, 0:512], AF.Copy, [R_r[m], cw_t], [acc], scale=cw_t[:, m, 0:1])
                    yield
                    for j in range(1, 5):
                        tmpc = tmpc_r()
                        act(tmpc.ap, Rb[:, m, j:j + 512], AF.Copy, [R_r[m], cw_t], [tmpc], scale=cw_t[:, m, j:j + 1])
                        tt("pool", acc.ap, acc.ap, tmpc.ap, ALU.add, [acc, tmpc], [acc])
                        yield
                cp("pool", Rb[:, m, 0:4], Rb[:, m, 512:516], [R_r[m]], [R_r[m]])
                sact = sact_r()
                act(sact.ap, acc.ap, AF.Silu, [acc], [sact])
                yield
                if m < 6:
                    sq = sq_r()
                    act(sq.ap, sact.ap, AF.Square, [sact], [sq])
                    yield
                    pb = bank()
                    mm(pb.ap, blockones.ap, sq.ap, [blockones, sq], [pb])
                    rn = rn_r()
                    act(rn.ap, pb.ap, AF.Ln, [pb, epsc], [rn, pb], bias=epsc[:, 0:1])
                    yield
                    act(rn.ap, rn.ap, AF.Exp, [rn], [rn], scale=-0.5)
                    yield
                    qn = qn_r()
                    stt("dve", qn.ap, sact.ap, 0.125 if m < 3 else 1.0, rn.ap, ALU.mult, ALU.mult,
                        [sact, rn], [qn])
                    dst = aqT_d if m < 3 else akT_d
                    nm = "aqT" if m < 3 else "akT"
                    r0 = (m % 3) * 128
                    dma("st_qk", dst[r0:r0 + 128, lo:hi], qn.ap, [qn], DR(nm, lo, hi))
                    yield
                    if m >= 3:
                        pb2 = bank()
                        for s in range(4):
                            tr(bf(pb2)[:, s * 128:(s + 1) * 128], qn[:, s * 128:(s + 1) * 128], identb.ap,
                               [qn, identb], [pb2])
                        cp(alt(), kst[:, :, (m - 3) * 128:(m - 2) * 128],
                           bf(pb2)[:, 0:512].rearrange("p (s c) -> p s c", s=4), [pb2], [kst, pb2])
                        yield
                else:
                    vb = qn_r()
                    cp("dve", vb.ap, sact.ap, [sact], [vb])
                    yield
                    pb2 = bank()
                    for s in range(4):
                        tr(bf(pb2)[:, s * 128:(s + 1) * 128], vb[:, s * 128:(s + 1) * 128], identb.ap,
                           [vb, identb], [pb2])
                    cp(alt(), vst[:, :, (m - 6) * 128:(m - 5) * 128],
                       bf(pb2)[:, 0:512].rearrange("p (s c) -> p s c", s=4), [pb2], [vst, pb2])
                    yield

            def window(si, T, t0, base):
                kst = kst_r()
                vst = vst_r()
                lo, hi = base + t0, base + t0 + 512
                pending = [win_m(m, lo, hi, kst, vst) for m in range(9)]
                live = []
                while pending or live:
                    while len(live) < 3 and pending:
                        live.append(pending.pop(0))
                    nxt = []
                    for ch in live:
                        try:
                            next(ch)
                            nxt.append(ch)
                        except StopIteration:
                            pass
                    live = nxt
                    yield
                dma("st_ktm", aktm_d[lo:hi, :].rearrange("(s p) f -> p s f", p=128), kst.ap, [kst],
                    DR("aktm", lo, hi))
                dma("st_vtm", avtm_d[lo:hi, :].rearrange("(s p) f -> p s f", p=128), vst.ap, [vst],
                    DR("avtm", lo, hi))

            for si, T in enumerate(seqs):
                base = bases[si]
                hb = xoff[si] if l == 0 else base
                for m in range(9):
                    mset("pool", Rb[:, m, 0:4], 0.0, [R_r[m]])
                def partA(t0):
                    lo, hi = base + t0, base + t0 + 512
                    xt = xt_r()
                    rd = [] if l == 0 else DR("h", hb + t0, hb + t0 + 512)
                    dma("ld_x", xt.ap, src_d[hb + t0:hb + t0 + 512, :].rearrange("(s p) f -> p s f", p=128), rd, [xt])
                    ss = ss_r()
                    mset("pool", ss.ap, 0.0, [ss])
                    for s in range(4):
                        act(junk.ap, xt[:, s, :], AF.Square, [xt, ss], [junk, ss], accum=ss[:, s:s + 1])
                    act(ss.ap, ss.ap, AF.Ln, [ss, epsc], [ss], bias=epsc[:, 0:1], scale=1.0 / D)
                    act(ss.ap, ss.ap, AF.Exp, [ss], [ss], scale=-0.5)
                    xnb = xnb_r()
                    for s in range(4):
                        if s % 2 == 0:
                            ts("dve", xnb[:, s, :], xt[:, s, :], ss[:, s:s + 1], ALU.mult, [xt, ss], [xnb])
                        else:
                            act(xnb[:, s, :], xt[:, s, :], AF.Copy, [xt, ss], [xnb], scale=ss[:, s:s + 1])
                    return xnb

                def partA2(xnb):
                    xnT = xnT_r()
                    for k in range(8):
                        pb = bank()
                        for s in range(4):
                            tr(bf(pb)[:, s * 128:(s + 1) * 128], xnb[:, s, k * 128:(k + 1) * 128], identb.ap,
                               [xnb, identb], [pb])
                        if k % 2 == 0:
                            act(xnT[:, k, :], bf(pb)[:, 0:512], AF.Copy, [pb, normg], [xnT, pb], scale=normg[:, k:k + 1])
                        else:
                            ts("dve", xnT[:, k, :], bf(pb)[:, 0:512], normg[:, k:k + 1], ALU.mult, [pb, normg], [xnT, pb])
                    return xnT

                def partC(t0, xnT):
                    lo, hi = base + t0, base + t0 + 512
                    for m in range(18):
                        pb = bank()
                        for k in range(8):
                            mm(pb.ap, Wfm[:, k, m * 128:(m + 1) * 128], xnT[:, k, :], [Wfm_r[(m * 128) // 256], xnT],
                               [pb], start=(k == 0), stop=(k == 7))
                        if m < 9:
                            cp(alt(), Rb[:, m, 4:516], pb.ap, [pb], [R_r[m], pb])
                        else:
                            st = stgb_r()
                            if m < 12:
                                ts("dve", st.ap, pb.ap, 0.125, ALU.mult, [pb], [st, pb])
                                dst, nm, r0 = bqT_d, "bqT", (m - 9) * 128
                            elif m < 15:
                                cp(alt(), st.ap, pb.ap, [pb], [st, pb])
                                dst, nm, r0 = bkT_d, "bkT", (m - 12) * 128
                            elif m < 17:
                                act(st.ap, pb.ap, AF.Copy, [pb], [st, pb], scale=0.125)
                                dst, nm, r0 = cqT_d, "cqT", (m - 15) * 128
                            else:
                                cp(alt(), st.ap, pb.ap, [pb], [st, pb])
                                dst, nm, r0 = ckT_d, "ckT", 0
                            dma("st_fm", dst[r0:r0 + 128, lo:hi], st.ap, [st], DR(nm, lo, hi))

                xnT_next = partA2(partA(0))
                for t0 in range(0, T, 512):
                    lo, hi = base + t0, base + t0 + 512
                    xnT = xnT_next
                    def tm_part(xnT=xnT, lo=lo, hi=hi, t0=t0):
                        for s in range(4):
                            gst = gst_r()
                            kvst = kvst_r()
                            ba = ba_r()
                            for g4 in range(4):
                                c0 = g4 * 512
                                cw = min(512, 1944 - c0)
                                pb = bank()
                                for k in range(8):
                                    mm(pb[:, 0:cw], xnT[:, k, s * 128:(s + 1) * 128], Wtm[:, k, c0:c0 + cw],
                                       [xnT, Wtm_r[c0 // 256], Wtm_r[(c0 + cw - 1) // 256]], [pb],
                                       start=(k == 0), stop=(k == 7))
                                if g4 < 2:
                                    act(gst[:, c0:c0 + 512], pb.ap, AF.Silu, [pb], [gst, pb])
                                elif g4 == 2:
                                    cp(alt(), kvst[:, 0:512], pb.ap, [pb], [kvst, pb])
                                else:
                                    cp(alt(), kvst[:, 512:896], pb[:, 0:384], [pb], [kvst, pb])
                                    cp("dve", ba.ap, pb[:, 384:408], [pb], [ba, pb])
                            sl, sh = lo + s * 128, lo + (s + 1) * 128
                            dma("st_gate", gate_d[sl:sh, :], gst.ap, [gst], DR("gate", sl, sh))
                            dma("st_kv", kvtm_d[sl:sh, :], kvst.ap, [kvst], DR("kvtm", sl, sh))
                            lanes.append(sm_chain(s, ba, tsst_s[s], dts_s[s]))
                            yield
                        yield

                    def sm_chain(s, ba, tsl_t, dts_t):
                        sm = sm_r()
                        z, nz, mn, ee, sp_, gg, bet, tmp = [sm[:, i * 12:(i + 1) * 12] for i in range(8)]
                        tsl = tsl_t.ap
                        act(bet, ba[:, 0:12], AF.Sigmoid, [ba], [sm])
                        tt("dve", z, ba[:, 12:24], dtb.ap, ALU.add, [ba, dtb], [sm])
                        yield
                        cp("dve", tsl[:, 24:36], bet, [sm], [tsl_t])
                        ts("dve", tsl[:, 0:12], bet, -1.0, ALU.mult, [sm], [tsl_t])
                        ts("dve", nz, z, -1.0, ALU.mult, [sm], [sm])
                        yield
                        tt("dve", mn, z, nz, ALU.min, [sm], [sm])
                        yield
                        act(ee, mn, AF.Exp, [sm], [sm])
                        ts("dve", sp_, z, 0.0, ALU.max, [sm], [sm])
                        yield
                        act(ee, ee, AF.Ln, [sm], [sm], bias=1.0)
                        yield
                        tt("dve", sp_, sp_, ee, ALU.add, [sm], [sm])
                        yield
                        tt("dve", gg, sp_, negA.ap, ALU.mult, [sm, negA], [sm])
                        yield
                        pd = bank()
                        mm(pd[:, 0:6], CUMf, gg[:, 0:6], [cst, sm], [pd])
                        mm(pd[:, 6:12], CUMb, gg[:, 6:12], [cst, sm], [pd])
                        mm(pd[:, 12:24], onesf, gg, [cst, sm], [pd])
                        mm(pd[0:6, 128:256], gg[:, 0:6], CUMf, [cst, sm], [pd])
                        mm(pd[0:6, 256:384], gg[:, 6:12], CUMb, [cst, sm], [pd])
                        cp("dve", tsl[:, 12:24], pd[:, 0:12], [pd], [tsl_t, pd])
                        act(tsl[:, 48:60], pd[:, 0:12], AF.Exp, [pd], [tsl_t, pd])
                        act(tsl[:, 72:84], pd[:, 12:24], AF.Exp, [pd], [tsl_t, pd])
                        tt("dve", tmp, pd[:, 12:24], tsl[:, 12:24], ALU.subtract, [pd, tsl_t], [sm, pd])
                        cp("dve", dts_t.ap, pd[0:6, 128:384].rearrange("p (d c) -> p d c", d=2), [pd], [dts_t, pd])
                        yield
                        act(tsl[:, 60:72], tmp, AF.Exp, [sm], [tsl_t])
                        tt("dve", tsl[:, 36:48], tsl[:, 24:36], tsl[:, 48:60], ALU.mult, [tsl_t], [tsl_t])
                        yield

                    def run_l(lanes):
                        live = []
                        idx = 0
                        while idx < len(lanes) or live:
                            while idx < len(lanes):
                                live.append(lanes[idx])
                                idx += 1
                            nxt = []
                            for ch in live:
                                try:
                                    next(ch)
                                    nxt.append(ch)
                                except StopIteration:
                                    pass
                            live = nxt

                    tsst = tsst_r()
                    dts = dts_r()
                    tsst_s = [Tl(tsst[:, s_, :], f"tsst{s_}") for s_ in range(4)]
                    dts_s = [Tl(dts[:, :, s_ * 128:(s_ + 1) * 128], f"dts{s_}") for s_ in range(4)]
                    for s_ in range(4):
                        tsst_s[s_].res.readers = list(tsst.res.readers)
                        tsst_s[s_].res.last_w = tsst.res.last_w
                        dts_s[s_].res.readers = list(dts.res.readers)
                        dts_s[s_].res.last_w = dts.res.last_w
                    lanes = [tm_part()]
                    if t0 > 0:
                        lanes.append(window(si, T, t0 - 512, base))
                    run_l(lanes)
                    dma("st_tsc", tsc_d[lo:hi, :].rearrange("(s p) f -> p s f", p=128), tsst.ap, tsst_s, [tsst] + DR("tsc", lo, hi))
                    dma("st_dT", dT_d[:, :, lo:hi].rearrange("d h t -> h d t"), dts.ap, dts_s, [dts] + DR("dT", lo, hi))
                    xnb_next = partA(t0 + 512) if t0 + 512 < T else None
                    partC(t0, xnT)
                    if xnb_next is not None:
                        xnT_next = partA2(xnb_next)
                def run2(g1, g2):
                    for _ in g1:
                        pass
                run2(window(si, T, T - 512, base), None)
                for m in range(9):
                    mset("pool", Rb[:, m, 4:516], 0.0, [R_r[m]])
                run2(window(si, T, T, base), None)

    def phase2(l):
        with ExitStack() as es:
            if dbg:
                print("P2 start sbuf remaining", nc.sbuf_bytes_remaining)
            lg = one(es, "lg", [128, 12], F32)
            dma("sm", lg.ap, retz_d[l:l + 1, :].partition_broadcast(128).rearrange("p a b -> p (a b)"), [], [lg])
            act(lg.ap, lg.ap, AF.Exp, [lg], [lg], scale=-1.0)
            act(lg.ap, lg.ap, AF.Ln, [lg], [lg], bias=1.0)
            ts("dve", lg.ap, lg.ap, -1.0, ALU.mult, [lg], [lg])
            RSKS = one(es, "rsks", [128, 36], F32)
            act(RSKS[:, 0:6], lg[:, 0:6], AF.Exp, [lg, cst], [RSKS], scale=posc[:, 0:1])
            act(RSKS[:, 6:12], lg[:, 6:12], AF.Exp, [lg, cst], [RSKS], scale=posc[:, 1:2])
            act(RSKS[:, 12:18], lg[:, 0:6], AF.Exp, [lg, cst], [RSKS], scale=posc[:, 2:3])
            act(RSKS[:, 18:24], lg[:, 6:12], AF.Exp, [lg, cst], [RSKS], scale=posc[:, 3:4])
            act(RSKS[:, 24:36], lg.ap, AF.Exp, [lg], [RSKS], scale=128.0)
            RSx = [one(es, f"RSx{d}", [128, 384], F32) for d in range(2)]
            KSx = [one(es, f"KSx{d}", [128, 384], F32) for d in range(2)]
            CDx = [one(es, f"CDx{d}", [128, 384], F32) for d in range(2)]
            for d in range(2):
                for h in range(6):
                    hs = slice(h * 64, (h + 1) * 64)
                    ts("dve", RSx[d][:, hs], ones64, RSKS[:, d * 6 + h:d * 6 + h + 1], ALU.mult, [cst, RSKS], [RSx[d]])
                    ts("dve", KSx[d][:, hs], ones64, RSKS[:, 12 + d * 6 + h:12 + d * 6 + h + 1], ALU.mult,
                       [cst, RSKS], [KSx[d]])
                    ts("dve", CDx[d][:, hs], ones64, RSKS[:, 24 + d * 6 + h:24 + d * 6 + h + 1], ALU.mult,
                       [cst, RSKS], [CDx[d]])
            DsumT = one(es, "DsumT", [128, 768], F32)
            tmpm = one(es, "tmpm", [128, 256], F32)
            for h in range(6):
                act(tmpm[:, 0:128], RELP, AF.Exp, [cst, lg], [tmpm], scale=lg[:, h:h + 1])
                act(tmpm[:, 128:256], RELN, AF.Exp, [cst, lg], [tmpm], scale=lg[:, 6 + h:7 + h])
                tt("dve", tmpm[:, 0:128], tmpm[:, 0:128], MASKF, ALU.mult, [tmpm, cst], [tmpm])
                tt("dve", tmpm[:, 128:256], tmpm[:, 128:256], MASKB, ALU.mult, [tmpm, cst], [tmpm])
                tt("dve", DsumT[:, h * 128:(h + 1) * 128], tmpm[:, 0:128], tmpm[:, 128:256], ALU.add, [tmpm], [DsumT])
            esink = one(es, "esink", [128, 4], F32)
            dma("sm", esink.ap, sink_d[l:l + 1, :].partition_broadcast(128).rearrange("p a b -> p (a b)"), [], [esink])
            act(esink.ap, esink.ap, AF.Exp, [esink], [esink])

            GB = {}
            for d in range(2):
                GB[d] = dict(
                    aq=ring(es, f"aq{d}", [64, 6, 256], BF16, 2), ak=ring(es, f"ak{d}", [64, 6, 256], BF16, 2),
                    aktm=ring(es, f"aktm{d}", [128, 2, 384], BF16, 2), avtm=ring(es, f"avtm{d}", [128, 2, 384], BF16, 2),
                    tsc=ring(es, f"tsc{d}", [128, 2, 84], F32, 2), dB=ring(es, f"dB{d}", [128, 6, 256], F32, 1),
                    bq=ring(es, f"bq{d}", [64, 6, 256], BF16, 1), bk=ring(es, f"bk{d}", [64, 6, 256], BF16, 1),
                    bkv=ring(es, f"bkv{d}", [128, 2, 768], BF16, 1))
            cq_r = ring(es, "cq", [64, 4, 256], BF16, 2)
            ck_r = ring(es, "ck", [64, 2, 512], BF16, 2)
            cv_t = [one(es, f"cv{i}", [128, 4, 2, 65], BF16) for i in range(2)]
            for i in range(2):
                mset("pool", cv_t[i].ap, 1.0, [cv_t[i]])
            cvst = [0]

            CH = {}
            for d in range(2):
                for hg in range(2):
                    n = f"{d}{hg}"
                    CH[(d, hg)] = dict(
                        X=one(es, "X" + n, [128, 384], F32), Gs=one(es, "Gs" + n, [128, 384], F32),
                        M=[one(es, f"M{i}" + n, [128, 384], BF16) for i in range(2)],
                        MT=[one(es, f"MT{i}" + n, [128, 384], BF16) for i in range(2)],
                        PT=[one(es, f"PT{i}" + n, [128, 384], BF16) for i in range(2)],
                        Off=one(es, "Off" + n, [128, 384], BF16),
                        QKm=one(es, "QKm" + n, [128, 384], BF16),
                        qkT=[one(es, f"qkT{i}" + n, [128, 384], BF16) for i in range(2)],
                        Tu=[one(es, f"Tu{i}" + n, [128, 384], BF16) for i in range(2)],
                        Tw=[one(es, f"Tw{i}" + n, [128, 384], BF16) for i in range(2)], nw=one(es, "nw" + n, [64, 384], BF16),
                        vn=one(es, "vn" + n, [128, 192], BF16), vd=one(es, "vd" + n, [128, 192], BF16),
                        to=one(es, "to" + n, [128, 192], F32),
                        S32=one(es, "S32" + n, [64, 192], F32), Sbf=one(es, "Sbf" + n, [64, 192], BF16))
            for kk_ in CH:
                CH[kk_]["Gi"] = CH[kk_]["X"]
                CH[kk_]["Ti"] = CH[kk_]["Gs"]
            RT = {}
            for d in range(2):
                RT[d] = dict(S32=one(es, f"rS32{d}", [64, 384], F32), Sbf=one(es, f"rSbf{d}", [64, 384], BF16),
                             vdec=one(es, f"rvdec{d}", [128, 384], BF16), tmp=one(es, f"rtmp{d}", [128, 384], F32))
            rqk = one(es, "rqk", [128, 768], BF16)
            odir_r = [ring(es, f"odir{d}", [128, 768], F32, 2) for d in range(2)]
            oc_r = ring(es, "oct", [128, 256], F32, 2)
            sbt_r = ring(es, "sbt", [128, 384], F32, 2)
            pTt_r = ring(es, "pTt", [128, 384], BF16, 4)
            den_r = ring(es, "den", [128, 8], F32, 2)

            for si, T in enumerate(seqs):
                base = bases[si]
                N = T // 128
                for d in range(2):
                    for hg in range(2):
                        mset("pool", CH[(d, hg)]["S32"].ap, 0.0, [CH[(d, hg)]["S32"]])
                        mset("pool", CH[(d, hg)]["Sbf"].ap, 0.0, [CH[(d, hg)]["Sbf"]])
                    mset("pool", RT[d]["S32"].ap, 0.0, [RT[d]["S32"]])
                    mset("pool", RT[d]["Sbf"].ap, 0.0, [RT[d]["Sbf"]])
                cur = {}
                curc = {}

                def load_A(d, gi):
                    t0 = gi * 256
                    lo, hi = base + t0, base + t0 + 256
                    wl, wh = lo + 2, hi + 2
                    b = {k: GB[d][k]() for k in ("aq", "ak", "aktm", "avtm", "tsc", "dB")}
                    key = f"g{d}"
                    dma(key, b["aq"].ap, aqT_d[:, wl:wh].rearrange("(h d) t -> d h t", d=64), DR("aqT", wl, wh), [b["aq"]])
                    dma(key, b["ak"].ap, akT_d[:, wl:wh].rearrange("(h d) t -> d h t", d=64), DR("akT", wl, wh), [b["ak"]])
                    dma(key, b["aktm"].ap, aktm_d[wl:wh, :].rearrange("(s p) f -> p s f", p=128), DR("aktm", wl, wh), [b["aktm"]])
                    dma(key, b["avtm"].ap, avtm_d[wl:wh, :].rearrange("(s p) f -> p s f", p=128), DR("avtm", wl, wh), [b["avtm"]])
                    dma(key, b["tsc"].ap, tsc_d[lo:hi, :].rearrange("(s p) f -> p s f", p=128), DR("tsc", lo, hi), [b["tsc"]])
                    dma(key, b["dB"].ap, dT_d[d, :, lo:hi].partition_broadcast(128), DR("dT", lo, hi), [b["dB"]])
                    dbv = b["dB"].ap.rearrange("p h (c t) -> p (h c) t", t=128)
                    tt("pool", dbv, dbv, NM[d].unsqueeze(1).broadcast_to([128, 12, 128]), ALU.subtract, [b["dB"], cst], [b["dB"]])
                    return b

                def load_B(d, gi):
                    t0 = gi * 256
                    lo, hi = base + t0, base + t0 + 256
                    b = {k: GB[d][k]() for k in ("bq", "bk", "bkv")}
                    key = f"g{d}"
                    dma(key, b["bq"].ap, bqT_d[:, lo:hi].rearrange("(h d) t -> d h t", d=64), DR("bqT", lo, hi), [b["bq"]])
                    dma(key, b["bk"].ap, bkT_d[:, lo:hi].rearrange("(h d) t -> d h t", d=64), DR("bkT", lo, hi), [b["bk"]])
                    dma(key, b["bkv"].ap, kvtm_d[lo:hi, 0:768].rearrange("(s p) f -> p s f", p=128), DR("kvtm", lo, hi), [b["bkv"]])
                    cur[d] = b
                    if d == 0:
                        cq = cq_r()
                        ck = ck_r()
                        cv = cv_t[cvst[0] % 2]
                        cvst[0] += 1
                        dma("gc", cq.ap, cqT_d[:, lo:hi].rearrange("(h d) t -> d h t", d=64), DR("cqT", lo, hi), [cq])
                        klo = max(t0 - 128, 0)
                        khi = min(t0 + 384, T)
                        off = klo - (t0 - 128)
                        nb = (khi - klo) // 128
                        dma("gc", ck[:, :, off:off + (khi - klo)],
                            ckT_d[:, base + klo:base + khi].rearrange("(h d) t -> d h t", d=64),
                            DR("ckT", base + klo, base + khi), [ck])
                        for kvh in range(2):
                            dma("gc", cv[:, off // 128:off // 128 + nb, kvh, 0:64],
                                kvtm_d[base + klo:base + khi, 768 + kvh * 64:832 + kvh * 64].rearrange(
                                    "(s p) e -> p s e", p=128),
                                DR("kvtm", base + klo, base + khi), [cv])
                        curc["cq"], curc["ck"], curc["cv"] = cq, ck, cv

                def dn_chain(part, d, hg, c, b, slot, odir):
                    W = dict(CH[(d, hg)])
                    for nm_ in ("Tu", "Tw", "qkT"):
                        W[nm_] = CH[(d, hg)][nm_][slot]
                    sc = c % 2
                    cs = slice(sc * 128, (sc + 1) * 128)
                    tsc = b["tsc"]
                    hs = [3 * hg + j for j in range(3)]

                    def col(k, h):
                        return tsc[:, sc, k * 12 + d * 6 + h:k * 12 + d * 6 + h + 1]

                    def colb(k, n, rows=128):
                        c0 = k * 12 + d * 6 + 3 * hg
                        return tsc[0:rows, sc, c0:c0 + 3].unsqueeze(2).broadcast_to([rows, 3, n])

                    def v3(ap, n):
                        return ap.rearrange("p (j c) -> p j c", c=n)

                    def J(j):
                        return slice(j * 128, (j + 1) * 128)

                    def J6(j):
                        return slice(j * 64, (j + 1) * 64)
                    if part == 1:
                        M, MT, PT = W["M"], W["MT"], W["PT"]
                        for j, h in enumerate(hs):
                            act(W["Gs"][:, J(j)], b["dB"][:, h, cs], AF.Exp, [b["dB"], tsc], [W["Gs"]], scale=-1.0, bias=col(1, h))
                        yield
                        tt("pool", v3(W["Gi"].ap, 128), v3(W["Gs"].ap, 128), identf.unsqueeze(1).broadcast_to([128, 3, 128]), ALU.add, [W["Gs"], cst], [W["Gi"]])
                        bG = bank()
                        for j, h in enumerate(hs):
                            mm(bG[:, J(j)], b["ak"][:, h, cs], b["ak"][:, h, cs], [b["ak"]], [bG])
                        for j, h in enumerate(hs):
                            stt("dve", M[0][:, J(j)], bG[:, J(j)], col(0, h), W["Gs"][:, J(j)], ALU.mult, ALU.mult,
                                [bG, tsc, W["Gs"]], [M[0], bG])
                        yield
                        bQ = bank()
                        for j, h in enumerate(hs):
                            mm(bQ[:, J(j)], b["aq"][:, h, cs], b["ak"][:, h, cs], [b["aq"], b["ak"]], [bQ])
                        tt("dve", W["QKm"].ap, bQ[:, 0:384], W["Gi"].ap, ALU.mult, [bQ, W["Gi"]], [W["QKm"], bQ])
                        tt("pool", v3(W["Off"].ap, 128), v3(M[0].ap, 128), OFF64.unsqueeze(1).broadcast_to([128, 3, 128]), ALU.mult, [M[0], cst], [W["Off"]])
                        yield
                        tt("pool", v3(M[0].ap, 128), v3(M[0].ap, 128), BD64.unsqueeze(1).broadcast_to([128, 3, 128]), ALU.mult, [M[0], cst], [M[0]])
                        bT2 = bank()
                        for j in range(3):
                            tr(bf(bT2)[:, J(j)], W["QKm"][:, J(j)], identb.ap, [W["QKm"], identb], [bT2])
                        cp("dve", W["qkT"].ap, bf(bT2)[:, 0:384], [bT2], [W["qkT"], bT2])
                        yield
                        bT1 = bank()
                        for j in range(3):
                            tr(bf(bT1)[:, J(j)], M[0][:, J(j)], identb.ap, [M[0], identb], [bT1])
                        cp("act", MT[0].ap, bf(bT1)[:, 0:384], [bT1], [MT[0], bT1])
                        yield
                        tt("pool", v3(PT[0].ap, 128), v3(MT[0].ap, 128), identf.unsqueeze(1).broadcast_to([128, 3, 128]), ALU.add, [MT[0], cst], [PT[0]])
                        yield
                        pc = 0
                        NLEV = 5
                        for k in range(1, NLEV + 2):
                            a, n = (k - 1) % 2, k % 2
                            if k <= NLEV:
                                bA = bank()
                                for j in range(3):
                                    mm(bA[:, J(j)], MT[a][:, J(j)], M[a][:, J(j)], [MT[a], M[a]], [bA])
                                if k < NLEV:
                                    bB = bank()
                                    for j in range(3):
                                        mm(bB[:, J(j)], M[a][:, J(j)], MT[a][:, J(j)], [MT[a], M[a]], [bB])
                            if k >= 2:
                                bC = bank()
                                for j in range(3):
                                    mm(bC[:, J(j)], M[a][:, J(j)], PT[pc][:, J(j)], [M[a], PT[pc]], [bC], start=True, stop=False)
                                    mm(bC[:, J(j)], identb.ap, PT[pc][:, J(j)], [identb, PT[pc]], [bC], start=False, stop=True)
                            if k <= NLEV:
                                cp("act", M[n].ap, bA[:, 0:384], [bA], [M[n], bA])
                                if k < NLEV:
                                    cp("dve", MT[n].ap, bB[:, 0:384], [bB], [MT[n], bB])
                            if k >= 2:
                                if k <= NLEV:
                                    cp("act" if k % 2 == 0 else "dve", PT[1 - pc].ap, bC[:, 0:384], [bC], [PT[1 - pc], bC])
                                    pc = 1 - pc
                                else:
                                    cp("dve", W["Ti"].ap, bC[:, 0:384], [bC], [W["Ti"], bC])
                                    cp("act", PT[1 - pc].ap, bC[:, 0:384], [bC], [PT[1 - pc], bC])
                                    pc = 1 - pc
                            yield
                        XTb, Xb, Yb = PT[pc], M[0], MT[0]
                        bX = bank()
                        bY = bank()
                        for j in range(3):
                            tr(bf(bX)[:, J(j)], XTb[:, J(j)], identb.ap, [XTb, identb], [bX])
                        for j in range(3):
                            mm(bY[:, J(j)], W["Off"][:, J(j)], XTb[:, J(j)], [W["Off"], XTb], [bY])
                        cp("act", Xb.ap, bf(bX)[:, 0:384], [bX], [Xb, bX])
                        cp("dve", Yb.ap, bY[:, 0:384], [bY], [Yb, bY])
                        yield
                        bZ = bank()
                        for j in range(3):
                            mm(bZ[:, J(j)], Xb[:, J(j)], Yb[:, J(j)], [Xb, Yb], [bZ])
                        tt("dve", W["Ti"].ap, bZ[:, 0:384], W["Ti"].ap, ALU.add, [bZ, W["Ti"]], [W["Ti"], bZ])
                        tt("dve", v3(W["Tu"].ap, 128), v3(W["Ti"].ap, 128), colb(2, 128), ALU.mult, [W["Ti"], tsc], [W["Tu"]])
                        tt("dve", v3(W["Tw"].ap, 128), v3(W["Ti"].ap, 128), colb(3, 128), ALU.mult, [W["Ti"], tsc], [W["Tw"]])
                        yield
                        return
                    bW = bank()
                    for j, h in enumerate(hs):
                        mm(bW[0:64, J(j)], b["aktm"][:, sc, h * 64:(h + 1) * 64], W["Tw"][:, J(j)], [b["aktm"], W["Tw"]], [bW])
                    act(W["nw"].ap, bW[0:64, 0:384], AF.Copy, [bW], [W["nw"], bW], scale=-1.0)
                    yield
                    bV = bank()
                    for j, h in enumerate(hs):
                        mm(bV[:, J6(j)], W["Tu"][:, J(j)], b["avtm"][:, sc, h * 64:(h + 1) * 64], [W["Tu"], b["avtm"]], [bV],
                           start=True, stop=False)
                        mm(bV[:, J6(j)], W["nw"][:, J(j)], W["Sbf"][:, J6(j)], [W["nw"], W["Sbf"]], [bV],
                           start=False, stop=True)
                    cp("act", W["vn"].ap, bV[:, 0:192], [bV], [W["vn"], bV])
                    tt("dve", v3(W["vd"].ap, 64), v3(bV[:, 0:192], 64), colb(5, 64), ALU.mult, [bV, tsc], [W["vd"], bV])
                    yield
                    bO1 = bank()
                    bO2 = bank()
                    for j, h in enumerate(hs):
                        mm(bO1[:, J6(j)], W["qkT"][:, J(j)], W["vn"][:, J6(j)], [W["qkT"], W["vn"]], [bO1])
                    for j, h in enumerate(hs):
                        mm(bO2[:, J6(j)], b["aq"][:, h, cs], W["Sbf"][:, J6(j)], [b["aq"], W["Sbf"]], [bO2])
                    tt("dve", v3(W["to"].ap, 64), v3(bO2[:, 0:192], 64), colb(4, 64), ALU.mult, [bO2, tsc], [W["to"], bO2])
                    tt("dve", odir[:, hg * 192:(hg + 1) * 192], bO1[:, 0:192], W["to"].ap, ALU.add, [bO1, W["to"]],
                       [odir, bO1])
                    yield
                    bS = bank()
                    for j, h in enumerate(hs):
                        mm(bS[0:64, J6(j)], b["aktm"][:, sc, h * 64:(h + 1) * 64], W["vd"][:, J6(j)], [b["aktm"], W["vd"]], [bS])
                    tt("pool", v3(W["S32"].ap, 64), v3(W["S32"].ap, 64), colb(6, 64, rows=64), ALU.mult, [W["S32"], tsc], [W["S32"]])
                    tt("dve", W["S32"].ap, W["S32"].ap, bS[0:64, 0:192], ALU.add, [W["S32"], bS], [W["S32"], bS])
                    cp("act", W["Sbf"].ap, W["S32"].ap, [W["S32"]], [W["Sbf"]])
                    yield

                def ret_chain(d, c, odir):
                    b = cur[d]
                    sc = c % 2
                    cs = slice(sc * 128, (sc + 1) * 128)
                    R_ = RT[d]
                    if d == 0:
                        b1 = bank()
                        b2 = bank()
                        for h in range(6):
                            bb = b1 if h < 4 else b2
                            hh = h % 4
                            mm(bb[:, hh * 128:(hh + 1) * 128], b["bk"][:, h, cs], b["bq"][:, h, cs], [b["bk"], b["bq"]], [bb])
                        tt("dve", rqk[:, 0:512], b1.ap, DsumT[:, 0:512], ALU.mult, [b1, DsumT], [rqk, b1])
                        tt("dve", rqk[:, 512:768], b2[:, 0:256], DsumT[:, 512:768], ALU.mult, [b2, DsumT], [rqk, b2])
                    yield
                    bO2 = bank()
                    for h in range(6):
                        mm(bO2[:, h * 64:(h + 1) * 64], b["bq"][:, h, cs], R_["Sbf"][:, h * 64:(h + 1) * 64],
                           [b["bq"], R_["Sbf"]], [bO2])
                    if d == 0:
                        bO1 = bank()
                        for h in range(6):
                            mm(bO1[:, h * 64:(h + 1) * 64], rqk[:, h * 128:(h + 1) * 128],
                               b["bkv"][:, sc, 384 + h * 64:384 + (h + 1) * 64], [rqk, b["bkv"]], [bO1])
                        tt("dve", R_["tmp"].ap, bO2[:, 0:384], RSx[d].ap, ALU.mult, [bO2, RSx[d]], [R_["tmp"], bO2])
                        tt("dve", odir[:, 384:768], bO1[:, 0:384], R_["tmp"].ap, ALU.add, [bO1, R_["tmp"]], [odir, bO1])
                    else:
                        tt("dve", odir[:, 384:768], bO2[:, 0:384], RSx[d].ap, ALU.mult, [bO2, RSx[d]], [odir, bO2])
                    tt("pool", R_["vdec"].ap, b["bkv"][:, sc, 384:768], KSx[d].ap, ALU.mult, [b["bkv"], KSx[d]], [R_["vdec"]])
                    yield
                    bS = bank()
                    for h in range(6):
                        mm(bS[0:64, h * 64:(h + 1) * 64], b["bkv"][:, sc, h * 64:(h + 1) * 64],
                           R_["vdec"][:, h * 64:(h + 1) * 64], [b["bkv"], R_["vdec"]], [bS])
                    tt("pool", R_["S32"].ap, R_["S32"].ap, CDx[d][0:64, :], ALU.mult, [R_["S32"], CDx[d]], [R_["S32"]])
                    tt("dve", R_["S32"].ap, R_["S32"].ap, bS[0:64, 0:384], ALU.add, [R_["S32"], bS], [R_["S32"], bS])
                    cp("act", R_["Sbf"].ap, R_["S32"].ap, [R_["S32"]], [R_["Sbf"]])
                    yield

                def att_chain(c, oct_):
                    sc = c % 2
                    cs = slice(sc * 128, (sc + 1) * 128)
                    cq, ck, cv = curc["cq"], curc["ck"], curc["cv"]
                    jvalid = [jj for jj in range(3) if 0 <= c - 1 + jj < N]
                    j0, j1 = jvalid[0], jvalid[-1] + 1
                    pts = []
                    for h in range(4):
                        kvh = h // 2
                        bs_ = bank()
                        for jj in jvalid:
                            kb = sc + jj
                            mm(bs_[:, jj * 128:(jj + 1) * 128], ck[:, kvh, kb * 128:(kb + 1) * 128], cq[:, h, cs],
                               [ck, cq], [bs_])
                        sbt = sbt_r()
                        tt("dve", sbt[:, j0 * 128:j1 * 128], bs_[:, j0 * 128:j1 * 128],
                           cbias[:, h, j0:j1, :].rearrange("p j q -> p (j q)"), ALU.add, [bs_, cst], [sbt, bs_])
                        pT = pTt_r()
                        act(pT[:, j0 * 128:j1 * 128], sbt[:, j0 * 128:j1 * 128], AF.Exp, [sbt], [pT])
                        pts.append(pT)
                        if h % 2 == 1:
                            yield
                    bo = bank()
                    for h2 in range(4):
                        for jj in jvalid:
                            kb = sc + jj
                            mm(bo[:, h2 * 65:h2 * 65 + 65], pts[h2][:, jj * 128:(jj + 1) * 128],
                               cv[:, kb, h2 // 2, :], [pts[h2], cv], [bo], start=(jj == j0), stop=(jj == j1 - 1))
                    den = den_r()
                    tt("dve", den[:, 0:4], bo[:, 0:260].rearrange("p (h e) -> p h e", e=65)[:, :, 64], esink.ap, ALU.add,
                       [bo, esink], [den, bo])
                    recip(den[:, 4:8], den[:, 0:4], [den], [den])
                    tt("dve", oct_.ap.rearrange("p (h e) -> p h e", e=64),
                       bo[:, 0:260].rearrange("p (h e) -> p h e", e=65)[:, :, 0:64],
                       den[:, 4:8].unsqueeze(2).broadcast_to([128, 4, 64]), ALU.mult, [bo, den], [oct_, bo])
                    yield

                def run_lanes(chains):
                    live = list(chains)
                    while live:
                        nxt = []
                        for ch in live:
                            try:
                                next(ch)
                                nxt.append(ch)
                            except StopIteration:
                                pass
                        live = nxt

                gA = {}

                def groupA(d, c):
                    gi = c // 2
                    if gA.get(d, (None, None))[0] != gi:
                        gA[d] = (gi, load_A(d, gi))
                    return gA[d][1]

                def chunk_of(d, t):
                    return t if d == 0 else N - 1 - t

                info = {}
                for d in range(2):
                    info[(d, 0)] = groupA(d, chunk_of(d, 0))
                run_lanes([dn_chain(1, d, hg, chunk_of(d, 0), info[(d, 0)], 0, None) for hg in range(2) for d in range(2)])
                gB = {}
                for t in range(N):
                    cf, cb = t, N - 1 - t
                    for d in range(2):
                        gi = chunk_of(d, t) // 2
                        if gB.get(d) != gi:
                            load_B(d, gi)
                            gB[d] = gi
                    od = [odir_r[0](), odir_r[1]()]
                    oct_ = oc_r()
                    chains = []
                    for hg in range(2):
                        for d in range(2):
                            chains.append(dn_chain(2, d, hg, chunk_of(d, t), info[(d, t)], t % 2, od[d]))
                    if t + 1 < N:
                        for d in range(2):
                            info[(d, t + 1)] = groupA(d, chunk_of(d, t + 1))
                        for hg in range(2):
                            for d in range(2):
                                chains.append(dn_chain(1, d, hg, chunk_of(d, t + 1), info[(d, t + 1)], (t + 1) % 2, None))
                    chains += [ret_chain(0, cf, od[0]), ret_chain(1, cb, od[1]), att_chain(cf, oct_)]
                    run_lanes(chains)
                    info.pop((0, t), None)
                    info.pop((1, t), None)
                    lo = base + cf * 128
                    dma("st_of", of_d[lo:lo + 128, :], od[0].ap, [od[0]], DR("of", lo, lo + 128))
                    dma("st_oc", oc_d[lo:lo + 128, :], oct_.ap, [oct_], DR("oc", lo, lo + 128))
                    lo = base + cb * 128
                    dma("st_ob", ob_d[lo:lo + 128, :], od[1].ap, [od[1]], DR("ob", lo, lo + 128))

    def phase3(l):
        last = (l == L - 1)
        src_d = x_d if l == 0 else h_d
        with ExitStack() as es:
            Wout = es.enter_context(nc.sbuf_tensor(un("Wout"), [128, 8, 1024], BF16)).ap()
            Wpg = es.enter_context(nc.sbuf_tensor(un("Wpg"), [128, 8, 1024], BF16)).ap()
            Wple = es.enter_context(nc.sbuf_tensor(un("Wple"), [128, 2, 1024], BF16)).ap()
            Wout_r = [Res(f"Wout{i}") for i in range(4)]
            Wpg_r = [Res(f"Wpg{i}") for i in range(4)]
            Wple_r = [Res(f"Wple{i}") for i in range(4)]
            with ExitStack() as es2:
                stg = ring(es2, "wstg3", [128, 8, 256], F32, 2)
                for (dst, src, kk, rs) in ((Wout, wout_d[l], 8, Wout_r), (Wpg, wpg_d[l], 8, Wpg_r),
                                           (Wple, wple_d[l], 2, Wple_r)):
                    for bi, c0 in enumerate(range(0, 1024, 256)):
                        st = stg()
                        dma("wld", st[:, 0:kk, :], src[:, :, c0:c0 + 256], [], [st])
                        cp(alt(("act", "dve")), dst[:, :, c0:c0 + 256], st[:, 0:kk, :], [st], [rs[bi]])
                P.barrier()
            dngx = one(es, "dngx", [128, 384], F32)
            for h in range(6):
                dma("sm", dngx[:, h * 64:(h + 1) * 64],
                    dng_d[l:l + 1, :].partition_broadcast(128).rearrange("p a b -> p (a b)"), [], [dngx])
            retg = one(es, "retgx", [128, 384], F32)
            dma("sm", retg.ap, retg_d[l:l + 1, :].partition_broadcast(128).rearrange("p a b -> p (a b)"), [], [retg])
            fing = one(es, "fing", [128, 1024], F32)
            dma("sm", fing.ap, fing_d.partition_broadcast(128).rearrange("p a b -> p (a b)"), [], [fing])

            of_r = ring(es, "of3", [128, 768], F32, 3)
            ob_r = ring(es, "ob3", [128, 768], F32, 3)
            oc_r3 = ring(es, "oc3", [128, 256], F32, 3)
            gt_r = ring(es, "gt3", [128, 1024], BF16, 3)
            x_r = ring(es, "x3", [128, 1024], F32, 3)
            p_r = ring(es, "p3", [128, 256], F32, 3)
            sq_r = ring(es, "sq3", [128, 768], F32, 3)
            st_r = ring(es, "st3", [128, 32], F32, 3)
            t12_r = ring(es, "t12", [128, 768], F32, 3)
            xc_r = ring(es, "xc3", [128, 384], F32, 3)
            mix_r = ring(es, "mix3", [128, 1024], BF16, 3)
            mixT_r = ring(es, "mixT3", [128, 1024], BF16, 3)
            h1_r = ring(es, "h13", [128, 1024], F32, 3)
            h1b_r = ring(es, "h1b3", [128, 1024], BF16, 3)
            h1T_r = ring(es, "h1T3", [128, 1024], BF16, 3)
            sig_r = ring(es, "sig3", [128, 1024], F32, 3)
            pb16_r = ring(es, "pb163", [128, 256], BF16, 3)
            pT_r = ring(es, "pT3", [128, 256], BF16, 3)
            h2_r = ring(es, "h23", [128, 1024], F32, 3)
            y_r = ring(es, "y3", [128, 1024], F32, 3)
            junk = one(es, "junk3", [128, 1024], BF16)

            def p3_sub(si, base, hb, c):
                lo, hi = base + c * 128, base + c * 128 + 128
                xlo = hb + c * 128
                of_, ob_, oc_, gt, xt, pt = of_r(), ob_r(), oc_r3(), gt_r(), x_r(), p_r()
                dma("l3a", of_.ap, of_d[lo:hi, :], DR("of", lo, hi), [of_])
                dma("l3b", ob_.ap, ob_d[lo:hi, :], DR("ob", lo, hi), [ob_])
                dma("l3c", oc_.ap, oc_d[lo:hi, :], DR("oc", lo, hi), [oc_])
                dma("l3d", gt.ap, gate_d[lo:hi, :], DR("gate", lo, hi), [gt])
                dma("l3e", xt.ap, src_d[xlo:xlo + 128, :], ([] if l == 0 else DR("h", xlo, xlo + 128)), [xt])
                po = xoff[si] + c * 128
                dma("l3f", pt.ap, p_d[l, po:po + 128, :], [], [pt])
                tt("pool", of_.ap, of_.ap, ob_.ap, ALU.add, [of_, ob_], [of_])
                t12 = t12_r()
                tt("pool", t12[:, 0:384], gt[:, 0:384], dngx.ap, ALU.mult, [gt, dngx], [t12])
                yield
                sq = sq_r()
                tt("pool", sq.ap, of_.ap, of_.ap, ALU.mult, [of_], [sq])
                tt("pool", t12[:, 384:768], gt[:, 384:768], retg.ap, ALU.mult, [gt, retg], [t12])
                yield
                st = st_r()
                P.op("dve", lambda e, st=st, sq=sq: e.tensor_reduce(
                    out=st[:, 0:12], in_=sq.ap.rearrange("p (h e) -> p h e", e=64), axis=AX.X, op=ALU.add),
                    _rl([sq]), _rl([st]))
                P.op("dve", lambda e, st=st, of_=of_: e.tensor_reduce(
                    out=st[:, 12:18], in_=of_[:, 384:768].rearrange("p (h e) -> p h e", e=64), axis=AX.X, op=ALU.add),
                    _rl([of_]), _rl([st]))
                mix = mix_r()
                tt("pool", mix[:, 768:1024], oc_.ap, gt[:, 768:1024], ALU.mult, [oc_, gt], [mix])
                yield
                act(st[:, 18:24], st[:, 0:6], AF.Sqrt, [st], [st], bias=EPS, scale=1.0 / 64)
                ts("dve", st[:, 12:18], st[:, 12:18], 1.0 / 64, ALU.mult, [st], [st])
                yield
                tt("dve", st[:, 24:30], st[:, 12:18], st[:, 12:18], ALU.mult, [st], [st])
                yield
                stt("dve", st[:, 24:30], st[:, 6:12], 1.0 / 64, st[:, 24:30], ALU.mult, ALU.subtract, [st], [st])
                yield
                act(st[:, 24:30], st[:, 24:30], AF.Sqrt, [st], [st], bias=EPS)
                recip(st[:, 18:24], st[:, 18:24], [st], [st])
                yield
                recip(st[:, 24:30], st[:, 24:30], [st], [st])
                xc = xc_r()
                h6 = lambda ap_: ap_.rearrange("p (h e) -> p h e", e=64)
                b6 = lambda ap_: ap_.unsqueeze(2).broadcast_to([128, 6, 64])
                tt("dve", h6(xc.ap), h6(of_[:, 0:384]), b6(st[:, 18:24]), ALU.mult, [of_, st], [xc])
                yield
                tt("dve", mix[:, 0:384], xc.ap, t12[:, 0:384], ALU.mult, [xc, t12], [mix])
                yield
                tt("pool", h6(xc.ap), h6(of_[:, 384:768]), b6(st[:, 12:18]), ALU.subtract, [of_, st, mix], [xc])
                yield
                tt("dve", h6(xc.ap), h6(xc.ap), b6(st[:, 24:30]), ALU.mult, [xc, st], [xc])
                yield
                tt("dve", mix[:, 384:768], xc.ap, t12[:, 384:768], ALU.mult, [xc, t12], [mix])
                yield
                pb = bank()
                for k in range(8):
                    tr(bf(pb)[:, k * 128:(k + 1) * 128], mix[:, k * 128:(k + 1) * 128], identb.ap, [mix, identb], [pb])
                mixT = mixT_r()
                cp("act", mixT.ap, bf(pb), [pb], [mixT, pb])
                yield
                h1 = h1_r()
                for cg in range(2):
                    pb = bank()
                    for k in range(8):
                        mm(pb.ap, mixT[:, k * 128:(k + 1) * 128], Wout[:, k, cg * 512:(cg + 1) * 512],
                           [mixT, Wout_r[2 * cg], Wout_r[2 * cg + 1]], [pb], start=(k == 0), stop=(k == 7))
                    tt("dve", h1[:, cg * 512:(cg + 1) * 512], pb.ap, xt[:, cg * 512:(cg + 1) * 512], ALU.add, [pb, xt],
                       [h1, pb])
                h1b = h1b_r()
                yield
                cp("act", h1b.ap, h1.ap, [h1], [h1b])
                pb = bank()
                for k in range(8):
                    tr(bf(pb)[:, k * 128:(k + 1) * 128], h1b[:, k * 128:(k + 1) * 128], identb.ap, [h1b, identb], [pb])
                h1T = h1T_r()
                cp("dve", h1T.ap, bf(pb), [pb], [h1T, pb])
                yield
                sig = sig_r()
                for cg in range(2):
                    pb = bank()
                    for k in range(8):
                        mm(pb.ap, h1T[:, k * 128:(k + 1) * 128], Wpg[:, k, cg * 512:(cg + 1) * 512],
                           [h1T, Wpg_r[2 * cg], Wpg_r[2 * cg + 1]], [pb], start=(k == 0), stop=(k == 7))
                    act(sig[:, cg * 512:(cg + 1) * 512], pb.ap, AF.Sigmoid, [pb], [sig, pb])
                pb16 = pb16_r()
                yield
                cp("act", pb16.ap, pt.ap, [pt], [pb16])
                pb = bank()
                for k in range(2):
                    tr(bf(pb)[:, k * 128:(k + 1) * 128], pb16[:, k * 128:(k + 1) * 128], identb.ap, [pb16, identb], [pb])
                pT = pT_r()
                cp("act", pT.ap, bf(pb)[:, 0:256], [pb], [pT, pb])
                yield
                h2 = h2_r()
                for cg in range(2):
                    pb = bank()
                    for k in range(2):
                        mm(pb.ap, pT[:, k * 128:(k + 1) * 128], Wple[:, k, cg * 512:(cg + 1) * 512],
                           [pT, Wple_r[2 * cg], Wple_r[2 * cg + 1]], [pb], start=(k == 0), stop=(k == 1))
                    cs_ = slice(cg * 512, (cg + 1) * 512)
                    tt("dve", h2[:, cs_], pb.ap, sig[:, cs_], ALU.mult, [pb, sig], [h2, pb])
                    tt("pool", h2[:, cs_], h2[:, cs_], h1[:, cs_], ALU.add, [h2, h1], [h2])
                yield
                if not last:
                    dma("st_h", h_d[lo:hi, :], h2.ap, [h2], DR("h", lo, hi))
                else:
                    mset("pool", st[:, 30:31], 0.0, [st])
                    act(junk.ap, h2.ap, AF.Square, [h2, st], [junk, st], accum=st[:, 30:31])
                    act(st[:, 31:32], st[:, 30:31], AF.Sqrt, [st], [st], bias=EPS, scale=1.0 / D)
                    recip(st[:, 31:32], st[:, 31:32], [st], [st])
                    yt = y_r()
                    stt("dve", yt.ap, h2.ap, st[:, 31:32], fing.ap, ALU.mult, ALU.mult, [h2, st, fing], [yt])
                    dma("st_y", y_d[po:po + 128, :], yt.ap, [yt], [], final=True)

                yield

            K3 = 3
            work = []
            for si, T in enumerate(seqs):
                base = bases[si]
                hb = xoff[si] if l == 0 else base
                for c in range(T // 128):
                    work.append((si, base, hb, c))
            live = []
            wi = 0
            rounds = 0
            while wi < len(work) or live:
                if wi < len(work) and len(live) < K3 and (not live or rounds % 4 == 0):
                    live.append(p3_sub(*work[wi]))
                    wi += 1
                nxt = []
                for ch in live:
                    try:
                        next(ch)
                        nxt.append(ch)
                    except StopIteration:
                        pass
                live = nxt
                rounds += 1

    for l in range(L):
        phase1(l)
        P.barrier()
        phase2(l)
        P.barrier()
        phase3(l)
        P.barrier()
    P.emit()
    return nc


def _wl(w, kk):
    L_, _, C = w.shape
    return np.ascontiguousarray(w.reshape(L_, kk, 128, C).transpose(0, 2, 1, 3))


def prep_shared(w_in, w_out, norm_g, conv_w, dn_a_log, dn_dt_bias, dn_norm_g, ret_decay_z, ret_norm_g,
                attn_sink, w_ple, w_pg, final_g):
    L_ = w_in.shape[0]
    sp = np.cumsum([0, 1152, 384, 12, 12, 384, 384, 384, 384, 256, 128, 128, 256])
    seg = lambda i: w_in[:, :, sp[i]:sp[i + 1]]
    wfm = np.concatenate([seg(0), seg(4), seg(5), seg(8), seg(9)], axis=2)
    wtm = np.concatenate([seg(1), seg(7), seg(11), seg(5), seg(6), seg(10), seg(2), seg(3)], axis=2)
    return dict(
        wfm=_wl(wfm, 8), wtm=_wl(wtm, 8), wout=_wl(w_out, 8), wpg=_wl(w_pg, 8), wple=_wl(w_ple, 2),
        normg=np.ascontiguousarray(norm_g.reshape(L_, 8, 128).transpose(0, 2, 1)),
        convw=np.ascontiguousarray(conv_w.reshape(L_, 5, 9, 128).transpose(0, 3, 2, 1)),
        alog=np.ascontiguousarray(dn_a_log.reshape(L_, 12)), dtb=np.ascontiguousarray(dn_dt_bias.reshape(L_, 12)),
        dng=np.ascontiguousarray(dn_norm_g), retz=np.ascontiguousarray(ret_decay_z.reshape(L_, 12)),
        retg=np.ascontiguousarray(ret_norm_g), sink=np.ascontiguousarray(attn_sink),
        fing=np.ascontiguousarray(final_g.reshape(1, D)), consts=make_consts())


def kernel(x_prompt, x_sample, p_prompt, p_sample, w_in, w_out, norm_g, conv_w, dn_a_log, dn_dt_bias,
           dn_norm_g, ret_decay_z, ret_norm_g, attn_sink, w_ple, w_pg, final_g):
    f = lambda a: np.asarray(a, np.float32)
    x_prompt, x_sample, p_prompt, p_sample = f(x_prompt), f(x_sample), f(p_prompt), f(p_sample)
    shared = prep_shared(*[f(a) for a in (w_in, w_out, norm_g, conv_w, dn_a_log, dn_dt_bias, dn_norm_g,
                                          ret_decay_z, ret_norm_g, attn_sink, w_ple, w_pg, final_g)])
    n = 8
    B, S, _ = x_prompt.shape
    DB, DS, _ = x_sample.shape
    per = DB // n
    seqs = [S] + [DS] * per
    nc = build_program(seqs, L=w_in.shape[0])
    in_maps = []
    for c in range(n):
        xs = [x_prompt[c]] + [x_sample[c * per + i] for i in range(per)]
        ps = [p_prompt[:, c]] + [p_sample[:, c * per + i] for i in range(per)]
        m = dict(shared)
        m["x"] = np.ascontiguousarray(np.concatenate(xs, axis=0))
        m["p"] = np.ascontiguousarray(np.concatenate(ps, axis=1))
        in_maps.append(m)
    res = run_bass_kernel_spmd(nc, in_maps, core_ids=list(range(n)))
    yp = np.empty((B, S, D), np.float32)
    ys = np.empty((DB, DS, D), np.float32)
    for c in range(n):
        y = res.results[c]["y"]
        yp[c] = y[0:S]
        for i in range(per):
            ys[c * per + i] = y[S + i * DS:S + (i + 1) * DS]
    return (yp, ys)
```
